# Optimizing a Trainium2 kernel written in Bass

```python
import math
import jax, jax.numpy as jnp
from jax import lax
import numpy as np

D_MODEL = 1024
BATCH = 8
SEQ = 4096
DEPTH = 2

CHUNK = 64
Q_BLOCK = 128
D_MIX = D_MODEL
N_GROUPS = 4
GROUP_WIDTH = D_MIX // N_GROUPS
HEAD_DIM = 64
GROUP_HEADS = GROUP_WIDTH // HEAD_DIM

MLA_Q_LORA = D_MODEL // 4
MLA_KV_LORA = D_MODEL // 8
MLA_NOPE = HEAD_DIM
MLA_ROPE = HEAD_DIM // 2
MLA_V = HEAD_DIM
ROPE_BASE = 10000.0

DIFF_QK = HEAD_DIM // 2

IDX_HEADS = 8
IDX_DIM = 32
TOPK_MAX = 256

CONV_WIDTH = 3

NUM_BUCKETS = 32
MAX_DISTANCE = 128
N_BIAS_HEADS = 2 * GROUP_HEADS + GROUP_HEADS

NORM_EPS = 1e-6
NEG = -1e30

IN_SPLITS = (
    ("a_cq", MLA_Q_LORA), ("a_ckv", MLA_KV_LORA), ("a_krope", MLA_ROPE), ("a_gate", GROUP_WIDTH),
    ("b_q", GROUP_HEADS * 2 * DIFF_QK), ("b_k", GROUP_HEADS * 2 * DIFF_QK), ("b_v", GROUP_WIDTH), ("b_gate", GROUP_WIDTH),
    ("c_q", GROUP_WIDTH), ("c_k", GROUP_WIDTH), ("c_v", GROUP_WIDTH),
    ("c_qidx", IDX_HEADS * IDX_DIM), ("c_kidx", IDX_DIM), ("c_widx", IDX_HEADS), ("c_gate", GROUP_WIDTH),
    ("d_b", GROUP_WIDTH), ("d_c", GROUP_WIDTH), ("d_h", GROUP_WIDTH), ("d_gate", GROUP_WIDTH),
)
IN_COLS = sum(w for _, w in IN_SPLITS)

kernel_name = "hybrid_parallel_mla_diff_dsa_conv"


def _rmsnorm(x, g):
    xf = x.astype(jnp.float32)
    y = xf * lax.rsqrt(jnp.mean(xf * xf, axis=-1, keepdims=True) + NORM_EPS)
    return (y * g.astype(jnp.float32)).astype(x.dtype)


def _split(p):
    out, off = {}, 0
    for name, width in IN_SPLITS:
        out[name] = p[..., off:off + width]
        off += width
    return out


def _rope(x, cos, sin):
    half = x.shape[-1] // 2
    x1, x2 = x[..., :half], x[..., half:]
    c, s = cos.astype(x.dtype), sin.astype(x.dtype)
    return jnp.concatenate([x1 * c - x2 * s, x2 * c + x1 * s], axis=-1)


def _rel_bucket(rel):
    nb = NUM_BUCKETS // 2
    max_exact = nb // 2
    ret = jnp.where(rel > 0, nb, 0)
    n = jnp.abs(rel)
    nf = jnp.maximum(n, max_exact).astype(jnp.float32)
    large = max_exact + (jnp.log(nf / max_exact) / math.log(MAX_DISTANCE / max_exact)
                         * (nb - max_exact)).astype(jnp.int32)
    large = jnp.minimum(large, nb - 1)
    return ret + jnp.where(n < max_exact, n, large)


def _masked_softmax(logits, mask):
    return jax.nn.softmax(jnp.where(mask, logits.astype(jnp.float32), NEG), axis=-1)


def _chunk_mask(qpos, kpos):
    return (kpos // CHUNK)[None, :] <= (qpos // CHUNK)[:, None]


def _sweep(fn, S):
    out = lax.map(fn, jnp.arange(S // Q_BLOCK, dtype=jnp.int32) * Q_BLOCK)
    out = jnp.moveaxis(out, 0, 1)
    return out.reshape((out.shape[0], S) + out.shape[3:])


def _mla(cq, ckv, krope, qa_g, w_uq, kva_g, w_ukv, cos, sin):
    B, S, _ = cq.shape
    H = GROUP_HEADS
    q = (_rmsnorm(cq, qa_g) @ w_uq).reshape(B, S, H, MLA_NOPE + MLA_ROPE)
    q_nope = q[..., :MLA_NOPE]
    q_rope = _rope(q[..., MLA_NOPE:], cos[:, None], sin[:, None])
    kv = (_rmsnorm(ckv, kva_g) @ w_ukv).reshape(B, S, H, MLA_NOPE + MLA_V)
    k_nope, v = kv[..., :MLA_NOPE], kv[..., MLA_NOPE:]
    k_rope = _rope(krope, cos, sin)
    scale = (MLA_NOPE + MLA_ROPE) ** -0.5
    kpos = jnp.arange(S, dtype=jnp.int32)

    def block(start):
        qpos = start + jnp.arange(Q_BLOCK, dtype=jnp.int32)
        qn = lax.dynamic_slice_in_dim(q_nope, start, Q_BLOCK, axis=1)
        qr = lax.dynamic_slice_in_dim(q_rope, start, Q_BLOCK, axis=1)
        logits = (jnp.einsum('bqhd,bkhd->bhqk', qn, k_nope)
                  + jnp.einsum('bqhr,bkr->bhqk', qr, k_rope)) * scale
        p = _masked_softmax(logits, _chunk_mask(qpos, kpos))
        return jnp.einsum('bhqk,bkhd->bqhd', p.astype(v.dtype), v)

    return _sweep(block, S).reshape(B, S, H * MLA_V)


def _diff_attn(q, k, v, lam_params, subln_g, rel_bias, lambda_init):
    B, S, _ = q.shape
    H = GROUP_HEADS
    q = q.reshape(B, S, H, 2, DIFF_QK)
    k = k.reshape(B, S, H, 2, DIFF_QK)
    v = v.reshape(B, S, H, HEAD_DIM)
    lp = lam_params.astype(jnp.float32)
    lam = jnp.exp(jnp.sum(lp[0] * lp[1])) - jnp.exp(jnp.sum(lp[2] * lp[3])) + lambda_init
    table = rel_bias[:, :2 * H].reshape(NUM_BUCKETS, H, 2)
    scale = DIFF_QK ** -0.5
    kpos = jnp.arange(S, dtype=jnp.int32)

    def block(start):
        qpos = start + jnp.arange(Q_BLOCK, dtype=jnp.int32)
        qb = lax.dynamic_slice_in_dim(q, start, Q_BLOCK, axis=1)
        bias = jnp.transpose(table[_rel_bucket(kpos[None, :] - qpos[:, None])], (2, 3, 0, 1))
        logits = jnp.einsum('bqhmd,bkhmd->bhmqk', qb, k) * scale + bias
        p = _masked_softmax(logits, _chunk_mask(qpos, kpos))
        a = p[:, :, 0] - lam * p[:, :, 1]
        return jnp.einsum('bhqk,bkhd->bqhd', a.astype(v.dtype), v)

    o = _sweep(block, S)
    o = _rmsnorm(o, subln_g) * (1.0 - lambda_init)
    return o.reshape(B, S, H * HEAD_DIM)


def _dsa(q, k, v, qidx, kidx, widx, rel_bias, topk):
    B, S, _ = q.shape
    H = GROUP_HEADS
    q = q.reshape(B, S, H, HEAD_DIM)
    k = k.reshape(B, S, H, HEAD_DIM)
    v = v.reshape(B, S, H, HEAD_DIM)
    qidx = qidx.reshape(B, S, IDX_HEADS, IDX_DIM)
    table = rel_bias[:, 2 * H:]
    scale = HEAD_DIM ** -0.5
    idx_scale = (IDX_HEADS ** -0.5) * (IDX_DIM ** -0.5)
    kpos = jnp.arange(S, dtype=jnp.int32)
    gather = jax.vmap(lambda xb, ib: xb[ib])

    def block(start):
        qpos = start + jnp.arange(Q_BLOCK, dtype=jnp.int32)
        qi = lax.dynamic_slice_in_dim(qidx, start, Q_BLOCK, axis=1)
        wi = lax.dynamic_slice_in_dim(widx, start, Q_BLOCK, axis=1).astype(jnp.float32)
        score = jax.nn.relu(jnp.einsum('bqhd,bkd->bqhk', qi, kidx).astype(jnp.float32))
        index = jnp.einsum('bqh,bqhk->bqk', wi, score) * idx_scale
        index = jnp.where(_chunk_mask(qpos, kpos)[None], index, NEG)
        _, sel = lax.top_k(index, topk)
        k_sel = gather(k, sel)
        v_sel = gather(v, sel)
        valid = (sel // CHUNK) <= (qpos // CHUNK)[None, :, None]
        bias = jnp.transpose(table[_rel_bucket(sel - qpos[None, :, None])], (0, 3, 1, 2))
        qb = lax.dynamic_slice_in_dim(q, start, Q_BLOCK, axis=1)
        logits = jnp.einsum('bqhd,bqjhd->bhqj', qb, k_sel) * scale + bias
        p = _masked_softmax(logits, valid[:, None])
        return jnp.einsum('bhqj,bqjhd->bqhd', p.astype(v.dtype), v_sel)

    return _sweep(block, S).reshape(B, S, H * HEAD_DIM)


def _short_conv(b, c, h, conv_w):
    u = c * h
    S = u.shape[1]
    up = jnp.pad(u, ((0, 0), (CONV_WIDTH - 1, 0), (0, 0)))
    y = conv_w[0] * up[:, 0:S]
    for j in range(1, CONV_WIDTH):
        y = y + conv_w[j] * up[:, j:j + S]
    return b * y


def setup_inputs(seed: int = 0) -> dict:
    key = jax.random.key(seed)
    ks = jax.random.split(key, 14)
    f32 = jnp.float32
    nrm = lambda k, shape: jax.random.normal(k, shape, f32)
    H = GROUP_HEADS
    return {
        "x": nrm(ks[0], (BATCH, SEQ, D_MODEL)),
        "norm_g": 1.0 + 0.05 * nrm(ks[1], (DEPTH, D_MODEL)),
        "w_in": nrm(ks[2], (DEPTH, D_MODEL, IN_COLS)) * D_MODEL ** -0.5,
        "mla_qa_g": 1.0 + 0.05 * nrm(ks[3], (DEPTH, MLA_Q_LORA)),
        "mla_w_uq": nrm(ks[4], (DEPTH, MLA_Q_LORA, H * (MLA_NOPE + MLA_ROPE))) * MLA_Q_LORA ** -0.5,
        "mla_kva_g": 1.0 + 0.05 * nrm(ks[5], (DEPTH, MLA_KV_LORA)),
        "mla_w_ukv": nrm(ks[6], (DEPTH, MLA_KV_LORA, H * (MLA_NOPE + MLA_V))) * MLA_KV_LORA ** -0.5,
        "diff_lambda": 0.1 * nrm(ks[7], (DEPTH, 4, DIFF_QK)),
        "diff_subln_g": 1.0 + 0.05 * nrm(ks[8], (DEPTH, HEAD_DIM)),
        "conv_w": nrm(ks[9], (DEPTH, CONV_WIDTH, GROUP_WIDTH)) * CONV_WIDTH ** -0.5,
        "w_out": nrm(ks[10], (DEPTH, D_MIX, D_MODEL)) * D_MIX ** -0.5,
        "rel_bias": 0.5 * nrm(ks[11], (NUM_BUCKETS, N_BIAS_HEADS)),
        "final_g": 1.0 + 0.05 * nrm(ks[12], (D_MODEL,)),
    }


def reference(x, norm_g, w_in, mla_qa_g, mla_w_uq, mla_kva_g, mla_w_ukv, diff_lambda,
              diff_subln_g, conv_w, w_out, rel_bias, final_g):
    S = x.shape[1]
    topk = min(TOPK_MAX, S // 4)
    half = MLA_ROPE // 2
    inv_freq = ROPE_BASE ** (-jnp.arange(half, dtype=jnp.float32) / half)
    ang = jnp.arange(S, dtype=jnp.float32)[:, None] * inv_freq[None, :]
    cos, sin = jnp.cos(ang), jnp.sin(ang)

    for l in range(DEPTH):
        lambda_init = 0.8 - 0.6 * math.exp(-0.3 * l)
        h = _rmsnorm(x, norm_g[l])
        p = _split(h @ w_in[l])
        o_a = _mla(p["a_cq"], p["a_ckv"], p["a_krope"], mla_qa_g[l], mla_w_uq[l],
                   mla_kva_g[l], mla_w_ukv[l], cos, sin)
        o_b = _diff_attn(p["b_q"], p["b_k"], p["b_v"], diff_lambda[l], diff_subln_g[l],
                         rel_bias, lambda_init)
        o_c = _dsa(p["c_q"], p["c_k"], p["c_v"], p["c_qidx"], p["c_kidx"], p["c_widx"],
                   rel_bias, topk)
        o_d = _short_conv(p["d_b"], p["d_c"], p["d_h"], conv_w[l])
        y = jnp.concatenate([
            o_a * jax.nn.silu(p["a_gate"]),
            o_b * jax.nn.silu(p["b_gate"]),
            o_c * jax.nn.silu(p["c_gate"]),
            o_d * jax.nn.silu(p["d_gate"]),
        ], axis=-1)
        x = x + y @ w_out[l]
    return _rmsnorm(x, final_g)
```

```python
import math
from contextlib import ExitStack

import numpy as np
import ml_dtypes

import concourse.bass as bass
import concourse.mybir as mybir
from concourse.bass_utils import run_bass_kernel_spmd

F32 = mybir.dt.float32
BF16 = mybir.dt.bfloat16
ALU = mybir.AluOpType
AF = mybir.ActivationFunctionType
AX = mybir.AxisListType

D = 1024
IN_COLS = 4040
DEPTH = 2
EPS = 1e-6
NEGM = -30000.0

OFF = {}
_o = 0
for _n, _w in (("a_cq", 256), ("a_ckv", 128), ("a_krope", 32), ("a_gate", 256),
               ("b_q", 256), ("b_k", 256), ("b_v", 256), ("b_gate", 256),
               ("c_q", 256), ("c_k", 256), ("c_v", 256),
               ("c_qidx", 256), ("c_kidx", 32), ("c_widx", 8), ("c_gate", 256),
               ("d_b", 256), ("d_c", 256), ("d_h", 256), ("d_gate", 256)):
    OFF[_n] = _o
    _o += _w
assert _o == IN_COLS


class Buf:
    __slots__ = ("name", "w", "r")

    def __init__(self, name=""):
        self.name = name
        self.w = None
        self.r = {}


class Op:
    __slots__ = ("eng", "fn", "deps", "inc", "pos", "val", "dma", "dsem", "dval", "dprev")


EPOCH = 30000
NDSEM = 8
ENGS = ("pe", "act", "dve", "pool", "sp")


class Sched:
    def __init__(self, nc):
        self.nc = nc
        self.ops = {e: [] for e in ENGS}

    def add(self, eng, fn, reads=(), writes=(), dma=False):
        op = Op()
        op.eng = eng
        op.fn = fn
        op.dma = dma
        op.inc = False
        op.val = 0
        lst = self.ops[eng]
        op.pos = len(lst)
        deps = {}
        for b in reads:
            if b.w is not None:
                deps[id(b.w)] = b.w
        for b in writes:
            if b.w is not None:
                deps[id(b.w)] = b.w
            for r in b.r.values():
                deps[id(r)] = r
        keep = []
        for d in deps.values():
            if not d.dma and d.eng == eng:
                if eng == "pe":
                    continue
                if op.pos - d.pos > 1:
                    continue
            keep.append(d)
            if not d.dma:
                d.inc = True
        op.deps = keep
        for b in reads:
            b.r[("d", id(op)) if dma else eng] = op
        for b in writes:
            b.w = op
            b.r = {}
        lst.append(op)
        return op

    def barrier(self):
        bar = {"pos": {e: len(self.ops[e]) for e in ENGS}, "snap": {}}
        for e in ENGS:
            for op in reversed(self.ops[e]):
                if not op.dma and op.fn is not None:
                    op.inc = True
                    break
        for e in ENGS:
            op = Op()
            op.eng = e
            op.fn = None
            op.dma = False
            op.inc = False
            op.val = 0
            op.pos = len(self.ops[e])
            op.deps = bar
            self.ops[e].append(op)

    def emit(self, stack):
        nc = self.nc
        engsems = {}
        for eng in ENGS:
            cnt = 0
            for op in self.ops[eng]:
                if op.dma:
                    continue
                if op.fn is None:
                    op.deps["snap"][eng] = cnt
                    continue
                if op.inc:
                    cnt += 1
                    op.val = cnt
            nep = (cnt + EPOCH - 1) // EPOCH + 1
            engsems[eng] = [stack.enter_context(nc.semaphore(f"s_{eng}_{i}")) for i in range(nep)]
        for eng in ENGS:
            dsems = None
            vals = [0] * NDSEM
            k = 0
            for op in self.ops[eng]:
                if op.fn is None:
                    op.deps["snap"]["d_" + eng] = (dsems, list(vals))
                    continue
                if not op.dma:
                    continue
                if dsems is None:
                    dsems = [stack.enter_context(nc.semaphore(f"d_{eng}_{i}")) for i in range(NDSEM)]
                s = k % NDSEM
                op.dsem = dsems[s]
                op.dprev = vals[s]
                vals[s] += 16
                op.dval = vals[s]
                k += 1

        def signal(d):
            if d.dma:
                return d.dsem, d.dval
            ep = (d.val - 1) // EPOCH
            return engsems[d.eng][ep], d.val - ep * EPOCH

        def run(e, eng):
            waited = {}
            for op in self.ops[eng]:
                if op.fn is None:
                    snap = op.deps["snap"]
                    for e2 in ENGS:
                        c = snap[e2]
                        if c > 0 and e2 != eng:
                            ep = (c - 1) // EPOCH
                            sem, val = engsems[e2][ep], c - ep * EPOCH
                            if waited.get(id(sem), 0) < val:
                                e.wait_ge(sem, val)
                                waited[id(sem)] = val
                        ds, vs = snap["d_" + e2]
                        if ds is not None:
                            for sem, val in zip(ds, vs):
                                if val > 0 and waited.get(id(sem), 0) < val:
                                    e.wait_ge(sem, val)
                                    waited[id(sem)] = val
                    continue
                waits = {}
                for d in op.deps:
                    sem, val = signal(d)
                    k = id(sem)
                    if k not in waits or waits[k][1] < val:
                        waits[k] = (sem, val)
                if op.dma and op.dprev > 0:
                    k = id(op.dsem)
                    if k not in waits or waits[k][1] < op.dprev:
                        waits[k] = (op.dsem, op.dprev)
                for k, (sem, val) in waits.items():
                    if waited.get(k, 0) < val:
                        e.wait_ge(sem, val)
                        waited[k] = val
                ins = op.fn(e)
                if op.dma:
                    ins.then_inc(op.dsem, 16)
                elif op.inc:
                    sem, _ = signal(op)
                    ins.then_inc(sem, 1)

        with nc.Block() as block:
            block.tensor(lambda e: run(e, "pe"))
            block.scalar(lambda e: run(e, "act"))
            block.vector(lambda e: run(e, "dve"))
            block.gpsimd(lambda e: run(e, "pool"))
            block.sync(lambda e: run(e, "sp"))


def _rel_bucket_np(rel):
    nb = 16
    max_exact = 8
    ret = np.where(rel > 0, nb, 0)
    n = np.abs(rel)
    nf = np.maximum(n, max_exact).astype(np.float32)
    large = max_exact + (np.log(nf / np.float32(max_exact)) / np.float32(math.log(128 / max_exact))
                         * np.float32(nb - max_exact)).astype(np.int32)
    large = np.minimum(large, nb - 1)
    return ret + np.where(n < max_exact, n, large)


def _host_consts(S):
    c = {}
    c["ident_f"] = np.eye(128, dtype=np.float32)
    c["ones_f"] = np.ones((128, 128), dtype=np.float32)
    q = np.arange(128)[:, None]
    k = np.arange(128)[None, :]
    cm = np.where((k // 64) <= (q // 64), 0.0, NEGM).astype(np.float32)
    c["cm"] = cm
    half = 16
    inv_freq = (np.float32(10000.0) ** (-np.arange(half, dtype=np.float32) / np.float32(half))).astype(np.float32)
    ang = np.arange(S, dtype=np.float32)[:, None] * inv_freq[None, :]
    cos = np.cos(ang).astype(np.float32).T
    sin = np.sin(ang).astype(np.float32).T
    cc = np.concatenate([cos, cos], 0)
    ss = np.concatenate([-sin, sin], 0)
    c["cc4"] = np.ascontiguousarray(np.tile(cc, (4, 1)))
    c["ss4"] = np.ascontiguousarray(np.tile(ss, (4, 1)))
    c["bk_diag"] = _rel_bucket_np(k - q)
    c["bk_sub"] = _rel_bucket_np(k - 128 - q)
    return c


PL = 8 + 2 + 1 + 6 + 128 + 64


class Prog:
    def __init__(self, nc, st, S, depth=DEPTH, mixers="abcd", dbg=False, nbis=12):
        self.nc = nc
        self.st = st
        self.S = S
        self.NT = S // 128
        self.NB = S // 512
        self.depth = depth
        self.mixers = mixers
        self.nbis = nbis
        self.s = Sched(nc)
        NT = self.NT
        dt = nc.dram_tensor
        self.x_d = dt("x", [S, D], F32, kind="ExternalInput").ap()
        self.win_d = dt("w_in", [depth, D, IN_COLS], F32, kind="ExternalInput").ap()
        self.wuq_d = dt("w_uq", [depth, 256, 384], F32, kind="ExternalInput").ap()
        self.wukv_d = dt("w_ukv", [depth, 128, 512], F32, kind="ExternalInput").ap()
        self.wout_d = dt("w_out", [depth, D, D], F32, kind="ExternalInput").ap()
        self.npk = depth * PL + D
        self.pk_d = dt("pk", [128, self.npk], F32, kind="ExternalInput").ap()
        self.cst_d = dt("cst", [128, 384], F32, kind="ExternalInput").ap()
        self.bias_d = dt("biasblk", [128, 2 * 12 * 128 + 12], F32, kind="ExternalInput").ap()
        self.rmask_d = dt("rmask", [128, 8], F32, kind="ExternalInput").ap()
        self.cc_d = dt("cc4", [128, S], F32, kind="ExternalInput").ap()
        self.ss_d = dt("ss4", [128, S], F32, kind="ExternalInput").ap()
        self.out_d = dt("out", [S, D], F32, kind="ExternalOutput").ap()
        self.xres_d = dt("xres", [S, D], F32).ap()
        self.yT_d = dt("yT", [D, S], BF16, kind="ExternalOutput" if dbg else "Internal").ap()
        self.gate_d = dt("gates", [3, S, 256], BF16).ap()
        o = {}
        off = 0

        def reg(name, nbytes):
            nonlocal off
            o[name] = off
            off += (nbytes + 31) // 32 * 32

        SR = max(S, 4096)
        NTR = SR // 128
        reg("P", 14336)
        reg("xT", 16 * SR)
        reg("rbc", 4 * SR)
        reg("wst", 2 * 4096)
        reg("wbf", 3 * 4096)
        reg("st", 4096)
        reg("QT", 4 * SR)
        reg("KT", 4 * SR)
        reg("VT", NTR * 4 * 65 * 2)
        reg("QI", 4 * SR + 32)
        reg("X1", 2 * SR)
        reg("W", 16384)
        reg("X2", 1024)
        self.o = o
        self.total = off
        self.A = st.enter_context(nc.sbuf_tensor("arena", [128, off // 4], F32))
        self.Ab = self.A.bitcast(BF16)
        self.A8 = self.A.bitcast(mybir.dt.float8e5)
        psh = [st.enter_context(nc.psum_tensor(f"ps{i}", [128, 1024], F32)) for i in range(4)]
        self.psp = [p[:, :] for p in psh]
        self.ps = [psh[i // 2][:, (i % 2) * 512:(i % 2 + 1) * 512] for i in range(8)]
        self.psb = [psh[i // 2].bitcast(BF16)[:, (i % 2) * 1024:(i % 2 + 1) * 1024] for i in range(8)]
        self.pstok = [Buf(f"ps{i}") for i in range(8)]
        self.rr = {}
        self.pools = {"A": [0, 1, 2, 3], "B": [4, 5, 6, 7], "as": [0, 1, 2, 3], "ao": [4, 5, 6, 7], "ia": [0, 1, 2, 3],
                      "ib": [4, 5, 6, 7]}
        self.trpool = "A"
        self.nPT = 4

    def f32(self, region, boff, n):
        b = (self.o[region] + boff) // 4
        return self.A[:, b:b + n]

    def bf(self, region, boff, n):
        b = (self.o[region] + boff) // 2
        return self.Ab[:, b:b + n]

    def bank(self, name):
        lst = self.pools[name]
        name = "A" if lst[0] == 0 else "B"
        k = self.rr.get(name, 0)
        self.rr[name] = k + 1
        i = lst[k % len(lst)]
        return self.ps[i], self.psb[i], self.pstok[i]

    def bankpair(self, name):
        lst = self.pools[name]
        name = "A" if lst[0] == 0 else "B"
        k = self.rr.get(name, 0)
        if k % 2:
            k += 1
        self.rr[name] = k + 2
        i = lst[k % len(lst)]
        return self.psp[i // 2], [self.pstok[i], self.pstok[i + 1]]

    def bankA(self):
        return self.bank("A")

    def bankB(self):
        return self.bank("B")

    def mm(self, out, lhsT, rhs, start, stop, R, Wt, **kw):
        self.s.add("pe", lambda e: e.matmul(out, lhsT, rhs, start=start, stop=stop, **kw), R, Wt)

    def tr(self, out, in_, ident, R, Wt):
        self.s.add("pe", lambda e: e.transpose(out, in_, ident), R, Wt)

    def act(self, out, in_, func, R, Wt, **kw):
        self.s.add("act", lambda e: e.activation(out, in_, func, **kw), R, Wt)

    def ts(self, eng, out, in0, s1, s2, op0, op1, R, Wt, **kw):
        if op1 is None:
            self.s.add(eng, lambda e: e.tensor_scalar(out, in0, s1, None, op0, **kw), R, Wt)
        else:
            self.s.add(eng, lambda e: e.tensor_scalar(out, in0, s1, s2, op0, op1, **kw), R, Wt)

    def tt(self, eng, out, in0, in1, op, R, Wt):
        self.s.add(eng, lambda e: e.tensor_tensor(out, in0, in1, op), R, Wt)

    def stt(self, out, in0, scalar, in1, op0, op1, R, Wt):
        self.s.add("dve", lambda e: e.scalar_tensor_tensor(out, in0, scalar, in1, op0, op1), R, Wt)

    def cp(self, eng, out, in_, R, Wt):
        if eng == "act":
            self.s.add("act", lambda e: e.activation(out, in_, AF.Copy), R, Wt)
        else:
            self.s.add(eng, lambda e: e.tensor_copy(out, in_), R, Wt)

    def memset(self, eng, ap, val, Wt):
        self.s.add(eng, lambda e: e.memset(ap, val), (), Wt)

    def dma(self, q, out, in_, R, Wt):
        self.s.add(q, lambda e: e.dma_start(out, in_), R, Wt, dma=True)


    def setup(self):
        S, NT = self.S, self.NT
        self.cst = self.f32("P", 0, 384)
        self.ident_f = self.cst[:, 0:128]
        self.ones_f = self.cst[:, 128:256]
        self.cm_f = self.cst[:, 256:384]
        self.identb = self.bf("P", 1536, 128)
        self.cmb = self.bf("P", 1792, 128)
        self.bd = self.bf("P", 2048, 12 * 128)
        self.bs = self.bf("P", 5120, 12 * 128)
        self.pk = self.f32("P", 8192, self.npk)
        self.cfar = self.f32("P", 8192 + 4 * self.npk, 12)
        self.T_const = Buf("const")
        T = self.T_const
        stg = self.f32("xT", 0, 3084)
        Tstg = Buf("stg")
        self.dma("sp", self.cst, self.cst_d, (), [T])
        self.dma("sp", self.pk, self.pk_d, (), [T])
        self.ident16k = self.bf("P", 14048, 128)
        self.rmask = self.f32("P", 14016, 8)
        self.dma("sp", self.rmask, self.rmask_d, (), [T])
        self.dma("sp", stg, self.bias_d, (), [Tstg])
        self.cp("dve", self.identb, self.ident_f, [T], [T])
        self.cp("dve", self.cmb, self.cm_f, [T], [T])
        self.ts("dve", self.ident16k, self.ident_f, 16384.0, None, ALU.mult, None, [T], [T])
        self.cp("dve", self.cfar, stg[:, 3072:3084], [Tstg], [T])
        for m in range(12):
            self.stt(self.bd[:, m * 128:(m + 1) * 128], stg[:, m * 128:(m + 1) * 128], self.cfar[:, m:m + 1],
                     self.cm_f, ALU.subtract, ALU.add, [Tstg, T], [T])
            self.ts("dve", self.bs[:, m * 128:(m + 1) * 128], stg[:, 1536 + m * 128:1536 + (m + 1) * 128],
                    self.cfar[:, m:m + 1], None, ALU.subtract, None, [Tstg, T], [T])
        self.ssq = self.f32("st", 0, NT)
        self.var = self.f32("st", 128, NT)
        self.rstd = self.f32("st", 256, NT)
        self.mhalf = self.f32("st", 384, NT)
        self.zeros_b = self.bf("st", 512, 260)
        self.mhalfw = self.f32("st", 1080, 2 * NT)
        self.memset("dve", self.mhalf, -0.5, [T])
        self.memset("dve", self.mhalfw, -0.5, [T])
        self.memset("dve", self.zeros_b, 0.0, [T])
        self.s.barrier()

    def pkl(self, l, off, n):
        return self.pk[:, l * PL + off: l * PL + off + n]

    def phase1(self, l, xsrc, Tx):
        S, NT, NB = self.S, self.NT, self.NB
        T = self.T_const
        xt = [self.f32("W", 0, 1024), self.f32("W", 4096, 1024)]
        Txt = [Buf("xt0"), Buf("xt1")]
        junk = self.bf("W", 8192, 1024)
        Tjunk = Buf("junk")
        diag = [self.f32("W", 10240 + i * 512, 128) for i in range(4)]
        Tdiag = [Buf(f"diag{i}") for i in range(4)]
        xTb = self.bf("xT", 0, 8 * S)
        self.xT = xTb
        self.TxT = [Buf(f"xT{t}") for t in range(NT)]
        Tssq = Buf("ssq")
        for t in range(NT):
            b = t % 2
            self.dma("sp", xt[b], xsrc[t * 128:(t + 1) * 128, :], [Tx[t]], [Txt[b]])
            self.act(junk, xt[b], AF.Square, [Txt[b]], [Tjunk, Tssq], accum_out=self.ssq[:, t:t + 1])
            for hb in range(2):
                ps, _, pt = self.bankA()
                for c4 in range(4):
                    c = hb * 4 + c4
                    self.tr(ps[:, c4 * 128:(c4 + 1) * 128], xt[b][:, c * 128:(c + 1) * 128], self.ident_f,
                            [Txt[b], T], [pt])
                dst = xTb.rearrange("p (c s) -> p c s", c=8)[:, hb * 4:(hb + 1) * 4, t * 128:(t + 1) * 128]
                self.cp("dve" if hb == 0 else "act", dst, ps.rearrange("p (c s) -> p c s", c=4), [pt], [self.TxT[t]])
        Trs = Buf("rstd")
        self.ts("dve", self.var, self.ssq, 1.0 / D, EPS, ALU.mult, ALU.add, [Tssq], [Trs])
        self.tt("pool", self.rstd, self.var, self.mhalf, ALU.pow, [Trs, T], [Trs])
        self.Trstd = Trs
        self.rbc = self.f32("rbc", 0, S)
        self.Trbc = [Buf(f"rbc{i}") for i in range(NB)]
        for TB in range(NB):
            self.bcast(self.rstd, TB, self.rbc[:, TB * 512:(TB + 1) * 512], [Trs], [self.Trbc[TB]], diag, Tdiag)

    def bcast(self, vec, TB, dst, R, Wt, diag, Tdiag):
        T = self.T_const
        ps, _, pt = self.bankA()
        for tt in range(4):
            t = TB * 4 + tt
            self.ts("dve", diag[tt], self.ident_f, vec[:, t:t + 1], None, ALU.mult, None, R + [T], [Tdiag[tt]])
            self.mm(ps[:, tt * 128:(tt + 1) * 128], self.ones_f, diag[tt], True, True, [Tdiag[tt], T], [pt])
        self.cp("act", dst, ps, [pt], Wt)

    def init_w(self):
        self.wst = [self.f32("wst", 0, 1024), self.f32("wst", 4096, 1024)]
        self.Twst = [Buf("wst0"), Buf("wst1")]
        self.wbf = [self.bf("wbf", i * 4096, 2048).rearrange("p (c n) -> p c n", c=8) for i in range(3)]
        self.Twbf = [Buf(f"wbf{i}") for i in range(3)]
        self.nw = 0

    def load_w(self, l, c0, n, slot, col, scale=1.0, src=None, gain=None, stg_view=None):
        k = self.nw % 2
        self.nw += 1
        T = self.T_const
        stg = self.wst[k].rearrange("p (c n) -> p c n", c=8)
        if src is None:
            src = self.win_d[l, :, c0:c0 + n].rearrange("(c p) n -> p c n", p=128)
        if isinstance(src, list):
            for dv, sv in src:
                self.dma("sp", dv, sv, (), [self.Twst[k]])
        else:
            self.dma("sp", stg[:, :, 0:n], src, (), [self.Twst[k]])
        for c in range(8):
            g = self.pkl(l, c, 1) if gain is None else gain(c)
            self.ts("pool", self.wbf[slot][:, c, col:col + n], stg[:, c, 0:n], g, float(scale),
                    ALU.mult, ALU.mult, [self.Twst[k], T], [self.Twbf[slot]])

    def proj_fm(self, lhs, M, TB, pool="A", ncol=512, c0=0):
        ps, psb, pt = self.bankA() if pool == "A" else self.bankB()
        xT3 = self.xT.rearrange("p (c s) -> p c s", c=8)
        R = [self.TxT[TB * 4 + i] for i in range(4)]
        for c in range(8):
            ap, toks = lhs(c)
            self.mm(ps[0:M, 0:ncol], ap, xT3[:, c, TB * 512 + c0:TB * 512 + c0 + ncol], c == 0, c == 7,
                    R + toks, [pt])
        return ps, pt

    def proj_tm(self, rhs, N, t, pool="A"):
        ps, psb, pt = self.bankA() if pool == "A" else self.bankB()
        xT3 = self.xT.rearrange("p (c s) -> p c s", c=8)
        for c in range(8):
            ap, toks = rhs(c)
            self.mm(ps[:, 0:N], xT3[:, c, t * 128:(t + 1) * 128], ap, c == 0, c == 7,
                    [self.TxT[t]] + toks, [pt])
        return ps, pt

    def wsl(self, slot, col, n):
        return lambda c: (self.wbf[slot][:, c, col:col + n], [self.Twbf[slot]])

    def mixer_d(self, l):
        S, NB = self.S, self.NB
        T = self.T_const
        u = self.f32("QI", 0, S + 2)
        Tu = Buf("u")
        self.memset("dve", u[:, 0:2], 0.0, [Tu])
        t1 = self.f32("W", 0, 512)
        t2 = self.f32("W", 2048, 512)
        acc = self.f32("W", 4096, 512)
        yo = [self.bf("W", 6144, 512), self.bf("W", 7168, 512)]
        Tt1, Tt2, Tacc = Buf("t1"), Buf("t2"), Buf("acc")
        Tyo = [Buf("yo0"), Buf("yo1")]
        k = 0
        for fc in range(2):
            for gi, name in enumerate(("d_b", "d_c", "d_h", "d_gate")):
                self.load_w(l, OFF[name] + fc * 128, 128, gi // 2, (gi % 2) * 128)
            cw = self.pkl(l, 11 + fc * 3, 3)
            for TB in range(NB):
                pool = "A" if TB % 2 == 0 else "B"
                psb_, ptb = self.proj_fm(self.wsl(0, 0, 128), 128, TB, pool)
                psc, ptc = self.proj_fm(self.wsl(0, 128, 128), 128, TB, pool)
                psh, pth = self.proj_fm(self.wsl(1, 0, 128), 128, TB, pool)
                psg, ptg = self.proj_fm(self.wsl(1, 128, 128), 128, TB, pool)
                rb = self.rbc[:, TB * 512:(TB + 1) * 512]
                Trb = [self.Trbc[TB]]
                ub = u[:, 2 + TB * 512: 2 + (TB + 1) * 512]
                self.tt("dve", t1, psc, rb, ALU.mult, [ptc] + Trb, [Tt1])
                self.tt("dve", t2, psh, rb, ALU.mult, [pth] + Trb, [Tt2])
                self.tt("dve", ub, t1, t2, ALU.mult, [Tt1, Tt2], [Tu])
                self.ts("dve", acc, u[:, TB * 512: TB * 512 + 512], cw[:, 0:1], None, ALU.mult, None,
                        [Tu, T], [Tacc])
                self.stt(acc, u[:, TB * 512 + 1: TB * 512 + 513], cw[:, 1:2], acc, ALU.mult, ALU.add,
                         [Tu, T, Tacc], [Tacc])
                self.stt(acc, ub, cw[:, 2:3], acc, ALU.mult, ALU.add, [Tu, T, Tacc], [Tacc])
                self.tt("dve", t1, psb_, rb, ALU.mult, [ptb] + Trb, [Tt1])
                self.tt("dve", acc, acc, t1, ALU.mult, [Tacc, Tt1], [Tacc])
                self.tt("dve", t2, psg, rb, ALU.mult, [ptg] + Trb, [Tt2])
                self.act(t1, t2, AF.Silu, [Tt2], [Tt1])
                y = yo[k % 2]
                Ty = Tyo[k % 2]
                k += 1
                self.tt("dve", y, acc, t1, ALU.mult, [Tacc, Tt1], [Ty])
                r0 = 768 + fc * 128
                self.dma("pool", self.yT_d[r0:r0 + 128, TB * 512:(TB + 1) * 512], y, [Ty],
                         [self.TyT[(r0 // 128, TB)]])

    def outproj(self, l, xsrc, Tx, Tdst, last):
        S, NT, NB = self.S, self.NT, self.NB
        T = self.T_const
        wo = self.bf("QT", 0, 8 * 1024).rearrange("p (c n) -> p c n", c=8)
        Two = Buf("wo")
        for j in range(8):
            k = self.nw % 2
            self.nw += 1
            stg = self.wst[k].rearrange("p (c n) -> p c n", c=8)
            self.dma("sp", stg, self.wout_d[l, :, j * 128:(j + 1) * 128].rearrange("(c p) n -> p c n", p=128),
                     (), [self.Twst[k]])
            self.cp("pool" if j % 2 else "dve", wo[:, :, j * 128:(j + 1) * 128], stg, [self.Twst[k]], [Two])
        base = 16384
        yTb = [self.bf("QT", base + i * 8192, 4096).rearrange("p (c s) -> p c s", c=8) for i in range(2)]
        TyTb = [Buf("yTb0"), Buf("yTb1")]
        xt = [self.f32("QT", base + 16384 + i * 4096, 1024) for i in range(2)]
        Txt = [Buf("oxt0"), Buf("oxt1")]
        xn = [self.f32("QT", base + 24576 + i * 4096, 1024) for i in range(2)]
        Txn = [Buf("xn0"), Buf("xn1")]
        junk = self.bf("W", 8192, 1024)
        Tjunk = Buf("junk2")
        st4 = self.f32("st", 1040, 8)
        Tst4 = Buf("st4")
        fg = self.pk[:, self.depth * PL: self.depth * PL + D]
        dst = self.xres_d if not last else self.out_d
        for TB in range(NB):
            yb = yTb[TB % 2]
            Ty = TyTb[TB % 2]
            self.dma("sp", yb, self.yT_d[:, TB * 512:(TB + 1) * 512].rearrange("(c p) s -> p c s", p=128),
                     [self.TyT[(r, TB)] for r in range(8)], [Ty])
            for tt_ in range(4):
                t = TB * 4 + tt_
                b = t % 2
                self.dma("sp", xt[b], xsrc[t * 128:(t + 1) * 128, :], [Tx[t]], [Txt[b]])
                for half in range(2):
                    ps, _, pt = self.bankA()
                    for c in range(8):
                        self.mm(ps, yb[:, c, tt_ * 128:(tt_ + 1) * 128], wo[:, c, half * 512:(half + 1) * 512],
                                c == 0, c == 7, [Ty, Two], [pt])
                    self.tt("dve", xn[b][:, half * 512:(half + 1) * 512], ps, xt[b][:, half * 512:(half + 1) * 512],
                            ALU.add, [pt, Txt[b]], [Txn[b]])
                if not last:
                    self.dma("pool", dst[t * 128:(t + 1) * 128, :], xn[b], [Txn[b]], [Tdst[t]])
                else:
                    c0 = (t % 2) * 4
                    self.act(junk, xn[b], AF.Square, [Txn[b]], [Tjunk, Tst4], accum_out=st4[:, c0:c0 + 1])
                    self.ts("dve", st4[:, c0 + 1:c0 + 2], st4[:, c0:c0 + 1], 1.0 / D, EPS, ALU.mult, ALU.add,
                            [Tst4], [Tst4])
                    self.tt("pool", st4[:, c0 + 2:c0 + 3], st4[:, c0 + 1:c0 + 2], self.mhalf[:, 0:1], ALU.pow,
                            [Tst4, T], [Tst4])
                    self.stt(xn[b], xn[b], st4[:, c0 + 2:c0 + 3], fg, ALU.mult, ALU.mult, [Txn[b], Tst4, T], [Txn[b]])
                    self.dma("pool", dst[t * 128:(t + 1) * 128, :], xn[b], [Txn[b]], [Tdst[t]])

    def init_attn(self):
        S, NT = self.S, self.NT
        self.QT3 = self.bf("QT", 0, 2 * S).rearrange("p (c s) -> p c s", c=2)
        self.KT3 = self.bf("KT", 0, 2 * S).rearrange("p (c s) -> p c s", c=2)
        self.VT4 = self.bf("VT", 0, NT * 260).rearrange("p (t h d) -> p t h d", t=NT, h=4)
        self.TQT = [Buf(f"QT{i}") for i in range(self.NB)]
        self.TKT = [Buf(f"KT{i}") for i in range(self.NB)]
        self.TVT = [Buf(f"VT{t}") for t in range(NT)]
        self.PT = [self.bf("W", 8192 + i * 1024, 512) for i in range(4)]
        self.TPT = [Buf(f"PT{i}") for i in range(4)]
        self.npt = 0
        self.nPT = 4
        self.ytm = self.bf("W", 12288, 1024).rearrange("p (t f) -> p t f", t=4)
        self.Tytm = Buf("ytm")
        self.sg = self.bf("W", 14336, 1024).rearrange("p (t f) -> p t f", t=4)
        self.Tsg = Buf("sg")
        self.yTs = self.bf("W", 6144, 1024).rearrange("p (c s) -> p c s", c=2)
        self.TyTs = Buf("yTs")

    def set_ones(self):
        for t in range(self.NT):
            self.memset("pool", self.VT4[:, t, :, 64:65], 1.0, [self.TVT[t]])

    def attn_map(self, I, qk, vfn, extra, cbias, pairs=True):
        g = self.attn_map_gen(I, qk, vfn, extra, cbias, pairs=pairs)
        while True:
            try:
                next(g)
            except StopIteration as e:
                return e.value

    def attn_map_gen(self, I, qk, vfn, extra, cbias, spool="A", opool="B", LA=2, pairs=False):
        T = self.T_const
        O, _, Ot = self.bank(opool)
        self.mm(O[:, 0:260], self.identb, self.zeros_b, True, True, [T], [Ot])
        if pairs:
            units = [(2 * p, 2 * p + 1) for p in range(2 * I)] + [(j,) for j in range(4 * I, 4 * I + 4)]
            LA = 1
        else:
            units = [(j,) for j in range(4 * I + 4)]
        pend = {}
        for s_ in range(len(units) + LA):
            if s_ > 0:
                yield
            if s_ < len(units):
                u = units[s_]
                if pairs:
                    psp, ptoks = self.bankpair(spool)
                    pi = self.npt % 2
                    P = self.bf("W", 8192 + pi * 2048, 1024)
                    Pt = self.TPT[pi]
                else:
                    ps1, _, pt1 = self.bank(spool)
                    psp, ptoks = ps1, [pt1]
                    pi = self.npt % self.nPT
                    P, Pt = self.PT[pi], self.TPT[pi]
                self.npt += 1
                info = []
                for k_, j in enumerate(u):
                    r0 = max(0, j - 4 * I)
                    a = r0 * 128
                    ps = psp[:, k_ * 512:(k_ + 1) * 512]
                    pt = ptoks[k_]
                    adds = []
                    for tt in range(r0, 4):
                        for item in extra(j, tt):
                            adds.append((tt, item[0], item[1], item[2] if len(item) > 2 else self.identb))
                    n = len(qk)
                    for k, (kfn, qfn, base) in enumerate(qk):
                        kap, ktok = kfn(j)
                        qap, qtok = qfn(I * 512 + a, (I + 1) * 512)
                        kw = {"tile_position": (96, 0)} if base == 96 else {}
                        self.mm(ps[:, a:512], kap, qap, k == 0, (k == n - 1) and not adds, ktok + qtok, [pt], **kw)
                    for k, (tt, ap, toks, rhs_) in enumerate(adds):
                        self.mm(ps[:, tt * 128:(tt + 1) * 128], ap, rhs_, False, k == len(adds) - 1,
                                toks + [T], [pt], skip_group_check=True)
                    info.append((j, r0, k_))
                a0 = info[0][1] * 128 if len(u) == 1 else 0
                w = 512 * len(u)
                if cbias is None:
                    self.act(P[:, a0:w], psp[:, a0:w], AF.Exp, list(ptoks), [Pt])
                else:
                    self.act(P[:, a0:w], psp[:, a0:w], AF.Exp, list(ptoks) + [T], [Pt], bias=cbias)
                pend[s_] = (P, Pt, info)
            sp = s_ - LA
            if sp >= 0:
                P, Pt, info = pend.pop(sp)
                for (j, r0, k_) in info:
                    vap, vtok = vfn(j)
                    for tt in range(r0, 4):
                        self.mm(O[:, tt * 65:(tt + 1) * 65], P[:, k_ * 512 + tt * 128:k_ * 512 + (tt + 1) * 128], vap,
                                False, j == 4 * I + tt, [Pt] + vtok, [Ot], skip_group_check=True)
        return O, Ot

    def run_chain(self, items, LA):
        prev = None
        for (pre, make, post, nunits) in items:
            if pre is not None:
                pre()
            g = make()
            res = None
            alive = True
            for _ in range(LA):
                try:
                    next(g)
                except StopIteration as e:
                    res = e.value
                    alive = False
                    break
            if prev is not None:
                self._finish(prev)
            for _ in range(nunits - LA):
                if not alive:
                    break
                try:
                    next(g)
                except StopIteration as e:
                    res = e.value
                    alive = False
            prev = (g, post, alive, res)
        if prev is not None:
            self._finish(prev)

    def flush_store(self):
        if getattr(self, "deferred", None) is not None:
            f = self.deferred
            self.deferred = None
            f()

    def _finish(self, p):
        g, post, alive, res = p
        while alive:
            try:
                next(g)
            except StopIteration as e:
                res = e.value
                alive = False
        post(*res)

    def load_sg(self, mi, I):
        self.dma("sp", self.sg, self.gate_d[mi, I * 512:(I + 1) * 512, :].rearrange("(t p) f -> p t f", p=128),
                 [self.Tgate[(mi, I * 4 + tt)] for tt in range(4)], [self.Tsg])

    def store_y(self, mi, I, r0=None, nfc=2):
        T = self.T_const
        if r0 is None:
            r0 = mi * 2
        for fc in range(nfc):
            ps, psb, pt = self.bank(self.trpool)
            for tt in range(4):
                self.tr(psb[:, tt * 128:(tt + 1) * 128], self.ytm[:, tt, fc * 128:(fc + 1) * 128], self.identb,
                        [self.Tytm, T], [pt])
            self.cp("act" if self.trpool == "as" else "dve", self.yTs[:, fc, :], psb[:, 0:512], [pt], [self.TyTs])
        self.dma("pool", self.yT_d[r0 * 128:(r0 + nfc) * 128, I * 512:(I + 1) * 512].rearrange("(c p) s -> p c s", p=128),
                 self.yTs[:, 0:nfc, :], [self.TyTs], [self.TyT[(r0 + i, I)] for i in range(nfc)])

    def proj_gate(self, l, mi, name):
        for hf in range(2):
            self.load_w(l, OFF[name] + hf * 128, 128, 2, hf * 128)
        gt = [self.bf("W", 15360, 256), self.bf("W", 15872, 256)]
        Tg = [Buf("gt0"), Buf("gt1")]
        for t in range(self.NT):
            ps, pt = self.proj_tm(self.wsl(2, 0, 256), 256, t, "B")
            b = t % 2
            self.act(gt[b], ps[:, 0:256], AF.Silu, [pt, self.Trstd], [Tg[b]], scale=self.rstd[:, t:t + 1])
            self.dma("pool", self.gate_d[mi, t * 128:(t + 1) * 128, :], gt[b], [Tg[b]], [self.Tgate[(mi, t)]])

    def proj_fm_to(self, l, name, ncols, dst3, Tdst, scale=1.0):
        for ch in range(ncols // 128):
            slot = ch % 2
            self.load_w(l, OFF[name] + ch * 128, 128, slot, 0, scale)
            for TB in range(self.NB):
                ps, pt = self.proj_fm(self.wsl(slot, 0, 128), 128, TB, "A")
                self.tt("dve", dst3[:, ch, TB * 512:(TB + 1) * 512], ps, self.rbc[:, TB * 512:(TB + 1) * 512],
                        ALU.mult, [pt, self.Trbc[TB]], [Tdst[TB]])

    def proj_v(self, l, name):
        for hf in range(2):
            self.load_w(l, OFF[name] + hf * 128, 128, 2, hf * 128)
        for t in range(self.NT):
            ps, pt = self.proj_tm(self.wsl(2, 0, 256), 256, t, "B")
            self.act(self.VT4[:, t, :, 0:64], ps[:, 0:256].rearrange("p (h d) -> p h d", h=4), AF.Copy,
                     [pt, self.Trstd], [self.TVT[t]], scale=self.rstd[:, t:t + 1])

    def mixer_b(self, l):
        S, NT, NB = self.S, self.NT, self.NB
        T = self.T_const
        lam_init = 0.8 - 0.6 * math.exp(-0.3 * l)
        self.proj_fm_to(l, "b_q", 256, self.QT3, self.TQT, scale=32 ** -0.5)
        self.proj_fm_to(l, "b_k", 256, self.KT3, self.TKT)
        self.proj_v(l, "b_v")
        self.proj_gate(l, 1, "b_gate")
        self.s.barrier()
        sm = self.f32("st", 1344, 32)
        Tsm = Buf("sm")
        lp = self.pkl(l, 17, 128)
        pr = self.f32("st", 1500, 64)
        self.tt("dve", pr[:, 0:32], lp[:, 0:32], lp[:, 32:64], ALU.mult, [T], [Tsm])
        self.tt("dve", pr[:, 32:64], lp[:, 64:96], lp[:, 96:128], ALU.mult, [T], [Tsm])
        self.s.add("dve", lambda e: e.tensor_reduce(sm[:, 0:2], pr.rearrange("p (a b) -> p a b", a=2), AX.X, ALU.add),
                   [Tsm], [Tsm])
        self.act(sm[:, 2:4], sm[:, 0:2], AF.Exp, [Tsm], [Tsm])
        self.tt("dve", sm[:, 4:5], sm[:, 2:3], sm[:, 3:4], ALU.subtract, [Tsm], [Tsm])
        self.ts("dve", sm[:, 5:6], sm[:, 4:5], -1.0, -lam_init, ALU.mult, ALU.add, [Tsm], [Tsm])
        neglam = sm[:, 5:6]
        gsub = self.pkl(l, 145, 64)
        rec = sm[:, 8:16]
        o_h = self.f32("W", 0, 256).rearrange("p (t d) -> p t d", t=4)
        sq = self.f32("W", 1024, 256).rearrange("p (t d) -> p t d", t=4)
        gsg = self.f32("W", 2048, 256).rearrange("p (t d) -> p t d", t=4)
        Toh, Tsq, Tgsg = Buf("oh"), Buf("sq"), Buf("gsg")
        cvar = 1.0 / (1.0 - lam_init) ** 2
        Qz = [self.bf("W", 3072, 512), self.bf("W", 4096, 512), self.bf("W", 5120, 512)]
        TQz = [Buf("qz0"), Buf("qz1"), Buf("qz2")]
        nqz = [0]
        for I in range(NB):
            self.load_sg(1, I)
            items = []
            Ores = {}
            for h in range(4):
                for m in range(2):
                    ch, base = h // 2, 0
                    mi = 2 * h + m
                    m4 = (h % 2) * 2 + m
                    qz, Tqz = Qz[nqz[0] % 3], TQz[nqz[0] % 3]
                    nqz[0] += 1

                    def pre(qz=qz, Tqz=Tqz, ch=ch, m4=m4, I=I):
                        self.ts("dve", qz, self.QT3[:, ch, I * 512:(I + 1) * 512], self.rmask[:, m4:m4 + 1], None,
                                ALU.mult, None, [self.TQT[I], T], [Tqz])
                    kfn = lambda j, ch=ch: (self.KT3[:, ch, j * 128:(j + 1) * 128], [self.TKT[j // 4]])
                    qfn = lambda c0, c1, qz=qz, Tqz=Tqz, I=I: (qz[:, c0 - I * 512:c1 - I * 512], [Tqz])
                    vfn = lambda j, h=h: (self.VT4[:, j, h, :], [self.TVT[j]])

                    def extra(j, tt, mi=mi, I=I):
                        i = 4 * I + tt
                        if j == i:
                            return [(self.bd[:, mi * 128:(mi + 1) * 128], [T])]
                        if j == i - 1:
                            return [(self.bs[:, mi * 128:(mi + 1) * 128], [T])]
                        return []

                    def make(kfn=kfn, qfn=qfn, vfn=vfn, extra=extra, base=base, I=I):
                        return self.attn_map_gen(I, [(kfn, qfn, base)], vfn, extra, None, pairs=True)

                    def post(O, Ot, h=h, m=m, I=I):
                        Ores[(h, m)] = (O, Ot)
                        if m == 0:
                            if h == 0:
                                self.flush_store()
                            return
                        (O0, T0), (O1, T1) = Ores[(h, 0)], Ores[(h, 1)]
                        O0v = O0[:, 0:260].rearrange("p (t d) -> p t d", t=4)
                        O1v = O1[:, 0:260].rearrange("p (t d) -> p t d", t=4)
                        self.s.add("dve", lambda e, O0v=O0v: e.reciprocal(rec[:, 0:4], O0v[:, :, 64]), [T0], [Tsm])
                        self.s.add("dve", lambda e, O1v=O1v: e.reciprocal(rec[:, 4:8], O1v[:, :, 64]), [T1], [Tsm])
                        self.ts("dve", rec[:, 4:8], rec[:, 4:8], neglam, None, ALU.mult, None, [Tsm], [Tsm])
                        for tt in range(4):
                            self.ts("dve", o_h[:, tt, :], O0v[:, tt, 0:64], rec[:, tt:tt + 1], None, ALU.mult, None,
                                    [T0, Tsm], [Toh])
                        for tt in range(4):
                            self.stt(o_h[:, tt, :], O1v[:, tt, 0:64], rec[:, 4 + tt:5 + tt], o_h[:, tt, :], ALU.mult, ALU.add,
                                     [T1, Tsm, Toh], [Toh])
                        self.tt("dve", sq, o_h, o_h, ALU.mult, [Toh], [Tsq])
                        self.s.add("dve", lambda e: e.tensor_reduce(sm[:, 16:20], sq, AX.X, ALU.add), [Tsq], [Tsm])
                        self.ts("dve", sm[:, 20:24], sm[:, 16:20], cvar / 64.0, EPS * cvar, ALU.mult, ALU.add, [Tsm], [Tsm])
                        self.tt("pool", sm[:, 24:28], sm[:, 20:24], self.mhalf[:, 0:4], ALU.pow, [Tsm, T], [Tsm])
                        for tt in range(4):
                            self.tt("dve", gsg[:, tt, :], self.sg[:, tt, h * 64:(h + 1) * 64], gsub, ALU.mult,
                                    [self.Tsg, T], [Tgsg])
                        for tt in range(4):
                            self.stt(self.ytm[:, tt, h * 64:(h + 1) * 64], o_h[:, tt, :], sm[:, 24 + tt:25 + tt], gsg[:, tt, :],
                                     ALU.mult, ALU.mult, [Toh, Tsm, Tgsg], [self.Tytm])
                    items.append((pre, make, post, 2 * I + 4))
            self.run_chain(items, 1)
            self.deferred = (lambda I=I: self.store_y(1, I))
        self.flush_store()

    def mixer_a(self, l):
        S, NT, NB = self.S, self.NT, self.NB
        T = self.T_const
        self.proj_gate(l, 0, "a_gate")
        for hf in range(2):
            self.load_w(l, OFF["a_cq"] + hf * 128, 128, 0, hf * 128)
        self.load_w(l, OFF["a_ckv"], 128, 1, 0)
        self.load_w(l, 320, 96, 1, 128)
        self.load_w(l, 320, 64, 2, 0)
        self.load_w(l, 400, 16, 2, 64)
        self.load_w(l, 384, 16, 2, 80)
        wq = self.bf("X1", 0, 768).rearrange("p (c n) -> p c n", c=2)
        wqs = self.bf("X1", 1536, 768).rearrange("p (c n) -> p c n", c=2)
        wkv = self.bf("X1", 3072, 512)
        Twq = Buf("wq")
        k = self.nw % 2
        self.nw += 1
        stg = self.wst[k][:, 0:768].rearrange("p (c n) -> p c n", c=2)
        self.dma("sp", stg, self.wuq_d[l].rearrange("(c p) n -> p c n", p=128), (), [self.Twst[k]])
        sc = 96 ** -0.5
        for c in range(2):
            g = self.pkl(l, 8 + c, 1)
            self.ts("pool", wq[:, c, :], stg[:, c, :], g, sc, ALU.mult, ALU.mult, [self.Twst[k], T], [Twq])
            s4 = stg[:, c, :].rearrange("p (h e) -> p h e", h=4)
            d4 = wqs[:, c, :].rearrange("p (h e) -> p h e", h=4)
            self.ts("pool", d4[:, :, 0:64], s4[:, :, 0:64], g, sc, ALU.mult, ALU.mult, [self.Twst[k], T], [Twq])
            self.ts("pool", d4[:, :, 64:80], s4[:, :, 80:96], g, sc, ALU.mult, ALU.mult, [self.Twst[k], T], [Twq])
            self.ts("pool", d4[:, :, 80:96], s4[:, :, 64:80], g, sc, ALU.mult, ALU.mult, [self.Twst[k], T], [Twq])
        k = self.nw % 2
        self.nw += 1
        stg2 = self.wst[k][:, 0:512]
        self.dma("sp", stg2, self.wukv_d[l], (), [self.Twst[k]])
        self.ts("pool", wkv, stg2, self.pkl(l, 10, 1), None, ALU.mult, None, [self.Twst[k], T], [Twq])
        wkv4 = wkv.rearrange("p (h e) -> p h e", h=4)
        sm = self.f32("st", 1344, 3 * NT + 8)
        Tsm = Buf("sma")
        junk = self.bf("W", 0, 256)
        Tj = Buf("junka")
        sq_q, sq_k = sm[:, 0:NT], sm[:, NT:2 * NT]
        for t in range(NT):
            ps, pt = self.proj_tm(self.wsl(0, 0, 256), 256, t, "B")
            self.act(junk, ps[:, 0:256], AF.Square, [pt, self.Trstd], [Tj, Tsm], scale=self.rstd[:, t:t + 1],
                     accum_out=sq_q[:, t:t + 1])
            ps, pt = self.proj_tm(self.wsl(1, 0, 128), 128, t, "B")
            self.act(junk[:, 0:128], ps[:, 0:128], AF.Square, [pt, self.Trstd], [Tj, Tsm], scale=self.rstd[:, t:t + 1],
                     accum_out=sq_k[:, t:t + 1])
        rqr = self.f32("st", 1760, 2 * NT)
        Trq = Buf("rqr")
        self.ts("dve", sq_q, sq_q, 1.0 / 256, EPS, ALU.mult, ALU.add, [Tsm], [Tsm])
        self.ts("dve", sq_k, sq_k, 1.0 / 128, EPS, ALU.mult, ALU.add, [Tsm], [Tsm])
        self.tt("pool", sm[:, 0:2 * NT], sm[:, 0:2 * NT], self.mhalfw, ALU.pow, [Tsm, T], [Tsm])
        self.tt("dve", rqr[:, 0:NT], sq_q, self.rstd, ALU.mult, [Tsm, self.Trstd], [Trq])
        self.tt("dve", rqr[:, NT:2 * NT], sq_k, self.rstd, ALU.mult, [Tsm, self.Trstd], [Trq])
        cqn = self.bf("W", 0, 1024).rearrange("p (c s) -> p c s", c=2)
        ckvn = self.bf("W", 2048, 512)
        ccb = self.f32("W", 3072, 512)
        ssb = self.f32("W", 5120, 512)
        t1 = self.f32("W", 7168, 512)
        t2 = self.f32("W", 9216, 512)
        rq_bc = self.f32("W", 11264, 512)
        rk_bc = self.f32("W", 13312, 512)
        Tcqn, Tckvn, Tcc, Tt1, Tt2, Trqb, Trkb = (Buf(n) for n in ("cqn", "ckvn", "cc", "t1a", "t2a", "rqb", "rkb"))
        diag = [self.f32("st", 2048 + i * 512, 128) for i in range(4)]
        Tdiag = [Buf(f"diaga{i}") for i in range(4)]
        Qh = [self.bf("QT", i * 2 * S, S) for i in range(2)]
        Kh = [self.bf("KT", i * 2 * S, S) for i in range(2)]
        rec = self.f32("st", 2016, 4)
        Trec = Buf("reca")
        for pair in range(2):
            self.s.barrier()
            for TB in range(NB):
                sl = slice(TB * 512, (TB + 1) * 512)
                Trb = self.Trbc[TB]
                self.dma("sp", ccb, self.cc_d[:, sl], (), [Tcc])
                self.dma("sp", ssb, self.ss_d[:, sl], (), [Tcc])
                self.bcast(rqr[:, 0:NT], TB, rq_bc, [Trq], [Trqb], diag, Tdiag)
                self.bcast(rqr[:, NT:2 * NT], TB, rk_bc, [Trq], [Trkb], diag, Tdiag)
                ps, pt = self.proj_fm(self.wsl(1, 0, 128), 128, TB, "A")
                self.tt("dve", ckvn, ps, rk_bc, ALU.mult, [pt, Trkb], [Tckvn])
                for hh in range(2):
                    h = pair * 2 + hh
                    ps, _, pt = self.bankA()
                    self.mm(ps[0:64, :], wkv4[:, h, 0:64], ckvn, True, True, [Twq, Tckvn], [pt])
                    self.cp("act", Kh[hh][0:64, sl], ps[0:64, :], [pt], [self.TKT[TB]])
                if pair == 0:
                    for tt_ in range(4):
                        t = TB * 4 + tt_
                        ps, _, pt = self.bankB()
                        self.mm(ps[:, 0:256].rearrange("p (h d) -> p h d", h=4), ckvn[:, tt_ * 128:(tt_ + 1) * 128],
                                wkv4[:, :, 64:128], True, True, [Twq, Tckvn], [pt])
                        self.cp("act", self.VT4[:, t, :, 0:64], ps[:, 0:256].rearrange("p (h d) -> p h d", h=4),
                                [pt], [self.TVT[t]])
                ps1, pt1 = self.proj_fm(self.wsl(1, 128, 96), 96, TB, "A")
                ps2, pt2 = self.proj_fm(self.wsl(2, 0, 96), 96, TB, "A")
                self.tt("dve", t1[64:96, :], ps1[64:96, :], ccb[64:96, :], ALU.mult, [pt1, Tcc], [Tt1])
                self.tt("dve", t2[64:96, :], ps2[64:96, :], ssb[64:96, :], ALU.mult, [pt2, Tcc], [Tt2])
                self.tt("dve", t1[64:96, :], t1[64:96, :], t2[64:96, :], ALU.add, [Tt1, Tt2], [Tt1])
                self.tt("dve", Kh[0][64:96, sl], t1[64:96, :], self.rbc[64:96, sl], ALU.mult, [Tt1, Trb], [self.TKT[TB]])
                self.tt("dve", Kh[1][64:96, sl], t1[64:96, :], self.rbc[64:96, sl], ALU.mult, [Tt1, Trb], [self.TKT[TB]])
                for c in range(2):
                    ps, pt = self.proj_fm(self.wsl(0, c * 128, 128), 128, TB, "A")
                    self.tt("dve", cqn[:, c, :], ps, rq_bc, ALU.mult, [pt, Trqb], [Tcqn])
                for hh in range(2):
                    h = pair * 2 + hh
                    psa, _, pta = self.bankA()
                    psb_, _, ptb = self.bankA()
                    for c in range(2):
                        self.mm(psa[0:96, :], wq[:, c, h * 96:(h + 1) * 96], cqn[:, c, :], c == 0, c == 1, [Twq, Tcqn], [pta])
                    for c in range(2):
                        self.mm(psb_[0:96, :], wqs[:, c, h * 96:(h + 1) * 96], cqn[:, c, :], c == 0, c == 1, [Twq, Tcqn], [ptb])
                    self.cp("act", Qh[hh][0:64, sl], psa[0:64, :], [pta], [self.TQT[TB]])
                    self.tt("dve", t1[64:96, :], psa[64:96, :], ccb[64:96, :], ALU.mult, [pta, Tcc], [Tt1])
                    self.tt("dve", t2[64:96, :], psb_[64:96, :], ssb[64:96, :], ALU.mult, [ptb, Tcc], [Tt2])
                    self.tt("dve", Qh[hh][64:96, sl], t1[64:96, :], t2[64:96, :], ALU.add, [Tt1, Tt2], [self.TQT[TB]])
            self.s.barrier()
            for I in range(NB):
                self.load_sg(0, I)
                items = []
                for hh in range(2):
                    h = pair * 2 + hh
                    kfn = lambda j, hh=hh: (Kh[hh][0:96, j * 128:(j + 1) * 128], [self.TKT[j // 4]])
                    qfn = lambda c0, c1, hh=hh, I=I: (Qh[hh][0:96, c0:c1], [self.TQT[I]])
                    vfn = lambda j, h=h: (self.VT4[:, j, h, :], [self.TVT[j]])

                    def extra(j, tt, I=I):
                        return [(self.cmb, [T])] if j == 4 * I + tt else []

                    def make(kfn=kfn, qfn=qfn, vfn=vfn, extra=extra, I=I):
                        return self.attn_map_gen(I, [(kfn, qfn, 0)], vfn, extra, None, pairs=True)

                    def post(O, Ot, hh=hh, h=h):
                        if hh == 0:
                            self.flush_store()
                        Ov = O[:, 0:260].rearrange("p (t d) -> p t d", t=4)
                        self.s.add("dve", lambda e, Ov=Ov: e.reciprocal(rec, Ov[:, :, 64]), [Ot], [Trec])
                        for tt in range(4):
                            self.stt(self.ytm[:, tt, hh * 64:(hh + 1) * 64], Ov[:, tt, 0:64], rec[:, tt:tt + 1],
                                     self.sg[:, tt, h * 64:(h + 1) * 64], ALU.mult, ALU.mult, [Ot, Trec, self.Tsg], [self.Tytm])
                    items.append((None, make, post, 2 * I + 4))
                self.run_chain(items, 1)
                self.deferred = (lambda I=I, pair=pair: self.store_y(0, I, r0=pair, nfc=1))
            self.flush_store()

    def mixer_c_proj(self, l):
        S, NT, NB = self.S, self.NT, self.NB
        T = self.T_const
        self.proj_gate(l, 2, "c_gate")
        self.proj_fm_to(l, "c_q", 256, self.QT3, self.TQT, scale=64 ** -0.5)
        self.proj_fm_to(l, "c_k", 256, self.KT3, self.TKT)
        self.proj_v(l, "c_v")
        self.QI3 = self.bf("QI", 0, 2 * S).rearrange("p (c s) -> p c s", c=2)
        self.TQI = [Buf(f"QI{i}") for i in range(NB)]
        self.proj_fm_to(l, "c_qidx", 256, self.QI3, self.TQI, scale=1.0 / 16.0)
        self.KX = self.bf("X1", 0, S)
        self.TKX = [Buf(f"KX{i}") for i in range(NB)]
        base = self.win_d[l, :, OFF["c_kidx"]:OFF["c_kidx"] + 32]
        k = self.nw % 2
        stg4 = self.wst[k].rearrange("p (c r n) -> p c r n", c=8, r=4)
        srcs = [(stg4[:, :, r, :], base.rearrange("(c p) n -> p c n", p=128)) for r in range(4)]
        self.load_w(l, 0, 128, 0, 0, src=srcs)
        for TB in range(NB):
            ps, pt = self.proj_fm(self.wsl(0, 0, 128), 128, TB, "A")
            self.tt("dve", self.KX[:, TB * 512:(TB + 1) * 512], ps, self.rbc[:, TB * 512:(TB + 1) * 512], ALU.mult,
                    [pt, self.Trbc[TB]], [self.TKX[TB]])
        self.load_w(l, OFF["c_widx"], 8, 1, 0)
        self.wabs = self.f32("st", 2048, NT * 8)
        self.wsgn = self.f32("st", 3072, NT * 8)
        self.Tww = Buf("ww")
        for t in range(NT):
            ps, pt = self.proj_tm(self.wsl(1, 0, 8), 8, t, "B")
            self.ts("dve", self.wabs[:, t * 8:(t + 1) * 8], ps[:, 0:8], self.rstd[:, t:t + 1], None, ALU.mult, None,
                    [pt, self.Trstd], [self.Tww])
        self.cp("dve", self.wsgn, self.wabs, [self.Tww], [self.Tww])

    def mixer_c_attn(self, l):
        S, NT, NB = self.S, self.NT, self.NB
        T = self.T_const
        nbis = self.nbis
        idx = self.f32("xT", 0, 4 * S).rearrange("p (g s) -> p g s", g=4)
        ob = self.o["rbc"]
        negms = [self.A8[:, ob + k * 4 * S: ob + (k + 1) * 4 * S].rearrange("p (g s) -> p g s", g=4) for k in range(2)]
        Tnegs = [[Buf(f"neg{k}_{g}") for g in range(4)] for k in range(2)]
        Tidx = [Buf(f"idx{g}") for g in range(4)]
        Rp = [self.bf("W", i * 2048, 1024).rearrange("p (b n) -> p b n", b=2) for i in range(4)]
        TRp = [Buf(f"Rp{i}") for i in range(4)]
        Dg = [self.bf("wbf", 8192 + i * 256, 128) for i in range(8)]
        self.yTs = self.bf("wbf", 10240, 1024).rearrange("p (c s) -> p c s", c=2)
        TDg = [Buf(f"Dg{i}") for i in range(8)]
        sm = self.f32("st", 1344, 64)
        Tsm = Buf("smc")
        rmax, rmin, lo, step, mid, cnt, tmp, thr = (sm[:, 4 * i:4 * i + 4] for i in range(8))
        rec = sm[:, 32:36]
        nmid = sm[:, 36:40]
        c256 = sm[:, 40:44]
        Tth = [Buf(f"th{g}") for g in range(4)]
        nthr = sm[:, 52:56]
        den = sm[:, 44:48]
        mone = sm[:, 48:52]
        self.memset("dve", mone, -1.0, [T])
        pw2 = self.f32("st", 1900, nbis)
        for it in range(nbis):
            self.memset("dve", pw2[:, it:it + 1], 2.0 ** -(it + 1), [T])
        Trec = Buf("recc")
        ncp = 0
        s2all = self.f32("st", 1600, 4 * nbis)
        Ts2 = Buf("s2all")

        def index_and_threshold(I, hb):
            negm = negms[I % 2]
            Tneg = Tnegs[I % 2]
            live = [None]
            est = 0

            def tick(k):
                if live[0] is None:
                    return
                for _ in range(k):
                    try:
                        next(live[0])
                    except StopIteration:
                        live[0] = None
                        return

            if True:
                tiles = [2 * hb, 2 * hb + 1]
                groups = []
                for tt in tiles:
                    i = 4 * I + tt
                    L = 128 * (i + 1)
                    nkb = (L + 511) // 512
                    for kb in range(nkb):
                        for half in range(2):
                            groups.append((tt, i, L, kb, half, kb == nkb - 1))
                state = {}

                def stageA(g, par):
                    tt, i, L, kb, half, lastkb = g
                    ncol = min(512, L - kb * 512)
                    for pr_ in range(2):
                        psp, ptoks = self.bankpair("ia")
                        rp = par * 2 + pr_
                        for b2 in range(2):
                            hq = pr_ * 2 + b2
                            b_ = 32 * hq
                            kw = {"tile_position": (96, 0)} if b_ == 96 else {}
                            self.mm(psp[:, b2 * 512:b2 * 512 + ncol], self.QI3[b_:b_ + 32, half, i * 128:(i + 1) * 128],
                                    self.KX[b_:b_ + 32, kb * 512:kb * 512 + ncol], True, True,
                                    [self.TQI[I]] + [self.TKX[kb]], [ptoks[b2]], **kw)
                        self.act(Rp[rp][:, :, 0:ncol], psp.rearrange("p (b n) -> p b n", b=2)[:, :, 0:ncol], AF.Relu,
                                 list(ptoks), [TRp[rp]])

                def stageB(g, par):
                    tt, i, L, kb, half, lastkb = g
                    ncol = min(512, L - kb * 512)
                    if kb == 0 and half == 0:
                        for hi in range(8):
                            self.ts("pool", Dg[hi], self.identb, self.wsgn[:, i * 8 + hi:i * 8 + hi + 1], 1.0, ALU.mult, ALU.mult,
                                    [T, self.Tww], [TDg[hi]])
                    if half == 0:
                        state["psB"] = self.bank("ib")
                    psB, _, ptB = state["psB"]
                    for hq in range(4):
                        hi = half * 4 + hq
                        rp = par * 2 + hq // 2
                        self.mm(psB[:, 0:ncol], Dg[hi], Rp[rp][:, hq % 2, 0:ncol], hi == 0, hi == 7, [TDg[hi], TRp[rp]], [ptB])
                    if half == 1:
                        self.cp("act", idx[:, tt, kb * 512:kb * 512 + ncol], psB[:, 0:ncol], [ptB], [Tidx[tt]])
                        if lastkb:
                            self.memset("pool", idx[0:64, tt, i * 128 + 64:i * 128 + 128], -1e30, [Tidx[tt]])

                kstep = (est + 2 * len(groups) - 1) // (2 * len(groups)) if est else 0
                for gi in range(len(groups) + 1):
                    if gi < len(groups):
                        stageA(groups[gi], gi % 2)
                    if gi >= 1:
                        stageB(groups[gi - 1], (gi - 1) % 2)
                    tick(kstep)
                tts = [tt for tt in tiles if 4 * I + tt >= 2]
                for tt in tiles:
                    if tt not in tts:
                        self.memset("dve", thr[:, tt:tt + 1], -1e29, [Tth[tt]])
                if tts:
                    Ls = {tt: 128 * (4 * I + tt + 1) for tt in tts}
                    for tt in tts:
                        L = Ls[tt]
                        self.s.add("dve", lambda e, tt=tt, L=L: e.tensor_reduce(rmax[:, tt:tt + 1], idx[:, tt, 0:L], AX.X, ALU.max),
                                   [Tidx[tt]], [Tth[tt]])
                    for tt in tts:
                        L = Ls[tt]
                        self.s.add("dve", lambda e, tt=tt, L=L: e.tensor_reduce(mid[:, tt:tt + 1], idx[:, tt, 0:320], AX.X, ALU.min),
                                   [Tidx[tt]], [Tth[tt]])
                    for tt in tts:
                        self.tt("dve", rmax[:, tt:tt + 1], rmax[:, tt:tt + 1], mid[:, tt:tt + 1], ALU.subtract, [Tth[tt]], [Tth[tt]])
                    for tt in tts:
                        self.ts("dve", s2all[:, tt * nbis:(tt + 1) * nbis], pw2, rmax[:, tt:tt + 1], None, ALU.mult, None,
                                [Tth[tt], T], [Ts2])
                    for tt in tts:
                        self.stt(mid[:, tt:tt + 1], rmax[:, tt:tt + 1], 0.5, mid[:, tt:tt + 1], ALU.mult, ALU.add,
                                 [Tth[tt]], [Tth[tt]])
                    for it in range(nbis):
                        for tt in tts:
                            L = Ls[tt]
                            self.ts("dve", negm[:, tt, 0:L], idx[:, tt, 0:L], mid[:, tt:tt + 1], None, ALU.is_ge, ALU.add,
                                    [Tidx[tt], Tth[tt]], [Tneg[tt], Tth[tt]], accum_out=cnt[:, tt:tt + 1], saturate=False)
                        for tt in tts:
                            self.ts("dve", tmp[:, tt:tt + 1], cnt[:, tt:tt + 1], 256.0, 0.5, ALU.is_ge, ALU.subtract,
                                    [Tth[tt]], [Tth[tt]])
                        for tt in tts:
                            self.stt(mid[:, tt:tt + 1], tmp[:, tt:tt + 1], s2all[:, tt * nbis + it:tt * nbis + it + 1],
                                     mid[:, tt:tt + 1], ALU.mult, ALU.add, [Tth[tt], Ts2], [Tth[tt]])
                    for tt in tts:
                        self.tt("dve", thr[:, tt:tt + 1], mid[:, tt:tt + 1], s2all[:, (tt + 1) * nbis - 1:(tt + 1) * nbis],
                                ALU.subtract, [Tth[tt], Ts2], [Tth[tt]])
                for tt in tiles:
                    self.ts("dve", nthr[:, tt:tt + 1], thr[:, tt:tt + 1], -1.0, None, ALU.mult, None, [Tth[tt]], [Tth[tt]])
                for tt in tiles:
                    L = 128 * (4 * I + tt + 1)
                    self.act(negm[:, tt, 0:L], idx[:, tt, 0:L], AF.Sign, [Tidx[tt], Tth[tt]], [Tneg[tt]],
                             bias=nthr[:, tt:tt + 1], saturate=False)

        def attention(I, hb):
            negm = negms[I % 2]
            Tneg = Tnegs[I % 2]
            if hb == 0:
                self.load_sg(2, I)
            items = []
            for h in (2 * hb, 2 * hb + 1):
                ch, base = h // 2, 0
                mi = 8 + h

                def pre(h=h):
                    self.ts("pool", qzc, self.QT3[:, h // 2, I * 512:(I + 1) * 512], self.rmask[:, 4 + h % 2:5 + h % 2], 1.0,
                            ALU.mult, ALU.mult, [self.TQT[I], T], [Tqzc])
                kfn = lambda j, ch=ch: (self.KT3[:, ch, j * 128:(j + 1) * 128], [self.TKT[j // 4]])
                qfn = lambda c0, c1, I=I: (qzc[:, c0 - I * 512:c1 - I * 512], [Tqzc])
                vfn = lambda j, h=h: (self.VT4[:, j, h, :], [self.TVT[j]])

                def extra(j, tt, mi=mi, I=I):
                    i = 4 * I + tt
                    r = [(negm[:, tt, j * 128:(j + 1) * 128], [Tneg[tt]], self.ident16k)]
                    if j == i:
                        r.append((self.bd[:, mi * 128:(mi + 1) * 128], [T]))
                    if j == i - 1:
                        r.append((self.bs[:, mi * 128:(mi + 1) * 128], [T]))
                    return r

                def make(kfn=kfn, qfn=qfn, vfn=vfn, extra=extra, base=base):
                    return self.attn_map_gen(I, [(kfn, qfn, base)], vfn, extra, -16384.0, spool="as", opool="ao", pairs=True)

                def post(O, Ot, h=h):
                    if h == 0:
                        self.flush_store()
                    Ov = O[:, 0:260].rearrange("p (t d) -> p t d", t=4)
                    self.act(den, Ov[:, :, 64], AF.Copy, [Ot], [Trec])
                    self.tt("pool", rec, den, mone, ALU.pow, [Trec, T], [Trec])
                    for tt in range(4):
                        self.act(self.ytm[:, tt, h * 64:(h + 1) * 64], Ov[:, tt, 0:64], AF.Copy, [Ot, Trec], [self.Tytm],
                                 scale=rec[:, tt:tt + 1])
                    self.tt("pool", self.ytm[:, :, h * 64:(h + 1) * 64], self.ytm[:, :, h * 64:(h + 1) * 64],
                            self.sg[:, :, h * 64:(h + 1) * 64], ALU.mult, [self.Tsg, self.Tytm], [self.Tytm])
                items.append((pre, make, post, 2 * I + 4))
            self.run_chain(items, 1)
            if hb == 1:
                self.deferred = (lambda I=I: self.store_y(2, I))

        self.trpool = "as"
        qzc = self.bf("X2", 0, 512)
        Tqzc = Buf("qzc")
        for I in range(NB + 1):
            for hb in range(2):
                if I < NB:
                    index_and_threshold(I, hb)
                if I >= 1:
                    attention(I - 1, hb)
        self.flush_store()
        self.trpool = "A"

    def zero_y(self, rows):
        z = self.bf("W", 0, 512)
        Tz = Buf("z")
        self.memset("dve", z, 0.0, [Tz])
        for r in rows:
            for TB in range(self.NB):
                self.dma("pool", self.yT_d[r * 128:(r + 1) * 128, TB * 512:(TB + 1) * 512], z, [Tz],
                         [self.TyT[(r, TB)]])

    def record(self):
        S, NT, NB = self.S, self.NT, self.NB
        self.Tout = [Buf(f"out{t}") for t in range(NT)]
        Tx_in = [Buf(f"xin{t}") for t in range(NT)]
        Tx_res = [Buf(f"xres{t}") for t in range(NT)]
        self.setup()
        self.init_w()
        for l in range(self.depth):
            last = l == self.depth - 1
            xsrc = self.x_d if l == 0 else self.xres_d
            Tx = Tx_in if l == 0 else Tx_res
            self.TyT = {(r, TB): Buf(f"yT{r}_{TB}") for r in range(8) for TB in range(NB)}
            self.phase1(l, xsrc, Tx)
            self.s.barrier()
            for mi, m in enumerate("abcd"):
                if m not in self.mixers:
                    self.zero_y([2 * mi, 2 * mi + 1])
            self.init_attn()
            self.Tgate = {(mi, t): Buf(f"g{mi}_{t}") for mi in range(3) for t in range(NT)}
            if "d" in self.mixers:
                self.mixer_d(l)
                self.s.barrier()
            self.set_ones()
            if "b" in self.mixers:
                self.mixer_b(l)
                self.s.barrier()
            if "a" in self.mixers:
                self.mixer_a(l)
                self.s.barrier()
            if "c" in self.mixers:
                self.mixer_c_proj(l)
                self.s.barrier()
                self.mixer_c_attn(l)
            self.s.barrier()
            self.outproj(l, xsrc, Tx, self.Tout if last else Tx_res, last)
            self.s.barrier()


def build(S, depth=DEPTH, mixers="abcd", dbg=False, nbis=12):
    nc = bass.Bass("TRN2", target_bir_lowering=False)
    with ExitStack() as st:
        p = Prog(nc, st, S, depth, mixers, dbg, nbis)
        p.record()
        p.s.emit(st)
    return nc


def host_inputs(S, depth, norm_g, w_in, mla_qa_g, mla_w_uq, mla_kva_g, mla_w_ukv, diff_lambda,
                diff_subln_g, conv_w, w_out, rel_bias, final_g):
    f = np.float32
    c = _host_consts(S)
    pk = np.zeros((128, depth * PL + D), f)
    for l in range(depth):
        b = l * PL
        pk[:, b:b + 8] = np.asarray(norm_g[l], f).reshape(8, 128).T
        pk[:, b + 8:b + 10] = np.asarray(mla_qa_g[l], f).reshape(2, 128).T
        pk[:, b + 10] = np.asarray(mla_kva_g[l], f)
        pk[:, b + 11:b + 17] = np.asarray(conv_w[l], f).reshape(3, 2, 128).transpose(2, 1, 0).reshape(128, 6)
        pk[:, b + 17:b + 145] = np.asarray(diff_lambda[l], f).reshape(1, 128)
        pk[:, b + 145:b + 209] = np.asarray(diff_subln_g[l], f).reshape(1, 64)
    pk[:, depth * PL:] = np.asarray(final_g, f).reshape(1, D)
    cst = np.concatenate([c["ident_f"], c["ones_f"], c["cm"]], axis=1).astype(f)
    rmask = np.zeros((128, 8), f)
    for p_ in range(128):
        rmask[p_, p_ // 32] = 1.0
        rmask[p_, 4 + p_ // 64] = 1.0
    rb = np.asarray(rel_bias, f)
    bias = np.zeros((128, 2 * 12 * 128 + 12), f)
    for m in range(12):
        bias[:, m * 128:(m + 1) * 128] = rb[c["bk_diag"], m]
        bias[:, 1536 + m * 128:1536 + (m + 1) * 128] = rb[c["bk_sub"], m]
    bias[:, 3072:3084] = rb[15:16, :]
    shared = {
        "w_in": np.ascontiguousarray(np.asarray(w_in, f)[:depth]),
        "w_uq": np.ascontiguousarray(np.asarray(mla_w_uq, f)[:depth]),
        "w_ukv": np.ascontiguousarray(np.asarray(mla_w_ukv, f)[:depth]),
        "w_out": np.ascontiguousarray(np.asarray(w_out, f)[:depth]),
        "pk": pk, "cst": cst, "biasblk": bias, "cc4": c["cc4"], "ss4": c["ss4"], "rmask": rmask,
    }
    return shared


_NC_CACHE = {}


def kernel(x, norm_g, w_in, mla_qa_g, mla_w_uq, mla_kva_g, mla_w_ukv, diff_lambda,
           diff_subln_g, conv_w, w_out, rel_bias, final_g):
    x = np.asarray(x, np.float32)
    B, S, _ = x.shape
    shared = host_inputs(S, DEPTH, norm_g, w_in, mla_qa_g, mla_w_uq, mla_kva_g, mla_w_ukv, diff_lambda,
                         diff_subln_g, conv_w, w_out, rel_bias, final_g)
    key = (S, DEPTH)
    if key not in _NC_CACHE:
        _NC_CACHE[key] = build(S, DEPTH)
    nc = _NC_CACHE[key]
    in_maps = []
    for b in range(B):
        m = dict(shared)
        m["x"] = np.ascontiguousarray(x[b])
        in_maps.append(m)
    res = run_bass_kernel_spmd(nc, in_maps, core_ids=list(range(B)))
    return np.stack([np.asarray(r["out"], np.float32) for r in res.results], axis=0)
```

```python
import math
from contextlib import ExitStack

import numpy as np
import ml_dtypes

import concourse.bass as bass
import concourse.mybir as mybir
from concourse.bass_utils import run_bass_kernel_spmd

F32 = mybir.dt.float32
BF16 = mybir.dt.bfloat16
ALU = mybir.AluOpType
AF = mybir.ActivationFunctionType
AX = mybir.AxisListType

D = 1024
IN_COLS = 4040
DEPTH = 2
EPS = 1e-6
NEGM = -30000.0

OFF = {}
_o = 0
for _n, _w in (("a_cq", 256), ("a_ckv", 128), ("a_krope", 32), ("a_gate", 256),
               ("b_q", 256), ("b_k", 256), ("b_v", 256), ("b_gate", 256),
               ("c_q", 256), ("c_k", 256), ("c_v", 256),
               ("c_qidx", 256), ("c_kidx", 32), ("c_widx", 8), ("c_gate", 256),
               ("d_b", 256), ("d_c", 256), ("d_h", 256), ("d_gate", 256)):
    OFF[_n] = _o
    _o += _w
assert _o == IN_COLS


class Buf:
    __slots__ = ("name", "w", "r")

    def __init__(self, name=""):
        self.name = name
        self.w = None
        self.r = {}


class Op:
    __slots__ = ("eng", "fn", "deps", "inc", "pos", "val", "dma", "dsem", "dval", "dprev")


EPOCH = 30000
NDSEM = 8
ENGS = ("pe", "act", "dve", "pool", "sp")


class Sched:
    def __init__(self, nc):
        self.nc = nc
        self.ops = {e: [] for e in ENGS}

    def add(self, eng, fn, reads=(), writes=(), dma=False):
        op = Op()
        op.eng = eng
        op.fn = fn
        op.dma = dma
        op.inc = False
        op.val = 0
        lst = self.ops[eng]
        op.pos = len(lst)
        deps = {}
        for b in reads:
            if b.w is not None:
                deps[id(b.w)] = b.w
        for b in writes:
            if b.w is not None:
                deps[id(b.w)] = b.w
            for r in b.r.values():
                deps[id(r)] = r
        keep = []
        for d in deps.values():
            if not d.dma and d.eng == eng:
                if eng == "pe":
                    continue
                if op.pos - d.pos > 1:
                    continue
            keep.append(d)
            if not d.dma:
                d.inc = True
        op.deps = keep
        for b in reads:
            b.r[("d", id(op)) if dma else eng] = op
        for b in writes:
            b.w = op
            b.r = {}
        lst.append(op)
        return op

    def barrier(self):
        bar = {"pos": {e: len(self.ops[e]) for e in ENGS}, "snap": {}}
        for e in ENGS:
            for op in reversed(self.ops[e]):
                if not op.dma and op.fn is not None:
                    op.inc = True
                    break
        for e in ENGS:
            op = Op()
            op.eng = e
            op.fn = None
            op.dma = False
            op.inc = False
            op.val = 0
            op.pos = len(self.ops[e])
            op.deps = bar
            self.ops[e].append(op)

    def emit(self, stack):
        nc = self.nc
        engsems = {}
        for eng in ENGS:
            cnt = 0
            for op in self.ops[eng]:
                if op.dma:
                    continue
                if op.fn is None:
                    op.deps["snap"][eng] = cnt
                    continue
                if op.inc:
                    cnt += 1
                    op.val = cnt
            nep = (cnt + EPOCH - 1) // EPOCH + 1
            engsems[eng] = [stack.enter_context(nc.semaphore(f"s_{eng}_{i}")) for i in range(nep)]
        for eng in ENGS:
            dsems = None
            vals = [0] * NDSEM
            k = 0
            for op in self.ops[eng]:
                if op.fn is None:
                    op.deps["snap"]["d_" + eng] = (dsems, list(vals))
                    continue
                if not op.dma:
                    continue
                if dsems is None:
                    dsems = [stack.enter_context(nc.semaphore(f"d_{eng}_{i}")) for i in range(NDSEM)]
                s = k % NDSEM
                op.dsem = dsems[s]
                op.dprev = vals[s]
                vals[s] += 16
                op.dval = vals[s]
                k += 1

        def signal(d):
            if d.dma:
                return d.dsem, d.dval
            ep = (d.val - 1) // EPOCH
            return engsems[d.eng][ep], d.val - ep * EPOCH

        def run(e, eng):
            waited = {}
            for op in self.ops[eng]:
                if op.fn is None:
                    snap = op.deps["snap"]
                    for e2 in ENGS:
                        c = snap[e2]
                        if c > 0 and e2 != eng:
                            ep = (c - 1) // EPOCH
                            sem, val = engsems[e2][ep], c - ep * EPOCH
                            if waited.get(id(sem), 0) < val:
                                e.wait_ge(sem, val)
                                waited[id(sem)] = val
                        ds, vs = snap["d_" + e2]
                        if ds is not None:
                            for sem, val in zip(ds, vs):
                                if val > 0 and waited.get(id(sem), 0) < val:
                                    e.wait_ge(sem, val)
                                    waited[id(sem)] = val
                    continue
                waits = {}
                for d in op.deps:
                    sem, val = signal(d)
                    k = id(sem)
                    if k not in waits or waits[k][1] < val:
                        waits[k] = (sem, val)
                if op.dma and op.dprev > 0:
                    k = id(op.dsem)
                    if k not in waits or waits[k][1] < op.dprev:
                        waits[k] = (op.dsem, op.dprev)
                for k, (sem, val) in waits.items():
                    if waited.get(k, 0) < val:
                        e.wait_ge(sem, val)
                        waited[k] = val
                ins = op.fn(e)
                if op.dma:
                    ins.then_inc(op.dsem, 16)
                elif op.inc:
                    sem, _ = signal(op)
                    ins.then_inc(sem, 1)

        with nc.Block() as block:
            block.tensor(lambda e: run(e, "pe"))
            block.scalar(lambda e: run(e, "act"))
            block.vector(lambda e: run(e, "dve"))
            block.gpsimd(lambda e: run(e, "pool"))
            block.sync(lambda e: run(e, "sp"))


def _rel_bucket_np(rel):
    nb = 16
    max_exact = 8
    ret = np.where(rel > 0, nb, 0)
    n = np.abs(rel)
    nf = np.maximum(n, max_exact).astype(np.float32)
    large = max_exact + (np.log(nf / np.float32(max_exact)) / np.float32(math.log(128 / max_exact))
                         * np.float32(nb - max_exact)).astype(np.int32)
    large = np.minimum(large, nb - 1)
    return ret + np.where(n < max_exact, n, large)


def _host_consts(S):
    c = {}
    c["ident_f"] = np.eye(128, dtype=np.float32)
    c["ones_f"] = np.ones((128, 128), dtype=np.float32)
    q = np.arange(128)[:, None]
    k = np.arange(128)[None, :]
    cm = np.where((k // 64) <= (q // 64), 0.0, NEGM).astype(np.float32)
    c["cm"] = cm
    half = 16
    inv_freq = (np.float32(10000.0) ** (-np.arange(half, dtype=np.float32) / np.float32(half))).astype(np.float32)
    ang = np.arange(S, dtype=np.float32)[:, None] * inv_freq[None, :]
    cos = np.cos(ang).astype(np.float32).T
    sin = np.sin(ang).astype(np.float32).T
    cc = np.concatenate([cos, cos], 0)
    ss = np.concatenate([-sin, sin], 0)
    c["cc4"] = np.ascontiguousarray(np.tile(cc, (4, 1)))
    c["ss4"] = np.ascontiguousarray(np.tile(ss, (4, 1)))
    c["bk_diag"] = _rel_bucket_np(k - q)
    c["bk_sub"] = _rel_bucket_np(k - 128 - q)
    return c


PL = 8 + 2 + 1 + 6 + 128 + 64


class Prog:
    def __init__(self, nc, st, S, depth=DEPTH, mixers="abcd", dbg=False, nbis=12):
        self.nc = nc
        self.st = st
        self.S = S
        self.NT = S // 128
        self.NB = S // 512
        self.depth = depth
        self.mixers = mixers
        self.nbis = nbis
        self.s = Sched(nc)
        NT = self.NT
        dt = nc.dram_tensor
        self.x_d = dt("x", [S, D], F32, kind="ExternalInput").ap()
        self.win_d = dt("w_in", [depth, D, IN_COLS], F32, kind="ExternalInput").ap()
        self.wuq_d = dt("w_uq", [depth, 256, 384], F32, kind="ExternalInput").ap()
        self.wukv_d = dt("w_ukv", [depth, 128, 512], F32, kind="ExternalInput").ap()
        self.wout_d = dt("w_out", [depth, D, D], F32, kind="ExternalInput").ap()
        self.npk = depth * PL + D
        self.pk_d = dt("pk", [128, self.npk], F32, kind="ExternalInput").ap()
        self.cst_d = dt("cst", [128, 384], F32, kind="ExternalInput").ap()
        self.bias_d = dt("biasblk", [128, 2 * 12 * 128 + 12], F32, kind="ExternalInput").ap()
        self.rmask_d = dt("rmask", [128, 8], F32, kind="ExternalInput").ap()
        self.cc_d = dt("cc4", [128, S], F32, kind="ExternalInput").ap()
        self.ss_d = dt("ss4", [128, S], F32, kind="ExternalInput").ap()
        self.out_d = dt("out", [S, D], F32, kind="ExternalOutput").ap()
        self.xres_d = dt("xres", [S, D], F32).ap()
        self.yT_d = dt("yT", [D, S], BF16, kind="ExternalOutput" if dbg else "Internal").ap()
        self.gate_d = dt("gates", [3, S, 256], BF16).ap()
        o = {}
        off = 0

        def reg(name, nbytes):
            nonlocal off
            o[name] = off
            off += (nbytes + 31) // 32 * 32

        SR = max(S, 4096)
        NTR = SR // 128
        reg("P", 14336)
        reg("xT", 16 * SR)
        reg("rbc", 4 * SR)
        reg("wst", 2 * 4096)
        reg("wbf", 3 * 4096)
        reg("st", 4096)
        reg("QT", 4 * SR)
        reg("KT", 4 * SR)
        reg("VT", NTR * 4 * 65 * 2)
        reg("QI", 4 * SR + 32)
        reg("X1", 2 * SR)
        reg("W", 16384)
        reg("X2", 1024)
        self.o = o
        self.total = off
        self.A = st.enter_context(nc.sbuf_tensor("arena", [128, off // 4], F32))
        self.Ab = self.A.bitcast(BF16)
        self.A8 = self.A.bitcast(mybir.dt.float8e5)
        psh = [st.enter_context(nc.psum_tensor(f"ps{i}", [128, 1024], F32)) for i in range(4)]
        self.psp = [p[:, :] for p in psh]
        self.ps = [psh[i // 2][:, (i % 2) * 512:(i % 2 + 1) * 512] for i in range(8)]
        self.psb = [psh[i // 2].bitcast(BF16)[:, (i % 2) * 1024:(i % 2 + 1) * 1024] for i in range(8)]
        self.pstok = [Buf(f"ps{i}") for i in range(8)]
        self.rr = {}
        self.pools = {"A": [0, 1, 2, 3], "B": [4, 5, 6, 7], "as": [0, 1, 2, 3], "ao": [4, 5, 6, 7], "ia": [0, 1, 2, 3],
                      "ib": [4, 5, 6, 7]}
        self.trpool = "A"
        self.nPT = 4

    def f32(self, region, boff, n):
        b = (self.o[region] + boff) // 4
        return self.A[:, b:b + n]

    def bf(self, region, boff, n):
        b = (self.o[region] + boff) // 2
        return self.Ab[:, b:b + n]

    def bank(self, name):
        lst = self.pools[name]
        name = "A" if lst[0] == 0 else "B"
        k = self.rr.get(name, 0)
        self.rr[name] = k + 1
        i = lst[k % len(lst)]
        return self.ps[i], self.psb[i], self.pstok[i]

    def bankpair(self, name):
        lst = self.pools[name]
        name = "A" if lst[0] == 0 else "B"
        k = self.rr.get(name, 0)
        if k % 2:
            k += 1
        self.rr[name] = k + 2
        i = lst[k % len(lst)]
        return self.psp[i // 2], [self.pstok[i], self.pstok[i + 1]]

    def bankA(self):
        return self.bank("A")

    def bankB(self):
        return self.bank("B")

    def mm(self, out, lhsT, rhs, start, stop, R, Wt, **kw):
        self.s.add("pe", lambda e: e.matmul(out, lhsT, rhs, start=start, stop=stop, **kw), R, Wt)

    def tr(self, out, in_, ident, R, Wt):
        self.s.add("pe", lambda e: e.transpose(out, in_, ident), R, Wt)

    def act(self, out, in_, func, R, Wt, **kw):
        self.s.add("act", lambda e: e.activation(out, in_, func, **kw), R, Wt)

    def ts(self, eng, out, in0, s1, s2, op0, op1, R, Wt, **kw):
        if op1 is None:
            self.s.add(eng, lambda e: e.tensor_scalar(out, in0, s1, None, op0, **kw), R, Wt)
        else:
            self.s.add(eng, lambda e: e.tensor_scalar(out, in0, s1, s2, op0, op1, **kw), R, Wt)

    def tt(self, eng, out, in0, in1, op, R, Wt):
        self.s.add(eng, lambda e: e.tensor_tensor(out, in0, in1, op), R, Wt)

    def stt(self, out, in0, scalar, in1, op0, op1, R, Wt):
        self.s.add("dve", lambda e: e.scalar_tensor_tensor(out, in0, scalar, in1, op0, op1), R, Wt)

    def cp(self, eng, out, in_, R, Wt):
        if eng == "act":
            self.s.add("act", lambda e: e.activation(out, in_, AF.Copy), R, Wt)
        else:
            self.s.add(eng, lambda e: e.tensor_copy(out, in_), R, Wt)

    def memset(self, eng, ap, val, Wt):
        self.s.add(eng, lambda e: e.memset(ap, val), (), Wt)

    def dma(self, q, out, in_, R, Wt):
        self.s.add(q, lambda e: e.dma_start(out, in_), R, Wt, dma=True)


    def setup(self):
        S, NT = self.S, self.NT
        self.cst = self.f32("P", 0, 384)
        self.ident_f = self.cst[:, 0:128]
        self.ones_f = self.cst[:, 128:256]
        self.cm_f = self.cst[:, 256:384]
        self.identb = self.bf("P", 1536, 128)
        self.cmb = self.bf("P", 1792, 128)
        self.bd = self.bf("P", 2048, 12 * 128)
        self.bs = self.bf("P", 5120, 12 * 128)
        self.pk = self.f32("P", 8192, self.npk)
        self.cfar = self.f32("P", 8192 + 4 * self.npk, 12)
        self.T_const = Buf("const")
        T = self.T_const
        stg = self.f32("xT", 0, 3084)
        Tstg = Buf("stg")
        self.dma("sp", self.cst, self.cst_d, (), [T])
        self.dma("sp", self.pk, self.pk_d, (), [T])
        self.ident16k = self.bf("P", 14048, 128)
        self.rmask = self.f32("P", 14016, 8)
        self.dma("sp", self.rmask, self.rmask_d, (), [T])
        self.dma("sp", stg, self.bias_d, (), [Tstg])
        self.cp("dve", self.identb, self.ident_f, [T], [T])
        self.cp("dve", self.cmb, self.cm_f, [T], [T])
        self.ts("dve", self.ident16k, self.ident_f, 16384.0, None, ALU.mult, None, [T], [T])
        self.cp("dve", self.cfar, stg[:, 3072:3084], [Tstg], [T])
        for m in range(12):
            self.stt(self.bd[:, m * 128:(m + 1) * 128], stg[:, m * 128:(m + 1) * 128], self.cfar[:, m:m + 1],
                     self.cm_f, ALU.subtract, ALU.add, [Tstg, T], [T])
            self.ts("dve", self.bs[:, m * 128:(m + 1) * 128], stg[:, 1536 + m * 128:1536 + (m + 1) * 128],
                    self.cfar[:, m:m + 1], None, ALU.subtract, None, [Tstg, T], [T])
        self.ssq = self.f32("st", 0, NT)
        self.var = self.f32("st", 128, NT)
        self.rstd = self.f32("st", 256, NT)
        self.mhalf = self.f32("st", 384, NT)
        self.zeros_b = self.bf("st", 512, 260)
        self.mhalfw = self.f32("st", 1080, 2 * NT)
        self.memset("dve", self.mhalf, -0.5, [T])
        self.memset("dve", self.mhalfw, -0.5, [T])
        self.memset("dve", self.zeros_b, 0.0, [T])
        self.s.barrier()

    def pkl(self, l, off, n):
        return self.pk[:, l * PL + off: l * PL + off + n]

    def phase1(self, l, xsrc, Tx):
        S, NT, NB = self.S, self.NT, self.NB
        T = self.T_const
        xt = [self.f32("W", 0, 1024), self.f32("W", 4096, 1024)]
        Txt = [Buf("xt0"), Buf("xt1")]
        junk = self.bf("W", 8192, 1024)
        Tjunk = Buf("junk")
        diag = [self.f32("W", 10240 + i * 512, 128) for i in range(4)]
        Tdiag = [Buf(f"diag{i}") for i in range(4)]
        xTb = self.bf("xT", 0, 8 * S)
        self.xT = xTb
        self.TxT = [Buf(f"xT{t}") for t in range(NT)]
        Tssq = Buf("ssq")
        for t in range(NT):
            b = t % 2
            self.dma("sp", xt[b], xsrc[t * 128:(t + 1) * 128, :], [Tx[t]], [Txt[b]])
            self.act(junk, xt[b], AF.Square, [Txt[b]], [Tjunk, Tssq], accum_out=self.ssq[:, t:t + 1])
            for hb in range(2):
                ps, _, pt = self.bankA()
                for c4 in range(4):
                    c = hb * 4 + c4
                    self.tr(ps[:, c4 * 128:(c4 + 1) * 128], xt[b][:, c * 128:(c + 1) * 128], self.ident_f,
                            [Txt[b], T], [pt])
                dst = xTb.rearrange("p (c s) -> p c s", c=8)[:, hb * 4:(hb + 1) * 4, t * 128:(t + 1) * 128]
                self.cp("dve" if hb == 0 else "act", dst, ps.rearrange("p (c s) -> p c s", c=4), [pt], [self.TxT[t]])
        Trs = Buf("rstd")
        self.ts("dve", self.var, self.ssq, 1.0 / D, EPS, ALU.mult, ALU.add, [Tssq], [Trs])
        self.tt("pool", self.rstd, self.var, self.mhalf, ALU.pow, [Trs, T], [Trs])
        self.Trstd = Trs
        self.rbc = self.f32("rbc", 0, S)
        self.Trbc = [Buf(f"rbc{i}") for i in range(NB)]
        for TB in range(NB):
            self.bcast(self.rstd, TB, self.rbc[:, TB * 512:(TB + 1) * 512], [Trs], [self.Trbc[TB]], diag, Tdiag)

    def bcast(self, vec, TB, dst, R, Wt, diag, Tdiag):
        T = self.T_const
        ps, _, pt = self.bankA()
        for tt in range(4):
            t = TB * 4 + tt
            self.ts("dve", diag[tt], self.ident_f, vec[:, t:t + 1], None, ALU.mult, None, R + [T], [Tdiag[tt]])
            self.mm(ps[:, tt * 128:(tt + 1) * 128], self.ones_f, diag[tt], True, True, [Tdiag[tt], T], [pt])
        self.cp("act", dst, ps, [pt], Wt)

    def init_w(self):
        self.wst = [self.f32("wst", 0, 1024), self.f32("wst", 4096, 1024)]
        self.Twst = [Buf("wst0"), Buf("wst1")]
        self.wbf = [self.bf("wbf", i * 4096, 2048).rearrange("p (c n) -> p c n", c=8) for i in range(3)]
        self.Twbf = [Buf(f"wbf{i}") for i in range(3)]
        self.nw = 0

    def load_w(self, l, c0, n, slot, col, scale=1.0, src=None, gain=None, stg_view=None):
        k = self.nw % 2
        self.nw += 1
        T = self.T_const
        stg = self.wst[k].rearrange("p (c n) -> p c n", c=8)
        if src is None:
            src = self.win_d[l, :, c0:c0 + n].rearrange("(c p) n -> p c n", p=128)
        if isinstance(src, list):
            for dv, sv in src:
                self.dma("sp", dv, sv, (), [self.Twst[k]])
        else:
            self.dma("sp", stg[:, :, 0:n], src, (), [self.Twst[k]])
        for c in range(8):
            g = self.pkl(l, c, 1) if gain is None else gain(c)
            self.ts("pool", self.wbf[slot][:, c, col:col + n], stg[:, c, 0:n], g, float(scale),
                    ALU.mult, ALU.mult, [self.Twst[k], T], [self.Twbf[slot]])

    def proj_fm(self, lhs, M, TB, pool="A", ncol=512, c0=0):
        ps, psb, pt = self.bankA() if pool == "A" else self.bankB()
        xT3 = self.xT.rearrange("p (c s) -> p c s", c=8)
        R = [self.TxT[TB * 4 + i] for i in range(4)]
        for c in range(8):
            ap, toks = lhs(c)
            self.mm(ps[0:M, 0:ncol], ap, xT3[:, c, TB * 512 + c0:TB * 512 + c0 + ncol], c == 0, c == 7,
                    R + toks, [pt])
        return ps, pt

    def proj_tm(self, rhs, N, t, pool="A"):
        ps, psb, pt = self.bankA() if pool == "A" else self.bankB()
        xT3 = self.xT.rearrange("p (c s) -> p c s", c=8)
        for c in range(8):
            ap, toks = rhs(c)
            self.mm(ps[:, 0:N], xT3[:, c, t * 128:(t + 1) * 128], ap, c == 0, c == 7,
                    [self.TxT[t]] + toks, [pt])
        return ps, pt

    def wsl(self, slot, col, n):
        return lambda c: (self.wbf[slot][:, c, col:col + n], [self.Twbf[slot]])

    def mixer_d(self, l):
        S, NB = self.S, self.NB
        T = self.T_const
        u = self.f32("QI", 0, S + 2)
        Tu = Buf("u")
        self.memset("dve", u[:, 0:2], 0.0, [Tu])
        t1 = self.f32("W", 0, 512)
        t2 = self.f32("W", 2048, 512)
        acc = self.f32("W", 4096, 512)
        yo = [self.bf("W", 6144, 512), self.bf("W", 7168, 512)]
        Tt1, Tt2, Tacc = Buf("t1"), Buf("t2"), Buf("acc")
        Tyo = [Buf("yo0"), Buf("yo1")]
        k = 0
        for fc in range(2):
            for gi, name in enumerate(("d_b", "d_c", "d_h", "d_gate")):
                self.load_w(l, OFF[name] + fc * 128, 128, gi // 2, (gi % 2) * 128)
            cw = self.pkl(l, 11 + fc * 3, 3)
            for TB in range(NB):
                pool = "A" if TB % 2 == 0 else "B"
                psb_, ptb = self.proj_fm(self.wsl(0, 0, 128), 128, TB, pool)
                psc, ptc = self.proj_fm(self.wsl(0, 128, 128), 128, TB, pool)
                psh, pth = self.proj_fm(self.wsl(1, 0, 128), 128, TB, pool)
                psg, ptg = self.proj_fm(self.wsl(1, 128, 128), 128, TB, pool)
                rb = self.rbc[:, TB * 512:(TB + 1) * 512]
                Trb = [self.Trbc[TB]]
                ub = u[:, 2 + TB * 512: 2 + (TB + 1) * 512]
                self.tt("dve", t1, psc, rb, ALU.mult, [ptc] + Trb, [Tt1])
                self.tt("dve", t2, psh, rb, ALU.mult, [pth] + Trb, [Tt2])
                self.tt("dve", ub, t1, t2, ALU.mult, [Tt1, Tt2], [Tu])
                self.ts("dve", acc, u[:, TB * 512: TB * 512 + 512], cw[:, 0:1], None, ALU.mult, None,
                        [Tu, T], [Tacc])
                self.stt(acc, u[:, TB * 512 + 1: TB * 512 + 513], cw[:, 1:2], acc, ALU.mult, ALU.add,
                         [Tu, T, Tacc], [Tacc])
                self.stt(acc, ub, cw[:, 2:3], acc, ALU.mult, ALU.add, [Tu, T, Tacc], [Tacc])
                self.tt("dve", t1, psb_, rb, ALU.mult, [ptb] + Trb, [Tt1])
                self.tt("dve", acc, acc, t1, ALU.mult, [Tacc, Tt1], [Tacc])
                self.tt("dve", t2, psg, rb, ALU.mult, [ptg] + Trb, [Tt2])
                self.act(t1, t2, AF.Silu, [Tt2], [Tt1])
                y = yo[k % 2]
                Ty = Tyo[k % 2]
                k += 1
                self.tt("dve", y, acc, t1, ALU.mult, [Tacc, Tt1], [Ty])
                r0 = 768 + fc * 128
                self.dma("pool", self.yT_d[r0:r0 + 128, TB * 512:(TB + 1) * 512], y, [Ty],
                         [self.TyT[(r0 // 128, TB)]])

    def outproj(self, l, xsrc, Tx, Tdst, last):
        S, NT, NB = self.S, self.NT, self.NB
        T = self.T_const
        wo = self.bf("QT", 0, 8 * 1024).rearrange("p (c n) -> p c n", c=8)
        Two = Buf("wo")
        for j in range(8):
            k = self.nw % 2
            self.nw += 1
            stg = self.wst[k].rearrange("p (c n) -> p c n", c=8)
            self.dma("sp", stg, self.wout_d[l, :, j * 128:(j + 1) * 128].rearrange("(c p) n -> p c n", p=128),
                     (), [self.Twst[k]])
            self.cp("pool" if j % 2 else "dve", wo[:, :, j * 128:(j + 1) * 128], stg, [self.Twst[k]], [Two])
        base = 16384
        yTb = [self.bf("QT", base + i * 8192, 4096).rearrange("p (c s) -> p c s", c=8) for i in range(2)]
        TyTb = [Buf("yTb0"), Buf("yTb1")]
        xt = [self.f32("QT", base + 16384 + i * 4096, 1024) for i in range(2)]
        Txt = [Buf("oxt0"), Buf("oxt1")]
        xn = [self.f32("QT", base + 24576 + i * 4096, 1024) for i in range(2)]
        Txn = [Buf("xn0"), Buf("xn1")]
        junk = self.bf("W", 8192, 1024)
        Tjunk = Buf("junk2")
        st4 = self.f32("st", 1040, 8)
        Tst4 = Buf("st4")
        fg = self.pk[:, self.depth * PL: self.depth * PL + D]
        dst = self.xres_d if not last else self.out_d
        for TB in range(NB):
            yb = yTb[TB % 2]
            Ty = TyTb[TB % 2]
            self.dma("sp", yb, self.yT_d[:, TB * 512:(TB + 1) * 512].rearrange("(c p) s -> p c s", p=128),
                     [self.TyT[(r, TB)] for r in range(8)], [Ty])
            for tt_ in range(4):
                t = TB * 4 + tt_
                b = t % 2
                self.dma("sp", xt[b], xsrc[t * 128:(t + 1) * 128, :], [Tx[t]], [Txt[b]])
                for half in range(2):
                    ps, _, pt = self.bankA()
                    for c in range(8):
                        self.mm(ps, yb[:, c, tt_ * 128:(tt_ + 1) * 128], wo[:, c, half * 512:(half + 1) * 512],
                                c == 0, c == 7, [Ty, Two], [pt])
                    self.tt("dve", xn[b][:, half * 512:(half + 1) * 512], ps, xt[b][:, half * 512:(half + 1) * 512],
                            ALU.add, [pt, Txt[b]], [Txn[b]])
                if not last:
                    self.dma("pool", dst[t * 128:(t + 1) * 128, :], xn[b], [Txn[b]], [Tdst[t]])
                else:
                    c0 = (t % 2) * 4
                    self.act(junk, xn[b], AF.Square, [Txn[b]], [Tjunk, Tst4], accum_out=st4[:, c0:c0 + 1])
                    self.ts("dve", st4[:, c0 + 1:c0 + 2], st4[:, c0:c0 + 1], 1.0 / D, EPS, ALU.mult, ALU.add,
                            [Tst4], [Tst4])
                    self.tt("pool", st4[:, c0 + 2:c0 + 3], st4[:, c0 + 1:c0 + 2], self.mhalf[:, 0:1], ALU.pow,
                            [Tst4, T], [Tst4])
                    self.stt(xn[b], xn[b], st4[:, c0 + 2:c0 + 3], fg, ALU.mult, ALU.mult, [Txn[b], Tst4, T], [Txn[b]])
                    self.dma("pool", dst[t * 128:(t + 1) * 128, :], xn[b], [Txn[b]], [Tdst[t]])

    def init_attn(self):
        S, NT = self.S, self.NT
        self.QT3 = self.bf("QT", 0, 2 * S).rearrange("p (c s) -> p c s", c=2)
        self.KT3 = self.bf("KT", 0, 2 * S).rearrange("p (c s) -> p c s", c=2)
        self.VT4 = self.bf("VT", 0, NT * 260).rearrange("p (t h d) -> p t h d", t=NT, h=4)
        self.TQT = [Buf(f"QT{i}") for i in range(self.NB)]
        self.TKT = [Buf(f"KT{i}") for i in range(self.NB)]
        self.TVT = [Buf(f"VT{t}") for t in range(NT)]
        self.PT = [self.bf("W", 8192 + i * 1024, 512) for i in range(4)]
        self.TPT = [Buf(f"PT{i}") for i in range(4)]
        self.npt = 0
        self.nPT = 4
        self.ytm = self.bf("W", 12288, 1024).rearrange("p (t f) -> p t f", t=4)
        self.Tytm = Buf("ytm")
        self.sg = self.bf("W", 14336, 1024).rearrange("p (t f) -> p t f", t=4)
        self.Tsg = Buf("sg")
        self.yTs = self.bf("W", 6144, 1024).rearrange("p (c s) -> p c s", c=2)
        self.TyTs = Buf("yTs")

    def set_ones(self):
        for t in range(self.NT):
            self.memset("pool", self.VT4[:, t, :, 64:65], 1.0, [self.TVT[t]])

    def attn_map(self, I, qk, vfn, extra, cbias, pairs=True):
        g = self.attn_map_gen(I, qk, vfn, extra, cbias, pairs=pairs)
        while True:
            try:
                next(g)
            except StopIteration as e:
                return e.value

    def attn_map_gen(self, I, qk, vfn, extra, cbias, spool="A", opool="B", LA=2, pairs=False):
        T = self.T_const
        O, _, Ot = self.bank(opool)
        self.mm(O[:, 0:260], self.identb, self.zeros_b, True, True, [T], [Ot])
        if pairs:
            units = [(2 * p, 2 * p + 1) for p in range(2 * I)] + [(j,) for j in range(4 * I, 4 * I + 4)]
            LA = 1
        else:
            units = [(j,) for j in range(4 * I + 4)]
        pend = {}
        for s_ in range(len(units) + LA):
            if s_ > 0:
                yield
            if s_ < len(units):
                u = units[s_]
                if pairs:
                    psp, ptoks = self.bankpair(spool)
                    pi = self.npt % 2
                    P = self.bf("W", 8192 + pi * 2048, 1024)
                    Pt = self.TPT[pi]
                else:
                    ps1, _, pt1 = self.bank(spool)
                    psp, ptoks = ps1, [pt1]
                    pi = self.npt % self.nPT
                    P, Pt = self.PT[pi], self.TPT[pi]
                self.npt += 1
                info = []
                for k_, j in enumerate(u):
                    r0 = max(0, j - 4 * I)
                    a = r0 * 128
                    ps = psp[:, k_ * 512:(k_ + 1) * 512]
                    pt = ptoks[k_]
                    adds = []
                    for tt in range(r0, 4):
                        for item in extra(j, tt):
                            adds.append((tt, item[0], item[1], item[2] if len(item) > 2 else self.identb))
                    n = len(qk)
                    for k, (kfn, qfn, base) in enumerate(qk):
                        kap, ktok = kfn(j)
                        qap, qtok = qfn(I * 512 + a, (I + 1) * 512)
                        kw = {"tile_position": (96, 0)} if base == 96 else {}
                        self.mm(ps[:, a:512], kap, qap, k == 0, (k == n - 1) and not adds, ktok + qtok, [pt], **kw)
                    for k, (tt, ap, toks, rhs_) in enumerate(adds):
                        self.mm(ps[:, tt * 128:(tt + 1) * 128], ap, rhs_, False, k == len(adds) - 1,
                                toks + [T], [pt], skip_group_check=True)
                    info.append((j, r0, k_))
                a0 = info[0][1] * 128 if len(u) == 1 else 0
                w = 512 * len(u)
                if cbias is None:
                    self.act(P[:, a0:w], psp[:, a0:w], AF.Exp, list(ptoks), [Pt])
                else:
                    self.act(P[:, a0:w], psp[:, a0:w], AF.Exp, list(ptoks) + [T], [Pt], bias=cbias)
                pend[s_] = (P, Pt, info)
            sp = s_ - LA
            if sp >= 0:
                P, Pt, info = pend.pop(sp)
                for (j, r0, k_) in info:
                    vap, vtok = vfn(j)
                    for tt in range(r0, 4):
                        self.mm(O[:, tt * 65:(tt + 1) * 65], P[:, k_ * 512 + tt * 128:k_ * 512 + (tt + 1) * 128], vap,
                                False, j == 4 * I + tt, [Pt] + vtok, [Ot], skip_group_check=True)
        return O, Ot

    def run_chain(self, items, LA):
        prev = None
        for (pre, make, post, nunits) in items:
            if pre is not None:
                pre()
            g = make()
            res = None
            alive = True
            for _ in range(LA):
                try:
                    next(g)
                except StopIteration as e:
                    res = e.value
                    alive = False
                    break
            if prev is not None:
                self._finish(prev)
            for _ in range(nunits - LA):
                if not alive:
                    break
                try:
                    next(g)
                except StopIteration as e:
                    res = e.value
                    alive = False
            prev = (g, post, alive, res)
        if prev is not None:
            self._finish(prev)

    def flush_store(self):
        if getattr(self, "deferred", None) is not None:
            f = self.deferred
            self.deferred = None
            f()

    def _finish(self, p):
        g, post, alive, res = p
        while alive:
            try:
                next(g)
            except StopIteration as e:
                res = e.value
                alive = False
        post(*res)

    def load_sg(self, mi, I):
        self.dma("sp", self.sg, self.gate_d[mi, I * 512:(I + 1) * 512, :].rearrange("(t p) f -> p t f", p=128),
                 [self.Tgate[(mi, I * 4 + tt)] for tt in range(4)], [self.Tsg])

    def store_y(self, mi, I, r0=None, nfc=2):
        T = self.T_const
        if r0 is None:
            r0 = mi * 2
        for fc in range(nfc):
            ps, psb, pt = self.bank(self.trpool)
            for tt in range(4):
                self.tr(psb[:, tt * 128:(tt + 1) * 128], self.ytm[:, tt, fc * 128:(fc + 1) * 128], self.identb,
                        [self.Tytm, T], [pt])
            self.cp("act" if self.trpool == "as" else "dve", self.yTs[:, fc, :], psb[:, 0:512], [pt], [self.TyTs])
        self.dma("pool", self.yT_d[r0 * 128:(r0 + nfc) * 128, I * 512:(I + 1) * 512].rearrange("(c p) s -> p c s", p=128),
                 self.yTs[:, 0:nfc, :], [self.TyTs], [self.TyT[(r0 + i, I)] for i in range(nfc)])

    def proj_gate(self, l, mi, name):
        for hf in range(2):
            self.load_w(l, OFF[name] + hf * 128, 128, 2, hf * 128)
        gt = [self.bf("W", 15360, 256), self.bf("W", 15872, 256)]
        Tg = [Buf("gt0"), Buf("gt1")]
        for t in range(self.NT):
            ps, pt = self.proj_tm(self.wsl(2, 0, 256), 256, t, "B")
            b = t % 2
            self.act(gt[b], ps[:, 0:256], AF.Silu, [pt, self.Trstd], [Tg[b]], scale=self.rstd[:, t:t + 1])
            self.dma("pool", self.gate_d[mi, t * 128:(t + 1) * 128, :], gt[b], [Tg[b]], [self.Tgate[(mi, t)]])

    def proj_fm_to(self, l, name, ncols, dst3, Tdst, scale=1.0):
        for ch in range(ncols // 128):
            slot = ch % 2
            self.load_w(l, OFF[name] + ch * 128, 128, slot, 0, scale)
            for TB in range(self.NB):
                ps, pt = self.proj_fm(self.wsl(slot, 0, 128), 128, TB, "A")
                self.tt("dve", dst3[:, ch, TB * 512:(TB + 1) * 512], ps, self.rbc[:, TB * 512:(TB + 1) * 512],
                        ALU.mult, [pt, self.Trbc[TB]], [Tdst[TB]])

    def proj_v(self, l, name):
        for hf in range(2):
            self.load_w(l, OFF[name] + hf * 128, 128, 2, hf * 128)
        for t in range(self.NT):
            ps, pt = self.proj_tm(self.wsl(2, 0, 256), 256, t, "B")
            self.act(self.VT4[:, t, :, 0:64], ps[:, 0:256].rearrange("p (h d) -> p h d", h=4), AF.Copy,
                     [pt, self.Trstd], [self.TVT[t]], scale=self.rstd[:, t:t + 1])

    def mixer_b(self, l):
        S, NT, NB = self.S, self.NT, self.NB
        T = self.T_const
        lam_init = 0.8 - 0.6 * math.exp(-0.3 * l)
        self.proj_fm_to(l, "b_q", 256, self.QT3, self.TQT, scale=32 ** -0.5)
        self.proj_fm_to(l, "b_k", 256, self.KT3, self.TKT)
        self.proj_v(l, "b_v")
        self.proj_gate(l, 1, "b_gate")
        self.s.barrier()
        sm = self.f32("st", 1344, 32)
        Tsm = Buf("sm")
        lp = self.pkl(l, 17, 128)
        pr = self.f32("st", 1500, 64)
        self.tt("dve", pr[:, 0:32], lp[:, 0:32], lp[:, 32:64], ALU.mult, [T], [Tsm])
        self.tt("dve", pr[:, 32:64], lp[:, 64:96], lp[:, 96:128], ALU.mult, [T], [Tsm])
        self.s.add("dve", lambda e: e.tensor_reduce(sm[:, 0:2], pr.rearrange("p (a b) -> p a b", a=2), AX.X, ALU.add),
                   [Tsm], [Tsm])
        self.act(sm[:, 2:4], sm[:, 0:2], AF.Exp, [Tsm], [Tsm])
        self.tt("dve", sm[:, 4:5], sm[:, 2:3], sm[:, 3:4], ALU.subtract, [Tsm], [Tsm])
        self.ts("dve", sm[:, 5:6], sm[:, 4:5], -1.0, -lam_init, ALU.mult, ALU.add, [Tsm], [Tsm])
        neglam = sm[:, 5:6]
        gsub = self.pkl(l, 145, 64)
        rec = sm[:, 8:16]
        o_h = self.f32("W", 0, 256).rearrange("p (t d) -> p t d", t=4)
        sq = self.f32("W", 1024, 256).rearrange("p (t d) -> p t d", t=4)
        gsg = self.f32("W", 2048, 256).rearrange("p (t d) -> p t d", t=4)
        Toh, Tsq, Tgsg = Buf("oh"), Buf("sq"), Buf("gsg")
        cvar = 1.0 / (1.0 - lam_init) ** 2
        Qz = [self.bf("W", 3072, 512), self.bf("W", 4096, 512), self.bf("W", 5120, 512)]
        TQz = [Buf("qz0"), Buf("qz1"), Buf("qz2")]
        nqz = [0]
        for I in range(NB):
            self.load_sg(1, I)
            items = []
            Ores = {}
            for h in range(4):
                for m in range(2):
                    ch, base = h // 2, 0
                    mi = 2 * h + m
                    m4 = (h % 2) * 2 + m
                    qz, Tqz = Qz[nqz[0] % 3], TQz[nqz[0] % 3]
                    nqz[0] += 1

                    def pre(qz=qz, Tqz=Tqz, ch=ch, m4=m4, I=I):
                        self.ts("dve", qz, self.QT3[:, ch, I * 512:(I + 1) * 512], self.rmask[:, m4:m4 + 1], None,
                                ALU.mult, None, [self.TQT[I], T], [Tqz])
                    kfn = lambda j, ch=ch: (self.KT3[:, ch, j * 128:(j + 1) * 128], [self.TKT[j // 4]])
                    qfn = lambda c0, c1, qz=qz, Tqz=Tqz, I=I: (qz[:, c0 - I * 512:c1 - I * 512], [Tqz])
                    vfn = lambda j, h=h: (self.VT4[:, j, h, :], [self.TVT[j]])

                    def extra(j, tt, mi=mi, I=I):
                        i = 4 * I + tt
                        if j == i:
                            return [(self.bd[:, mi * 128:(mi + 1) * 128], [T])]
                        if j == i - 1:
                            return [(self.bs[:, mi * 128:(mi + 1) * 128], [T])]
                        return []

                    def make(kfn=kfn, qfn=qfn, vfn=vfn, extra=extra, base=base, I=I):
                        return self.attn_map_gen(I, [(kfn, qfn, base)], vfn, extra, None, pairs=True)

                    def post(O, Ot, h=h, m=m, I=I):
                        Ores[(h, m)] = (O, Ot)
                        if m == 0:
                            if h == 0:
                                self.flush_store()
                            return
                        (O0, T0), (O1, T1) = Ores[(h, 0)], Ores[(h, 1)]
                        O0v = O0[:, 0:260].rearrange("p (t d) -> p t d", t=4)
                        O1v = O1[:, 0:260].rearrange("p (t d) -> p t d", t=4)
                        self.s.add("dve", lambda e, O0v=O0v: e.reciprocal(rec[:, 0:4], O0v[:, :, 64]), [T0], [Tsm])
                        self.s.add("dve", lambda e, O1v=O1v: e.reciprocal(rec[:, 4:8], O1v[:, :, 64]), [T1], [Tsm])
                        self.ts("dve", rec[:, 4:8], rec[:, 4:8], neglam, None, ALU.mult, None, [Tsm], [Tsm])
                        for tt in range(4):
                            self.ts("dve", o_h[:, tt, :], O0v[:, tt, 0:64], rec[:, tt:tt + 1], None, ALU.mult, None,
                                    [T0, Tsm], [Toh])
                        for tt in range(4):
                            self.stt(o_h[:, tt, :], O1v[:, tt, 0:64], rec[:, 4 + tt:5 + tt], o_h[:, tt, :], ALU.mult, ALU.add,
                                     [T1, Tsm, Toh], [Toh])
                        self.tt("dve", sq, o_h, o_h, ALU.mult, [Toh], [Tsq])
                        self.s.add("dve", lambda e: e.tensor_reduce(sm[:, 16:20], sq, AX.X, ALU.add), [Tsq], [Tsm])
                        self.ts("dve", sm[:, 20:24], sm[:, 16:20], cvar / 64.0, EPS * cvar, ALU.mult, ALU.add, [Tsm], [Tsm])
                        self.tt("pool", sm[:, 24:28], sm[:, 20:24], self.mhalf[:, 0:4], ALU.pow, [Tsm, T], [Tsm])
                        for tt in range(4):
                            self.tt("dve", gsg[:, tt, :], self.sg[:, tt, h * 64:(h + 1) * 64], gsub, ALU.mult,
                                    [self.Tsg, T], [Tgsg])
                        for tt in range(4):
                            self.stt(self.ytm[:, tt, h * 64:(h + 1) * 64], o_h[:, tt, :], sm[:, 24 + tt:25 + tt], gsg[:, tt, :],
                                     ALU.mult, ALU.mult, [Toh, Tsm, Tgsg], [self.Tytm])
                    items.append((pre, make, post, 2 * I + 4))
            self.run_chain(items, 1)
            self.deferred = (lambda I=I: self.store_y(1, I))
        self.flush_store()

    def mixer_a(self, l):
        S, NT, NB = self.S, self.NT, self.NB
        T = self.T_const
        self.proj_gate(l, 0, "a_gate")
        for hf in range(2):
            self.load_w(l, OFF["a_cq"] + hf * 128, 128, 0, hf * 128)
        self.load_w(l, OFF["a_ckv"], 128, 1, 0)
        self.load_w(l, 320, 96, 1, 128)
        self.load_w(l, 320, 64, 2, 0)
        self.load_w(l, 400, 16, 2, 64)
        self.load_w(l, 384, 16, 2, 80)
        wq = self.bf("X1", 0, 768).rearrange("p (c n) -> p c n", c=2)
        wqs = self.bf("X1", 1536, 768).rearrange("p (c n) -> p c n", c=2)
        wkv = self.bf("X1", 3072, 512)
        Twq = Buf("wq")
        k = self.nw % 2
        self.nw += 1
        stg = self.wst[k][:, 0:768].rearrange("p (c n) -> p c n", c=2)
        self.dma("sp", stg, self.wuq_d[l].rearrange("(c p) n -> p c n", p=128), (), [self.Twst[k]])
        sc = 96 ** -0.5
        for c in range(2):
            g = self.pkl(l, 8 + c, 1)
            self.ts("pool", wq[:, c, :], stg[:, c, :], g, sc, ALU.mult, ALU.mult, [self.Twst[k], T], [Twq])
            s4 = stg[:, c, :].rearrange("p (h e) -> p h e", h=4)
            d4 = wqs[:, c, :].rearrange("p (h e) -> p h e", h=4)
            self.ts("pool", d4[:, :, 0:64], s4[:, :, 0:64], g, sc, ALU.mult, ALU.mult, [self.Twst[k], T], [Twq])
            self.ts("pool", d4[:, :, 64:80], s4[:, :, 80:96], g, sc, ALU.mult, ALU.mult, [self.Twst[k], T], [Twq])
            self.ts("pool", d4[:, :, 80:96], s4[:, :, 64:80], g, sc, ALU.mult, ALU.mult, [self.Twst[k], T], [Twq])
        k = self.nw % 2
        self.nw += 1
        stg2 = self.wst[k][:, 0:512]
        self.dma("sp", stg2, self.wukv_d[l], (), [self.Twst[k]])
        self.ts("pool", wkv, stg2, self.pkl(l, 10, 1), None, ALU.mult, None, [self.Twst[k], T], [Twq])
        wkv4 = wkv.rearrange("p (h e) -> p h e", h=4)
        sm = self.f32("st", 1344, 3 * NT + 8)
        Tsm = Buf("sma")
        junk = self.bf("W", 0, 256)
        Tj = Buf("junka")
        sq_q, sq_k = sm[:, 0:NT], sm[:, NT:2 * NT]
        for t in range(NT):
            ps, pt = self.proj_tm(self.wsl(0, 0, 256), 256, t, "B")
            self.act(junk, ps[:, 0:256], AF.Square, [pt, self.Trstd], [Tj, Tsm], scale=self.rstd[:, t:t + 1],
                     accum_out=sq_q[:, t:t + 1])
            ps, pt = self.proj_tm(self.wsl(1, 0, 128), 128, t, "B")
            self.act(junk[:, 0:128], ps[:, 0:128], AF.Square, [pt, self.Trstd], [Tj, Tsm], scale=self.rstd[:, t:t + 1],
                     accum_out=sq_k[:, t:t + 1])
        rqr = self.f32("st", 1760, 2 * NT)
        Trq = Buf("rqr")
        self.ts("dve", sq_q, sq_q, 1.0 / 256, EPS, ALU.mult, ALU.add, [Tsm], [Tsm])
        self.ts("dve", sq_k, sq_k, 1.0 / 128, EPS, ALU.mult, ALU.add, [Tsm], [Tsm])
        self.tt("pool", sm[:, 0:2 * NT], sm[:, 0:2 * NT], self.mhalfw, ALU.pow, [Tsm, T], [Tsm])
        self.tt("dve", rqr[:, 0:NT], sq_q, self.rstd, ALU.mult, [Tsm, self.Trstd], [Trq])
        self.tt("dve", rqr[:, NT:2 * NT], sq_k, self.rstd, ALU.mult, [Tsm, self.Trstd], [Trq])
        cqn = self.bf("W", 0, 1024).rearrange("p (c s) -> p c s", c=2)
        ckvn = self.bf("W", 2048, 512)
        ccb = self.f32("W", 3072, 512)
        ssb = self.f32("W", 5120, 512)
        t1 = self.f32("W", 7168, 512)
        t2 = self.f32("W", 9216, 512)
        rq_bc = self.f32("W", 11264, 512)
        rk_bc = self.f32("W", 13312, 512)
        Tcqn, Tckvn, Tcc, Tt1, Tt2, Trqb, Trkb = (Buf(n) for n in ("cqn", "ckvn", "cc", "t1a", "t2a", "rqb", "rkb"))
        diag = [self.f32("st", 2048 + i * 512, 128) for i in range(4)]
        Tdiag = [Buf(f"diaga{i}") for i in range(4)]
        Qh = [self.bf("QT", i * 2 * S, S) for i in range(2)]
        Kh = [self.bf("KT", i * 2 * S, S) for i in range(2)]
        rec = self.f32("st", 2016, 4)
        Trec = Buf("reca")
        for pair in range(2):
            self.s.barrier()
            for TB in range(NB):
                sl = slice(TB * 512, (TB + 1) * 512)
                Trb = self.Trbc[TB]
                self.dma("sp", ccb, self.cc_d[:, sl], (), [Tcc])
                self.dma("sp", ssb, self.ss_d[:, sl], (), [Tcc])
                self.bcast(rqr[:, 0:NT], TB, rq_bc, [Trq], [Trqb], diag, Tdiag)
                self.bcast(rqr[:, NT:2 * NT], TB, rk_bc, [Trq], [Trkb], diag, Tdiag)
                ps, pt = self.proj_fm(self.wsl(1, 0, 128), 128, TB, "A")
                self.tt("dve", ckvn, ps, rk_bc, ALU.mult, [pt, Trkb], [Tckvn])
                for hh in range(2):
                    h = pair * 2 + hh
                    ps, _, pt = self.bankA()
                    self.mm(ps[0:64, :], wkv4[:, h, 0:64], ckvn, True, True, [Twq, Tckvn], [pt])
                    self.cp("act", Kh[hh][0:64, sl], ps[0:64, :], [pt], [self.TKT[TB]])
                if pair == 0:
                    for tt_ in range(4):
                        t = TB * 4 + tt_
                        ps, _, pt = self.bankB()
                        self.mm(ps[:, 0:256].rearrange("p (h d) -> p h d", h=4), ckvn[:, tt_ * 128:(tt_ + 1) * 128],
                                wkv4[:, :, 64:128], True, True, [Twq, Tckvn], [pt])
                        self.cp("act", self.VT4[:, t, :, 0:64], ps[:, 0:256].rearrange("p (h d) -> p h d", h=4),
                                [pt], [self.TVT[t]])
                ps1, pt1 = self.proj_fm(self.wsl(1, 128, 96), 96, TB, "A")
                ps2, pt2 = self.proj_fm(self.wsl(2, 0, 96), 96, TB, "A")
                self.tt("dve", t1[64:96, :], ps1[64:96, :], ccb[64:96, :], ALU.mult, [pt1, Tcc], [Tt1])
                self.tt("dve", t2[64:96, :], ps2[64:96, :], ssb[64:96, :], ALU.mult, [pt2, Tcc], [Tt2])
                self.tt("dve", t1[64:96, :], t1[64:96, :], t2[64:96, :], ALU.add, [Tt1, Tt2], [Tt1])
                self.tt("dve", Kh[0][64:96, sl], t1[64:96, :], self.rbc[64:96, sl], ALU.mult, [Tt1, Trb], [self.TKT[TB]])
                self.tt("dve", Kh[1][64:96, sl], t1[64:96, :], self.rbc[64:96, sl], ALU.mult, [Tt1, Trb], [self.TKT[TB]])
                for c in range(2):
                    ps, pt = self.proj_fm(self.wsl(0, c * 128, 128), 128, TB, "A")
                    self.tt("dve", cqn[:, c, :], ps, rq_bc, ALU.mult, [pt, Trqb], [Tcqn])
                for hh in range(2):
                    h = pair * 2 + hh
                    psa, _, pta = self.bankA()
                    psb_, _, ptb = self.bankA()
                    for c in range(2):
                        self.mm(psa[0:96, :], wq[:, c, h * 96:(h + 1) * 96], cqn[:, c, :], c == 0, c == 1, [Twq, Tcqn], [pta])
                    for c in range(2):
                        self.mm(psb_[0:96, :], wqs[:, c, h * 96:(h + 1) * 96], cqn[:, c, :], c == 0, c == 1, [Twq, Tcqn], [ptb])
                    self.cp("act", Qh[hh][0:64, sl], psa[0:64, :], [pta], [self.TQT[TB]])
                    self.tt("dve", t1[64:96, :], psa[64:96, :], ccb[64:96, :], ALU.mult, [pta, Tcc], [Tt1])
                    self.tt("dve", t2[64:96, :], psb_[64:96, :], ssb[64:96, :], ALU.mult, [ptb, Tcc], [Tt2])
                    self.tt("dve", Qh[hh][64:96, sl], t1[64:96, :], t2[64:96, :], ALU.add, [Tt1, Tt2], [self.TQT[TB]])
            self.s.barrier()
            for I in range(NB):
                self.load_sg(0, I)
                items = []
                for hh in range(2):
                    h = pair * 2 + hh
                    kfn = lambda j, hh=hh: (Kh[hh][0:96, j * 128:(j + 1) * 128], [self.TKT[j // 4]])
                    qfn = lambda c0, c1, hh=hh, I=I: (Qh[hh][0:96, c0:c1], [self.TQT[I]])
                    vfn = lambda j, h=h: (self.VT4[:, j, h, :], [self.TVT[j]])

                    def extra(j, tt, I=I):
                        return [(self.cmb, [T])] if j == 4 * I + tt else []

                    def make(kfn=kfn, qfn=qfn, vfn=vfn, extra=extra, I=I):
                        return self.attn_map_gen(I, [(kfn, qfn, 0)], vfn, extra, None, pairs=True)

                    def post(O, Ot, hh=hh, h=h):
                        if hh == 0:
                            self.flush_store()
                        Ov = O[:, 0:260].rearrange("p (t d) -> p t d", t=4)
                        self.s.add("dve", lambda e, Ov=Ov: e.reciprocal(rec, Ov[:, :, 64]), [Ot], [Trec])
                        for tt in range(4):
                            self.stt(self.ytm[:, tt, hh * 64:(hh + 1) * 64], Ov[:, tt, 0:64], rec[:, tt:tt + 1],
                                     self.sg[:, tt, h * 64:(h + 1) * 64], ALU.mult, ALU.mult, [Ot, Trec, self.Tsg], [self.Tytm])
                    items.append((None, make, post, 2 * I + 4))
                self.run_chain(items, 1)
                self.deferred = (lambda I=I, pair=pair: self.store_y(0, I, r0=pair, nfc=1))
            self.flush_store()

    def mixer_c_proj(self, l):
        S, NT, NB = self.S, self.NT, self.NB
        T = self.T_const
        self.proj_gate(l, 2, "c_gate")
        self.proj_fm_to(l, "c_q", 256, self.QT3, self.TQT, scale=64 ** -0.5)
        self.proj_fm_to(l, "c_k", 256, self.KT3, self.TKT)
        self.proj_v(l, "c_v")
        self.QI3 = self.bf("QI", 0, 2 * S).rearrange("p (c s) -> p c s", c=2)
        self.TQI = [Buf(f"QI{i}") for i in range(NB)]
        self.proj_fm_to(l, "c_qidx", 256, self.QI3, self.TQI, scale=1.0 / 16.0)
        self.KX = self.bf("X1", 0, S)
        self.TKX = [Buf(f"KX{i}") for i in range(NB)]
        base = self.win_d[l, :, OFF["c_kidx"]:OFF["c_kidx"] + 32]
        k = self.nw % 2
        stg4 = self.wst[k].rearrange("p (c r n) -> p c r n", c=8, r=4)
        srcs = [(stg4[:, :, r, :], base.rearrange("(c p) n -> p c n", p=128)) for r in range(4)]
        self.load_w(l, 0, 128, 0, 0, src=srcs)
        for TB in range(NB):
            ps, pt = self.proj_fm(self.wsl(0, 0, 128), 128, TB, "A")
            self.tt("dve", self.KX[:, TB * 512:(TB + 1) * 512], ps, self.rbc[:, TB * 512:(TB + 1) * 512], ALU.mult,
                    [pt, self.Trbc[TB]], [self.TKX[TB]])
        self.load_w(l, OFF["c_widx"], 8, 1, 0)
        self.wabs = self.f32("st", 2048, NT * 8)
        self.wsgn = self.f32("st", 3072, NT * 8)
        self.Tww = Buf("ww")
        for t in range(NT):
            ps, pt = self.proj_tm(self.wsl(1, 0, 8), 8, t, "B")
            self.ts("dve", self.wabs[:, t * 8:(t + 1) * 8], ps[:, 0:8], self.rstd[:, t:t + 1], None, ALU.mult, None,
                    [pt, self.Trstd], [self.Tww])
        self.cp("dve", self.wsgn, self.wabs, [self.Tww], [self.Tww])

    def mixer_c_attn(self, l):
        S, NT, NB = self.S, self.NT, self.NB
        T = self.T_const
        nbis = self.nbis
        idx = self.f32("xT", 0, 4 * S).rearrange("p (g s) -> p g s", g=4)
        ob = self.o["rbc"]
        negms = [self.A8[:, ob + k * 4 * S: ob + (k + 1) * 4 * S].rearrange("p (g s) -> p g s", g=4) for k in range(2)]
        Tnegs = [[Buf(f"neg{k}_{g}") for g in range(4)] for k in range(2)]
        Tidx = [Buf(f"idx{g}") for g in range(4)]
        Rp = [self.bf("W", i * 2048, 1024).rearrange("p (b n) -> p b n", b=2) for i in range(4)]
        TRp = [Buf(f"Rp{i}") for i in range(4)]
        Dg = [self.bf("wbf", 8192 + i * 256, 128) for i in range(8)]
        self.yTs = self.bf("wbf", 10240, 1024).rearrange("p (c s) -> p c s", c=2)
        TDg = [Buf(f"Dg{i}") for i in range(8)]
        sm = self.f32("st", 1344, 64)
        Tsm = Buf("smc")
        rmax, rmin, lo, step, mid, cnt, tmp, thr = (sm[:, 4 * i:4 * i + 4] for i in range(8))
        rec = sm[:, 32:36]
        nmid = sm[:, 36:40]
        c256 = sm[:, 40:44]
        Tth = [Buf(f"th{g}") for g in range(4)]
        nthr = sm[:, 52:56]
        Tnth = [Buf(f"nth{g}") for g in range(4)]
        den = sm[:, 44:48]
        mone = sm[:, 48:52]
        self.memset("dve", mone, -1.0, [T])
        pw2 = self.f32("st", 1900, nbis)
        for it in range(nbis):
            self.memset("dve", pw2[:, it:it + 1], 2.0 ** -(it + 1), [T])
        Trec = Buf("recc")
        ncp = 0
        s2all = self.f32("st", 1600, 4 * nbis)
        Ts2 = Buf("s2all")

        def index_and_threshold(I, hb):
            negm = negms[I % 2]
            Tneg = Tnegs[I % 2]
            live = [None]
            est = 0

            def tick(k):
                if live[0] is None:
                    return
                for _ in range(k):
                    try:
                        next(live[0])
                    except StopIteration:
                        live[0] = None
                        return

            if True:
                tiles = [2 * hb, 2 * hb + 1]
                groups = []
                for tt in tiles:
                    i = 4 * I + tt
                    L = 128 * (i + 1)
                    nkb = (L + 511) // 512
                    for kb in range(nkb):
                        for half in range(2):
                            groups.append((tt, i, L, kb, half, kb == nkb - 1))
                state = {}

                def stageA(g, par):
                    tt, i, L, kb, half, lastkb = g
                    ncol = min(512, L - kb * 512)
                    for pr_ in range(2):
                        psp, ptoks = self.bankpair("ia")
                        rp = par * 2 + pr_
                        for b2 in range(2):
                            hq = pr_ * 2 + b2
                            b_ = 32 * hq
                            kw = {"tile_position": (96, 0)} if b_ == 96 else {}
                            self.mm(psp[:, b2 * 512:b2 * 512 + ncol], self.QI3[b_:b_ + 32, half, i * 128:(i + 1) * 128],
                                    self.KX[b_:b_ + 32, kb * 512:kb * 512 + ncol], True, True,
                                    [self.TQI[I]] + [self.TKX[kb]], [ptoks[b2]], **kw)
                        self.act(Rp[rp][:, :, 0:ncol], psp.rearrange("p (b n) -> p b n", b=2)[:, :, 0:ncol], AF.Relu,
                                 list(ptoks), [TRp[rp]])

                def stageB(g, par):
                    tt, i, L, kb, half, lastkb = g
                    ncol = min(512, L - kb * 512)
                    if kb == 0 and half == 0:
                        for hi in range(8):
                            self.ts("pool", Dg[hi], self.identb, self.wsgn[:, i * 8 + hi:i * 8 + hi + 1], 1.0, ALU.mult, ALU.mult,
                                    [T, self.Tww], [TDg[hi]])
                    if half == 0:
                        state["psB"] = self.bank("ib")
                    psB, _, ptB = state["psB"]
                    for hq in range(4):
                        hi = half * 4 + hq
                        rp = par * 2 + hq // 2
                        self.mm(psB[:, 0:ncol], Dg[hi], Rp[rp][:, hq % 2, 0:ncol], hi == 0, hi == 7, [TDg[hi], TRp[rp]], [ptB])
                    if half == 1:
                        self.cp("act", idx[:, tt, kb * 512:kb * 512 + ncol], psB[:, 0:ncol], [ptB], [Tidx[tt]])
                        if lastkb:
                            self.memset("pool", idx[0:64, tt, i * 128 + 64:i * 128 + 128], -1e30, [Tidx[tt]])

                kstep = (est + 2 * len(groups) - 1) // (2 * len(groups)) if est else 0
                for gi in range(len(groups) + 1):
                    if gi < len(groups):
                        stageA(groups[gi], gi % 2)
                    if gi >= 1:
                        stageB(groups[gi - 1], (gi - 1) % 2)
                    tick(kstep)
                tts = [tt for tt in tiles if 4 * I + tt >= 2]
                for tt in tiles:
                    if tt not in tts:
                        self.memset("dve", thr[:, tt:tt + 1], -1e29, [Tth[tt]])
                if tts:
                    Ls = {tt: 128 * (4 * I + tt + 1) for tt in tts}
                    for tt in tts:
                        L = Ls[tt]
                        self.s.add("dve", lambda e, tt=tt, L=L: e.tensor_reduce(rmax[:, tt:tt + 1], idx[:, tt, 0:L], AX.X, ALU.max),
                                   [Tidx[tt]], [Tth[tt]])
                    for tt in tts:
                        L = Ls[tt]
                        self.s.add("dve", lambda e, tt=tt, L=L: e.tensor_reduce(mid[:, tt:tt + 1], idx[:, tt, 0:320], AX.X, ALU.min),
                                   [Tidx[tt]], [Tth[tt]])
                    for tt in tts:
                        self.tt("dve", rmax[:, tt:tt + 1], rmax[:, tt:tt + 1], mid[:, tt:tt + 1], ALU.subtract, [Tth[tt]], [Tth[tt]])
                    for tt in tts:
                        self.ts("dve", s2all[:, tt * nbis:(tt + 1) * nbis], pw2, rmax[:, tt:tt + 1], None, ALU.mult, None,
                                [Tth[tt], T], [Ts2])
                    for tt in tts:
                        self.stt(mid[:, tt:tt + 1], rmax[:, tt:tt + 1], 0.5, mid[:, tt:tt + 1], ALU.mult, ALU.add,
                                 [Tth[tt]], [Tth[tt]])
                    for it in range(nbis):
                        for tt in tts:
                            L = Ls[tt]
                            self.ts("dve", negm[:, tt, 0:L], idx[:, tt, 0:L], mid[:, tt:tt + 1], None, ALU.is_ge, ALU.add,
                                    [Tidx[tt], Tth[tt]], [Tneg[tt], Tth[tt]], accum_out=cnt[:, tt:tt + 1], saturate=False)
                        for tt in tts:
                            self.ts("dve", tmp[:, tt:tt + 1], cnt[:, tt:tt + 1], 256.0, 0.5, ALU.is_ge, ALU.subtract,
                                    [Tth[tt]], [Tth[tt]])
                        for tt in tts:
                            self.stt(mid[:, tt:tt + 1], tmp[:, tt:tt + 1], s2all[:, tt * nbis + it:tt * nbis + it + 1],
                                     mid[:, tt:tt + 1], ALU.mult, ALU.add, [Tth[tt], Ts2], [Tth[tt]])
                    for tt in tts:
                        self.tt("dve", thr[:, tt:tt + 1], mid[:, tt:tt + 1], s2all[:, (tt + 1) * nbis - 1:(tt + 1) * nbis],
                                ALU.subtract, [Tth[tt], Ts2], [Tth[tt]])
                for tt in tiles:
                    self.ts("dve", nthr[:, tt:tt + 1], thr[:, tt:tt + 1], -1.0, None, ALU.mult, None, [Tth[tt]], [Tnth[tt]])

                def maskgen(tiles=tiles, I=I, negm=negm, Tneg=Tneg):
                    for tt in tiles:
                        L = 128 * (4 * I + tt + 1)
                        self.act(negm[:, tt, 0:L], idx[:, tt, 0:L], AF.Sign, [Tidx[tt], Tnth[tt]], [Tneg[tt]],
                                 bias=nthr[:, tt:tt + 1], saturate=False)
                return maskgen

        def attention(I, hb):
            negm = negms[I % 2]
            Tneg = Tnegs[I % 2]
            if hb == 0:
                self.load_sg(2, I)
            items = []
            for h in (2 * hb, 2 * hb + 1):
                ch, base = h // 2, 0
                mi = 8 + h

                def pre(h=h):
                    self.ts("pool", qzc, self.QT3[:, h // 2, I * 512:(I + 1) * 512], self.rmask[:, 4 + h % 2:5 + h % 2], 1.0,
                            ALU.mult, ALU.mult, [self.TQT[I], T], [Tqzc])
                kfn = lambda j, ch=ch: (self.KT3[:, ch, j * 128:(j + 1) * 128], [self.TKT[j // 4]])
                qfn = lambda c0, c1, I=I: (qzc[:, c0 - I * 512:c1 - I * 512], [Tqzc])
                vfn = lambda j, h=h: (self.VT4[:, j, h, :], [self.TVT[j]])

                def extra(j, tt, mi=mi, I=I):
                    i = 4 * I + tt
                    r = [(negm[:, tt, j * 128:(j + 1) * 128], [Tneg[tt]], self.ident16k)]
                    if j == i:
                        r.append((self.bd[:, mi * 128:(mi + 1) * 128], [T]))
                    if j == i - 1:
                        r.append((self.bs[:, mi * 128:(mi + 1) * 128], [T]))
                    return r

                def make(kfn=kfn, qfn=qfn, vfn=vfn, extra=extra, base=base):
                    return self.attn_map_gen(I, [(kfn, qfn, base)], vfn, extra, -16384.0, spool="as", opool="ao", pairs=True)

                def post(O, Ot, h=h):
                    if h == 0:
                        self.flush_store()
                    Ov = O[:, 0:260].rearrange("p (t d) -> p t d", t=4)
                    self.act(den, Ov[:, :, 64], AF.Copy, [Ot], [Trec])
                    self.tt("pool", rec, den, mone, ALU.pow, [Trec, T], [Trec])
                    for tt in range(4):
                        self.act(self.ytm[:, tt, h * 64:(h + 1) * 64], Ov[:, tt, 0:64], AF.Copy, [Ot, Trec], [self.Tytm],
                                 scale=rec[:, tt:tt + 1])
                    self.tt("pool", self.ytm[:, :, h * 64:(h + 1) * 64], self.ytm[:, :, h * 64:(h + 1) * 64],
                            self.sg[:, :, h * 64:(h + 1) * 64], ALU.mult, [self.Tsg, self.Tytm], [self.Tytm])
                items.append((pre, make, post, 2 * I + 4))
            self.run_chain(items, 1)
            if hb == 1:
                self.deferred = (lambda I=I: self.store_y(2, I))

        self.trpool = "as"
        qzc = self.bf("X2", 0, 512)
        Tqzc = Buf("qzc")
        pending = None
        for I in range(NB + 1):
            for hb in range(2):
                mg = index_and_threshold(I, hb) if I < NB else None
                if pending is not None:
                    pending()
                pending = mg
                if I >= 1:
                    attention(I - 1, hb)
        self.flush_store()
        self.trpool = "A"

    def zero_y(self, rows):
        z = self.bf("W", 0, 512)
        Tz = Buf("z")
        self.memset("dve", z, 0.0, [Tz])
        for r in rows:
            for TB in range(self.NB):
                self.dma("pool", self.yT_d[r * 128:(r + 1) * 128, TB * 512:(TB + 1) * 512], z, [Tz],
                         [self.TyT[(r, TB)]])

    def record(self):
        S, NT, NB = self.S, self.NT, self.NB
        self.Tout = [Buf(f"out{t}") for t in range(NT)]
        Tx_in = [Buf(f"xin{t}") for t in range(NT)]
        Tx_res = [Buf(f"xres{t}") for t in range(NT)]
        self.setup()
        self.init_w()
        for l in range(self.depth):
            last = l == self.depth - 1
            xsrc = self.x_d if l == 0 else self.xres_d
            Tx = Tx_in if l == 0 else Tx_res
            self.TyT = {(r, TB): Buf(f"yT{r}_{TB}") for r in range(8) for TB in range(NB)}
            self.phase1(l, xsrc, Tx)
            self.s.barrier()
            for mi, m in enumerate("abcd"):
                if m not in self.mixers:
                    self.zero_y([2 * mi, 2 * mi + 1])
            self.init_attn()
            self.Tgate = {(mi, t): Buf(f"g{mi}_{t}") for mi in range(3) for t in range(NT)}
            if "d" in self.mixers:
                self.mixer_d(l)
                self.s.barrier()
            self.set_ones()
            if "b" in self.mixers:
                self.mixer_b(l)
                self.s.barrier()
            if "a" in self.mixers:
                self.mixer_a(l)
                self.s.barrier()
            if "c" in self.mixers:
                self.mixer_c_proj(l)
                self.s.barrier()
                self.mixer_c_attn(l)
            self.s.barrier()
            self.outproj(l, xsrc, Tx, self.Tout if last else Tx_res, last)
            self.s.barrier()


def build(S, depth=DEPTH, mixers="abcd", dbg=False, nbis=12):
    nc = bass.Bass("TRN2", target_bir_lowering=False)
    with ExitStack() as st:
        p = Prog(nc, st, S, depth, mixers, dbg, nbis)
        p.record()
        p.s.emit(st)
    return nc


def host_inputs(S, depth, norm_g, w_in, mla_qa_g, mla_w_uq, mla_kva_g, mla_w_ukv, diff_lambda,
                diff_subln_g, conv_w, w_out, rel_bias, final_g):
    f = np.float32
    c = _host_consts(S)
    pk = np.zeros((128, depth * PL + D), f)
    for l in range(depth):
        b = l * PL
        pk[:, b:b + 8] = np.asarray(norm_g[l], f).reshape(8, 128).T
        pk[:, b + 8:b + 10] = np.asarray(mla_qa_g[l], f).reshape(2, 128).T
        pk[:, b + 10] = np.asarray(mla_kva_g[l], f)
        pk[:, b + 11:b + 17] = np.asarray(conv_w[l], f).reshape(3, 2, 128).transpose(2, 1, 0).reshape(128, 6)
        pk[:, b + 17:b + 145] = np.asarray(diff_lambda[l], f).reshape(1, 128)
        pk[:, b + 145:b + 209] = np.asarray(diff_subln_g[l], f).reshape(1, 64)
    pk[:, depth * PL:] = np.asarray(final_g, f).reshape(1, D)
    cst = np.concatenate([c["ident_f"], c["ones_f"], c["cm"]], axis=1).astype(f)
    rmask = np.zeros((128, 8), f)
    for p_ in range(128):
        rmask[p_, p_ // 32] = 1.0
        rmask[p_, 4 + p_ // 64] = 1.0
    rb = np.asarray(rel_bias, f)
    bias = np.zeros((128, 2 * 12 * 128 + 12), f)
    for m in range(12):
        bias[:, m * 128:(m + 1) * 128] = rb[c["bk_diag"], m]
        bias[:, 1536 + m * 128:1536 + (m + 1) * 128] = rb[c["bk_sub"], m]
    bias[:, 3072:3084] = rb[15:16, :]
    shared = {
        "w_in": np.ascontiguousarray(np.asarray(w_in, f)[:depth]),
        "w_uq": np.ascontiguousarray(np.asarray(mla_w_uq, f)[:depth]),
        "w_ukv": np.ascontiguousarray(np.asarray(mla_w_ukv, f)[:depth]),
        "w_out": np.ascontiguousarray(np.asarray(w_out, f)[:depth]),
        "pk": pk, "cst": cst, "biasblk": bias, "cc4": c["cc4"], "ss4": c["ss4"], "rmask": rmask,
    }
    return shared


_NC_CACHE = {}


def kernel(x, norm_g, w_in, mla_qa_g, mla_w_uq, mla_kva_g, mla_w_ukv, diff_lambda,
           diff_subln_g, conv_w, w_out, rel_bias, final_g):
    x = np.asarray(x, np.float32)
    B, S, _ = x.shape
    shared = host_inputs(S, DEPTH, norm_g, w_in, mla_qa_g, mla_w_uq, mla_kva_g, mla_w_ukv, diff_lambda,
                         diff_subln_g, conv_w, w_out, rel_bias, final_g)
    key = (S, DEPTH)
    if key not in _NC_CACHE:
        _NC_CACHE[key] = build(S, DEPTH)
    nc = _NC_CACHE[key]
    in_maps = []
    for b in range(B):
        m = dict(shared)
        m["x"] = np.ascontiguousarray(x[b])
        in_maps.append(m)
    res = run_bass_kernel_spmd(nc, in_maps, core_ids=list(range(B)))
    return np.stack([np.asarray(r["out"], np.float32) for r in res.results], axis=0)
```

```python
import math
from contextlib import ExitStack

import numpy as np
import ml_dtypes

import concourse.bass as bass
import concourse.mybir as mybir
from concourse.bass_utils import run_bass_kernel_spmd

F32 = mybir.dt.float32
BF16 = mybir.dt.bfloat16
ALU = mybir.AluOpType
AF = mybir.ActivationFunctionType
AX = mybir.AxisListType

D = 1024
IN_COLS = 4040
DEPTH = 2
EPS = 1e-6
NEGM = -30000.0

OFF = {}
_o = 0
for _n, _w in (("a_cq", 256), ("a_ckv", 128), ("a_krope", 32), ("a_gate", 256),
               ("b_q", 256), ("b_k", 256), ("b_v", 256), ("b_gate", 256),
               ("c_q", 256), ("c_k", 256), ("c_v", 256),
               ("c_qidx", 256), ("c_kidx", 32), ("c_widx", 8), ("c_gate", 256),
               ("d_b", 256), ("d_c", 256), ("d_h", 256), ("d_gate", 256)):
    OFF[_n] = _o
    _o += _w
assert _o == IN_COLS


class Buf:
    __slots__ = ("name", "w", "r")

    def __init__(self, name=""):
        self.name = name
        self.w = None
        self.r = {}


class Op:
    __slots__ = ("eng", "fn", "deps", "inc", "pos", "val", "dma", "dsem", "dval", "dprev")


EPOCH = 30000
NDSEM = 8
ENGS = ("pe", "act", "dve", "pool", "sp")


class Sched:
    def __init__(self, nc):
        self.nc = nc
        self.ops = {e: [] for e in ENGS}

    def add(self, eng, fn, reads=(), writes=(), dma=False):
        op = Op()
        op.eng = eng
        op.fn = fn
        op.dma = dma
        op.inc = False
        op.val = 0
        lst = self.ops[eng]
        op.pos = len(lst)
        deps = {}
        for b in reads:
            if b.w is not None:
                deps[id(b.w)] = b.w
        for b in writes:
            if b.w is not None:
                deps[id(b.w)] = b.w
            for r in b.r.values():
                deps[id(r)] = r
        keep = []
        for d in deps.values():
            if not d.dma and d.eng == eng:
                if eng == "pe":
                    continue
                if op.pos - d.pos > 1:
                    continue
            keep.append(d)
            if not d.dma:
                d.inc = True
        op.deps = keep
        for b in reads:
            b.r[("d", id(op)) if dma else eng] = op
        for b in writes:
            b.w = op
            b.r = {}
        lst.append(op)
        return op

    def barrier(self):
        bar = {"pos": {e: len(self.ops[e]) for e in ENGS}, "snap": {}}
        for e in ENGS:
            for op in reversed(self.ops[e]):
                if not op.dma and op.fn is not None:
                    op.inc = True
                    break
        for e in ENGS:
            op = Op()
            op.eng = e
            op.fn = None
            op.dma = False
            op.inc = False
            op.val = 0
            op.pos = len(self.ops[e])
            op.deps = bar
            self.ops[e].append(op)

    def emit(self, stack):
        nc = self.nc
        engsems = {}
        for eng in ENGS:
            cnt = 0
            for op in self.ops[eng]:
                if op.dma:
                    continue
                if op.fn is None:
                    op.deps["snap"][eng] = cnt
                    continue
                if op.inc:
                    cnt += 1
                    op.val = cnt
            nep = (cnt + EPOCH - 1) // EPOCH + 1
            engsems[eng] = [stack.enter_context(nc.semaphore(f"s_{eng}_{i}")) for i in range(nep)]
        for eng in ENGS:
            dsems = None
            vals = [0] * NDSEM
            k = 0
            for op in self.ops[eng]:
                if op.fn is None:
                    op.deps["snap"]["d_" + eng] = (dsems, list(vals))
                    continue
                if not op.dma:
                    continue
                if dsems is None:
                    dsems = [stack.enter_context(nc.semaphore(f"d_{eng}_{i}")) for i in range(NDSEM)]
                s = k % NDSEM
                op.dsem = dsems[s]
                op.dprev = vals[s]
                vals[s] += 16
                op.dval = vals[s]
                k += 1

        def signal(d):
            if d.dma:
                return d.dsem, d.dval
            ep = (d.val - 1) // EPOCH
            return engsems[d.eng][ep], d.val - ep * EPOCH

        def run(e, eng):
            waited = {}
            for op in self.ops[eng]:
                if op.fn is None:
                    snap = op.deps["snap"]
                    for e2 in ENGS:
                        c = snap[e2]
                        if c > 0 and e2 != eng:
                            ep = (c - 1) // EPOCH
                            sem, val = engsems[e2][ep], c - ep * EPOCH
                            if waited.get(id(sem), 0) < val:
                                e.wait_ge(sem, val)
                                waited[id(sem)] = val
                        ds, vs = snap["d_" + e2]
                        if ds is not None:
                            for sem, val in zip(ds, vs):
                                if val > 0 and waited.get(id(sem), 0) < val:
                                    e.wait_ge(sem, val)
                                    waited[id(sem)] = val
                    continue
                waits = {}
                for d in op.deps:
                    sem, val = signal(d)
                    k = id(sem)
                    if k not in waits or waits[k][1] < val:
                        waits[k] = (sem, val)
                if op.dma and op.dprev > 0:
                    k = id(op.dsem)
                    if k not in waits or waits[k][1] < op.dprev:
                        waits[k] = (op.dsem, op.dprev)
                for k, (sem, val) in waits.items():
                    if waited.get(k, 0) < val:
                        e.wait_ge(sem, val)
                        waited[k] = val
                ins = op.fn(e)
                if op.dma:
                    ins.then_inc(op.dsem, 16)
                elif op.inc:
                    sem, _ = signal(op)
                    ins.then_inc(sem, 1)

        with nc.Block() as block:
            block.tensor(lambda e: run(e, "pe"))
            block.scalar(lambda e: run(e, "act"))
            block.vector(lambda e: run(e, "dve"))
            block.gpsimd(lambda e: run(e, "pool"))
            block.sync(lambda e: run(e, "sp"))


def _rel_bucket_np(rel):
    nb = 16
    max_exact = 8
    ret = np.where(rel > 0, nb, 0)
    n = np.abs(rel)
    nf = np.maximum(n, max_exact).astype(np.float32)
    large = max_exact + (np.log(nf / np.float32(max_exact)) / np.float32(math.log(128 / max_exact))
                         * np.float32(nb - max_exact)).astype(np.int32)
    large = np.minimum(large, nb - 1)
    return ret + np.where(n < max_exact, n, large)


def _host_consts(S):
    c = {}
    c["ident_f"] = np.eye(128, dtype=np.float32)
    c["ones_f"] = np.ones((128, 128), dtype=np.float32)
    q = np.arange(128)[:, None]
    k = np.arange(128)[None, :]
    cm = np.where((k // 64) <= (q // 64), 0.0, NEGM).astype(np.float32)
    c["cm"] = cm
    half = 16
    inv_freq = (np.float32(10000.0) ** (-np.arange(half, dtype=np.float32) / np.float32(half))).astype(np.float32)
    ang = np.arange(S, dtype=np.float32)[:, None] * inv_freq[None, :]
    cos = np.cos(ang).astype(np.float32).T
    sin = np.sin(ang).astype(np.float32).T
    cc = np.concatenate([cos, cos], 0)
    ss = np.concatenate([-sin, sin], 0)
    c["cc4"] = np.ascontiguousarray(np.tile(cc, (4, 1)))
    c["ss4"] = np.ascontiguousarray(np.tile(ss, (4, 1)))
    c["bk_diag"] = _rel_bucket_np(k - q)
    c["bk_sub"] = _rel_bucket_np(k - 128 - q)
    return c


PL = 8 + 2 + 1 + 6 + 128 + 64


class Prog:
    def __init__(self, nc, st, S, depth=DEPTH, mixers="abcd", dbg=False, nbis=12):
        self.nc = nc
        self.st = st
        self.S = S
        self.NT = S // 128
        self.NB = S // 512
        self.depth = depth
        self.mixers = mixers
        self.nbis = nbis
        self.s = Sched(nc)
        NT = self.NT
        dt = nc.dram_tensor
        self.x_d = dt("x", [S, D], F32, kind="ExternalInput").ap()
        self.win_d = dt("w_in", [depth, D, IN_COLS], F32, kind="ExternalInput").ap()
        self.wuq_d = dt("w_uq", [depth, 256, 384], F32, kind="ExternalInput").ap()
        self.wukv_d = dt("w_ukv", [depth, 128, 512], F32, kind="ExternalInput").ap()
        self.wout_d = dt("w_out", [depth, D, D], F32, kind="ExternalInput").ap()
        self.npk = depth * PL + D
        self.pk_d = dt("pk", [128, self.npk], F32, kind="ExternalInput").ap()
        self.cst_d = dt("cst", [128, 384], F32, kind="ExternalInput").ap()
        self.bias_d = dt("biasblk", [128, 2 * 12 * 128 + 12], F32, kind="ExternalInput").ap()
        self.rmask_d = dt("rmask", [128, 8], F32, kind="ExternalInput").ap()
        self.cc_d = dt("cc4", [128, S], F32, kind="ExternalInput").ap()
        self.ss_d = dt("ss4", [128, S], F32, kind="ExternalInput").ap()
        self.out_d = dt("out", [S, D], F32, kind="ExternalOutput").ap()
        self.xres_d = dt("xres", [S, D], F32).ap()
        self.yT_d = dt("yT", [D, S], BF16, kind="ExternalOutput" if dbg else "Internal").ap()
        self.gate_d = dt("gates", [3, S, 256], BF16).ap()
        o = {}
        off = 0

        def reg(name, nbytes):
            nonlocal off
            o[name] = off
            off += (nbytes + 31) // 32 * 32

        SR = max(S, 4096)
        NTR = SR // 128
        reg("P", 14336)
        reg("xT", 16 * SR)
        reg("rbc", 4 * SR)
        reg("wst", 2 * 4096)
        reg("wbf", 3 * 4096)
        reg("st", 4096)
        reg("QT", 4 * SR)
        reg("KT", 4 * SR)
        reg("VT", NTR * 4 * 65 * 2)
        reg("QI", 4 * SR + 32)
        reg("X1", 2 * SR)
        reg("W", 16384)
        reg("X2", 1024)
        self.o = o
        self.total = off
        self.A = st.enter_context(nc.sbuf_tensor("arena", [128, off // 4], F32))
        self.Ab = self.A.bitcast(BF16)
        self.A8 = self.A.bitcast(mybir.dt.float8e5)
        psh = [st.enter_context(nc.psum_tensor(f"ps{i}", [128, 1024], F32)) for i in range(4)]
        self.psp = [p[:, :] for p in psh]
        self.ps = [psh[i // 2][:, (i % 2) * 512:(i % 2 + 1) * 512] for i in range(8)]
        self.psb = [psh[i // 2].bitcast(BF16)[:, (i % 2) * 1024:(i % 2 + 1) * 1024] for i in range(8)]
        self.pstok = [Buf(f"ps{i}") for i in range(8)]
        self.rr = {}
        self.pools = {"A": [0, 1, 2, 3], "B": [4, 5, 6, 7], "as": [0, 1, 2, 3], "ao": [4, 5, 6, 7], "ia": [0, 1, 2, 3],
                      "ib": [4, 5, 6, 7]}
        self.trpool = "A"
        self.nPT = 4

    def f32(self, region, boff, n):
        b = (self.o[region] + boff) // 4
        return self.A[:, b:b + n]

    def bf(self, region, boff, n):
        b = (self.o[region] + boff) // 2
        return self.Ab[:, b:b + n]

    def bank(self, name):
        lst = self.pools[name]
        name = "A" if lst[0] == 0 else "B"
        k = self.rr.get(name, 0)
        self.rr[name] = k + 1
        i = lst[k % len(lst)]
        return self.ps[i], self.psb[i], self.pstok[i]

    def bankpair(self, name):
        lst = self.pools[name]
        name = "A" if lst[0] == 0 else "B"
        k = self.rr.get(name, 0)
        if k % 2:
            k += 1
        self.rr[name] = k + 2
        i = lst[k % len(lst)]
        return self.psp[i // 2], [self.pstok[i], self.pstok[i + 1]]

    def bankA(self):
        return self.bank("A")

    def bankB(self):
        return self.bank("B")

    def mm(self, out, lhsT, rhs, start, stop, R, Wt, **kw):
        self.s.add("pe", lambda e: e.matmul(out, lhsT, rhs, start=start, stop=stop, **kw), R, Wt)

    def tr(self, out, in_, ident, R, Wt):
        self.s.add("pe", lambda e: e.transpose(out, in_, ident), R, Wt)

    def act(self, out, in_, func, R, Wt, **kw):
        self.s.add("act", lambda e: e.activation(out, in_, func, **kw), R, Wt)

    def ts(self, eng, out, in0, s1, s2, op0, op1, R, Wt, **kw):
        if op1 is None:
            self.s.add(eng, lambda e: e.tensor_scalar(out, in0, s1, None, op0, **kw), R, Wt)
        else:
            self.s.add(eng, lambda e: e.tensor_scalar(out, in0, s1, s2, op0, op1, **kw), R, Wt)

    def tt(self, eng, out, in0, in1, op, R, Wt):
        self.s.add(eng, lambda e: e.tensor_tensor(out, in0, in1, op), R, Wt)

    def stt(self, out, in0, scalar, in1, op0, op1, R, Wt):
        self.s.add("dve", lambda e: e.scalar_tensor_tensor(out, in0, scalar, in1, op0, op1), R, Wt)

    def cp(self, eng, out, in_, R, Wt):
        if eng == "act":
            self.s.add("act", lambda e: e.activation(out, in_, AF.Copy), R, Wt)
        else:
            self.s.add(eng, lambda e: e.tensor_copy(out, in_), R, Wt)

    def memset(self, eng, ap, val, Wt):
        self.s.add(eng, lambda e: e.memset(ap, val), (), Wt)

    def dma(self, q, out, in_, R, Wt):
        self.s.add(q, lambda e: e.dma_start(out, in_), R, Wt, dma=True)


    def setup(self):
        S, NT = self.S, self.NT
        self.cst = self.f32("P", 0, 384)
        self.ident_f = self.cst[:, 0:128]
        self.ones_f = self.cst[:, 128:256]
        self.cm_f = self.cst[:, 256:384]
        self.identb = self.bf("P", 1536, 128)
        self.cmb = self.bf("P", 1792, 128)
        self.bd = self.bf("P", 2048, 12 * 128)
        self.bs = self.bf("P", 5120, 12 * 128)
        self.pk = self.f32("P", 8192, self.npk)
        self.cfar = self.f32("P", 8192 + 4 * self.npk, 12)
        self.T_const = Buf("const")
        T = self.T_const
        stg = self.f32("xT", 0, 3084)
        Tstg = Buf("stg")
        self.dma("sp", self.cst, self.cst_d, (), [T])
        self.dma("sp", self.pk, self.pk_d, (), [T])
        self.rmask = self.f32("P", 14016, 8)
        self.dma("sp", self.rmask, self.rmask_d, (), [T])
        self.dma("sp", stg, self.bias_d, (), [Tstg])
        self.cp("dve", self.identb, self.ident_f, [T], [T])
        self.cp("dve", self.cmb, self.cm_f, [T], [T])
        self.cp("dve", self.cfar, stg[:, 3072:3084], [Tstg], [T])
        for m in range(12):
            self.stt(self.bd[:, m * 128:(m + 1) * 128], stg[:, m * 128:(m + 1) * 128], self.cfar[:, m:m + 1],
                     self.cm_f, ALU.subtract, ALU.add, [Tstg, T], [T])
            self.ts("dve", self.bs[:, m * 128:(m + 1) * 128], stg[:, 1536 + m * 128:1536 + (m + 1) * 128],
                    self.cfar[:, m:m + 1], None, ALU.subtract, None, [Tstg, T], [T])
        self.ssq = self.f32("st", 0, NT)
        self.var = self.f32("st", 128, NT)
        self.rstd = self.f32("st", 256, NT)
        self.mhalf = self.f32("st", 384, NT)
        self.zeros_b = self.bf("st", 512, 260)
        self.mhalfw = self.f32("st", 1080, 2 * NT)
        self.memset("dve", self.mhalf, -0.5, [T])
        self.memset("dve", self.mhalfw, -0.5, [T])
        self.memset("dve", self.zeros_b, 0.0, [T])
        self.s.barrier()

    def pkl(self, l, off, n):
        return self.pk[:, l * PL + off: l * PL + off + n]

    def phase1(self, l, xsrc, Tx):
        S, NT, NB = self.S, self.NT, self.NB
        T = self.T_const
        xt = [self.f32("W", 0, 1024), self.f32("W", 4096, 1024)]
        Txt = [Buf("xt0"), Buf("xt1")]
        junk = self.bf("W", 8192, 1024)
        Tjunk = Buf("junk")
        diag = [self.f32("W", 10240 + i * 512, 128) for i in range(4)]
        Tdiag = [Buf(f"diag{i}") for i in range(4)]
        xTb = self.bf("xT", 0, 8 * S)
        self.xT = xTb
        self.TxT = [Buf(f"xT{t}") for t in range(NT)]
        Tssq = Buf("ssq")
        for t in range(NT):
            b = t % 2
            self.dma("sp", xt[b], xsrc[t * 128:(t + 1) * 128, :], [Tx[t]], [Txt[b]])
            self.act(junk, xt[b], AF.Square, [Txt[b]], [Tjunk, Tssq], accum_out=self.ssq[:, t:t + 1])
            for hb in range(2):
                ps, _, pt = self.bankA()
                for c4 in range(4):
                    c = hb * 4 + c4
                    self.tr(ps[:, c4 * 128:(c4 + 1) * 128], xt[b][:, c * 128:(c + 1) * 128], self.ident_f,
                            [Txt[b], T], [pt])
                dst = xTb.rearrange("p (c s) -> p c s", c=8)[:, hb * 4:(hb + 1) * 4, t * 128:(t + 1) * 128]
                self.cp("dve" if hb == 0 else "act", dst, ps.rearrange("p (c s) -> p c s", c=4), [pt], [self.TxT[t]])
        Trs = Buf("rstd")
        self.ts("dve", self.var, self.ssq, 1.0 / D, EPS, ALU.mult, ALU.add, [Tssq], [Trs])
        self.tt("pool", self.rstd, self.var, self.mhalf, ALU.pow, [Trs, T], [Trs])
        self.Trstd = Trs
        self.rbc = self.f32("rbc", 0, S)
        self.Trbc = [Buf(f"rbc{i}") for i in range(NB)]
        for TB in range(NB):
            self.bcast(self.rstd, TB, self.rbc[:, TB * 512:(TB + 1) * 512], [Trs], [self.Trbc[TB]], diag, Tdiag)

    def bcast(self, vec, TB, dst, R, Wt, diag, Tdiag):
        T = self.T_const
        ps, _, pt = self.bankA()
        for tt in range(4):
            t = TB * 4 + tt
            self.ts("dve", diag[tt], self.ident_f, vec[:, t:t + 1], None, ALU.mult, None, R + [T], [Tdiag[tt]])
            self.mm(ps[:, tt * 128:(tt + 1) * 128], self.ones_f, diag[tt], True, True, [Tdiag[tt], T], [pt])
        self.cp("act", dst, ps, [pt], Wt)

    def init_w(self):
        self.wst = [self.f32("wst", 0, 1024), self.f32("wst", 4096, 1024)]
        self.Twst = [Buf("wst0"), Buf("wst1")]
        self.wbf = [self.bf("wbf", i * 4096, 2048).rearrange("p (c n) -> p c n", c=8) for i in range(3)]
        self.Twbf = [Buf(f"wbf{i}") for i in range(3)]
        self.nw = 0

    def load_w(self, l, c0, n, slot, col, scale=1.0, src=None, gain=None, stg_view=None):
        k = self.nw % 2
        self.nw += 1
        T = self.T_const
        stg = self.wst[k].rearrange("p (c n) -> p c n", c=8)
        if src is None:
            src = self.win_d[l, :, c0:c0 + n].rearrange("(c p) n -> p c n", p=128)
        if isinstance(src, list):
            for dv, sv in src:
                self.dma("sp", dv, sv, (), [self.Twst[k]])
        else:
            self.dma("sp", stg[:, :, 0:n], src, (), [self.Twst[k]])
        for c in range(8):
            g = self.pkl(l, c, 1) if gain is None else gain(c)
            self.ts("pool", self.wbf[slot][:, c, col:col + n], stg[:, c, 0:n], g, float(scale),
                    ALU.mult, ALU.mult, [self.Twst[k], T], [self.Twbf[slot]])

    def proj_fm(self, lhs, M, TB, pool="A", ncol=512, c0=0):
        ps, psb, pt = self.bankA() if pool == "A" else self.bankB()
        xT3 = self.xT.rearrange("p (c s) -> p c s", c=8)
        R = [self.TxT[TB * 4 + i] for i in range(4)]
        for c in range(8):
            ap, toks = lhs(c)
            self.mm(ps[0:M, 0:ncol], ap, xT3[:, c, TB * 512 + c0:TB * 512 + c0 + ncol], c == 0, c == 7,
                    R + toks, [pt])
        return ps, pt

    def proj_tm(self, rhs, N, t, pool="A"):
        ps, psb, pt = self.bankA() if pool == "A" else self.bankB()
        xT3 = self.xT.rearrange("p (c s) -> p c s", c=8)
        for c in range(8):
            ap, toks = rhs(c)
            self.mm(ps[:, 0:N], xT3[:, c, t * 128:(t + 1) * 128], ap, c == 0, c == 7,
                    [self.TxT[t]] + toks, [pt])
        return ps, pt

    def wsl(self, slot, col, n):
        return lambda c: (self.wbf[slot][:, c, col:col + n], [self.Twbf[slot]])

    def mixer_d(self, l):
        S, NB = self.S, self.NB
        T = self.T_const
        u = self.f32("QI", 0, S + 2)
        Tu = Buf("u")
        self.memset("dve", u[:, 0:2], 0.0, [Tu])
        t1 = self.f32("W", 0, 512)
        t2 = self.f32("W", 2048, 512)
        acc = self.f32("W", 4096, 512)
        yo = [self.bf("W", 6144, 512), self.bf("W", 7168, 512)]
        Tt1, Tt2, Tacc = Buf("t1"), Buf("t2"), Buf("acc")
        Tyo = [Buf("yo0"), Buf("yo1")]
        k = 0
        for fc in range(2):
            for gi, name in enumerate(("d_b", "d_c", "d_h", "d_gate")):
                self.load_w(l, OFF[name] + fc * 128, 128, gi // 2, (gi % 2) * 128)
            cw = self.pkl(l, 11 + fc * 3, 3)
            for TB in range(NB):
                pool = "A" if TB % 2 == 0 else "B"
                psb_, ptb = self.proj_fm(self.wsl(0, 0, 128), 128, TB, pool)
                psc, ptc = self.proj_fm(self.wsl(0, 128, 128), 128, TB, pool)
                psh, pth = self.proj_fm(self.wsl(1, 0, 128), 128, TB, pool)
                psg, ptg = self.proj_fm(self.wsl(1, 128, 128), 128, TB, pool)
                rb = self.rbc[:, TB * 512:(TB + 1) * 512]
                Trb = [self.Trbc[TB]]
                ub = u[:, 2 + TB * 512: 2 + (TB + 1) * 512]
                self.tt("dve", t1, psc, rb, ALU.mult, [ptc] + Trb, [Tt1])
                self.tt("dve", t2, psh, rb, ALU.mult, [pth] + Trb, [Tt2])
                self.tt("dve", ub, t1, t2, ALU.mult, [Tt1, Tt2], [Tu])
                self.ts("dve", acc, u[:, TB * 512: TB * 512 + 512], cw[:, 0:1], None, ALU.mult, None,
                        [Tu, T], [Tacc])
                self.stt(acc, u[:, TB * 512 + 1: TB * 512 + 513], cw[:, 1:2], acc, ALU.mult, ALU.add,
                         [Tu, T, Tacc], [Tacc])
                self.stt(acc, ub, cw[:, 2:3], acc, ALU.mult, ALU.add, [Tu, T, Tacc], [Tacc])
                self.tt("dve", t1, psb_, rb, ALU.mult, [ptb] + Trb, [Tt1])
                self.tt("dve", acc, acc, t1, ALU.mult, [Tacc, Tt1], [Tacc])
                self.tt("dve", t2, psg, rb, ALU.mult, [ptg] + Trb, [Tt2])
                self.act(t1, t2, AF.Silu, [Tt2], [Tt1])
                y = yo[k % 2]
                Ty = Tyo[k % 2]
                k += 1
                self.tt("dve", y, acc, t1, ALU.mult, [Tacc, Tt1], [Ty])
                r0 = 768 + fc * 128
                self.dma("pool", self.yT_d[r0:r0 + 128, TB * 512:(TB + 1) * 512], y, [Ty],
                         [self.TyT[(r0 // 128, TB)]])

    def outproj(self, l, xsrc, Tx, Tdst, last):
        S, NT, NB = self.S, self.NT, self.NB
        T = self.T_const
        wo = self.bf("QT", 0, 8 * 1024).rearrange("p (c n) -> p c n", c=8)
        Two = Buf("wo")
        for j in range(8):
            k = self.nw % 2
            self.nw += 1
            stg = self.wst[k].rearrange("p (c n) -> p c n", c=8)
            self.dma("sp", stg, self.wout_d[l, :, j * 128:(j + 1) * 128].rearrange("(c p) n -> p c n", p=128),
                     (), [self.Twst[k]])
            self.cp("pool" if j % 2 else "dve", wo[:, :, j * 128:(j + 1) * 128], stg, [self.Twst[k]], [Two])
        base = 16384
        yTb = [self.bf("QT", base + i * 8192, 4096).rearrange("p (c s) -> p c s", c=8) for i in range(2)]
        TyTb = [Buf("yTb0"), Buf("yTb1")]
        xt = [self.f32("QT", base + 16384 + i * 4096, 1024) for i in range(2)]
        Txt = [Buf("oxt0"), Buf("oxt1")]
        xn = [self.f32("QT", base + 24576 + i * 4096, 1024) for i in range(2)]
        Txn = [Buf("xn0"), Buf("xn1")]
        junk = self.bf("W", 8192, 1024)
        Tjunk = Buf("junk2")
        st4 = self.f32("st", 1040, 8)
        Tst4 = Buf("st4")
        fg = self.pk[:, self.depth * PL: self.depth * PL + D]
        dst = self.xres_d if not last else self.out_d
        for TB in range(NB):
            yb = yTb[TB % 2]
            Ty = TyTb[TB % 2]
            self.dma("sp", yb, self.yT_d[:, TB * 512:(TB + 1) * 512].rearrange("(c p) s -> p c s", p=128),
                     [self.TyT[(r, TB)] for r in range(8)], [Ty])
            for tt_ in range(4):
                t = TB * 4 + tt_
                b = t % 2
                self.dma("sp", xt[b], xsrc[t * 128:(t + 1) * 128, :], [Tx[t]], [Txt[b]])
                for half in range(2):
                    ps, _, pt = self.bankA()
                    for c in range(8):
                        self.mm(ps, yb[:, c, tt_ * 128:(tt_ + 1) * 128], wo[:, c, half * 512:(half + 1) * 512],
                                c == 0, c == 7, [Ty, Two], [pt])
                    self.tt("dve", xn[b][:, half * 512:(half + 1) * 512], ps, xt[b][:, half * 512:(half + 1) * 512],
                            ALU.add, [pt, Txt[b]], [Txn[b]])
                if not last:
                    self.dma("pool", dst[t * 128:(t + 1) * 128, :], xn[b], [Txn[b]], [Tdst[t]])
                else:
                    c0 = (t % 2) * 4
                    self.act(junk, xn[b], AF.Square, [Txn[b]], [Tjunk, Tst4], accum_out=st4[:, c0:c0 + 1])
                    self.ts("dve", st4[:, c0 + 1:c0 + 2], st4[:, c0:c0 + 1], 1.0 / D, EPS, ALU.mult, ALU.add,
                            [Tst4], [Tst4])
                    self.tt("pool", st4[:, c0 + 2:c0 + 3], st4[:, c0 + 1:c0 + 2], self.mhalf[:, 0:1], ALU.pow,
                            [Tst4, T], [Tst4])
                    self.stt(xn[b], xn[b], st4[:, c0 + 2:c0 + 3], fg, ALU.mult, ALU.mult, [Txn[b], Tst4, T], [Txn[b]])
                    self.dma("pool", dst[t * 128:(t + 1) * 128, :], xn[b], [Txn[b]], [Tdst[t]])

    def init_attn(self):
        S, NT = self.S, self.NT
        self.QT3 = self.bf("QT", 0, 2 * S).rearrange("p (c s) -> p c s", c=2)
        self.KT3 = self.bf("KT", 0, 2 * S).rearrange("p (c s) -> p c s", c=2)
        self.VT4 = self.bf("VT", 0, NT * 260).rearrange("p (t h d) -> p t h d", t=NT, h=4)
        self.TQT = [Buf(f"QT{i}") for i in range(self.NB)]
        self.TKT = [Buf(f"KT{i}") for i in range(self.NB)]
        self.TVT = [Buf(f"VT{t}") for t in range(NT)]
        self.PT = [self.bf("W", 8192 + i * 1024, 512) for i in range(4)]
        self.TPT = [Buf(f"PT{i}") for i in range(4)]
        self.npt = 0
        self.nPT = 4
        self.ytm = self.bf("W", 12288, 1024).rearrange("p (t f) -> p t f", t=4)
        self.Tytm = Buf("ytm")
        self.sg = self.bf("W", 14336, 1024).rearrange("p (t f) -> p t f", t=4)
        self.Tsg = Buf("sg")
        self.yTs = self.bf("W", 6144, 1024).rearrange("p (c s) -> p c s", c=2)
        self.TyTs = Buf("yTs")

    def set_ones(self):
        for t in range(self.NT):
            self.memset("pool", self.VT4[:, t, :, 64:65], 1.0, [self.TVT[t]])

    def attn_map(self, I, qk, vfn, extra, cbias, pairs=True):
        g = self.attn_map_gen(I, qk, vfn, extra, cbias, pairs=pairs)
        while True:
            try:
                next(g)
            except StopIteration as e:
                return e.value

    def attn_map_gen(self, I, qk, vfn, extra, cbias, spool="A", opool="B", LA=2, pairs=False):
        T = self.T_const
        O, _, Ot = self.bank(opool)
        self.mm(O[:, 0:260], self.identb, self.zeros_b, True, True, [T], [Ot])
        if pairs:
            units = [(2 * p, 2 * p + 1) for p in range(2 * I)] + [(j,) for j in range(4 * I, 4 * I + 4)]
            LA = 1
        else:
            units = [(j,) for j in range(4 * I + 4)]
        pend = {}
        for s_ in range(len(units) + LA):
            if s_ > 0:
                yield
            if s_ < len(units):
                u = units[s_]
                if pairs:
                    psp, ptoks = self.bankpair(spool)
                    pi = self.npt % 2
                    P = self.bf("W", 8192 + pi * 2048, 1024)
                    Pt = self.TPT[pi]
                else:
                    ps1, _, pt1 = self.bank(spool)
                    psp, ptoks = ps1, [pt1]
                    pi = self.npt % self.nPT
                    P, Pt = self.PT[pi], self.TPT[pi]
                self.npt += 1
                info = []
                for k_, j in enumerate(u):
                    r0 = max(0, j - 4 * I)
                    a = r0 * 128
                    ps = psp[:, k_ * 512:(k_ + 1) * 512]
                    pt = ptoks[k_]
                    adds = []
                    for tt in range(r0, 4):
                        for (ap, toks) in extra(j, tt):
                            adds.append((tt, ap, toks))
                    n = len(qk)
                    for k, (kfn, qfn, base) in enumerate(qk):
                        kap, ktok = kfn(j)
                        qap, qtok = qfn(I * 512 + a, (I + 1) * 512)
                        kw = {"tile_position": (96, 0)} if base == 96 else {}
                        self.mm(ps[:, a:512], kap, qap, k == 0, (k == n - 1) and not adds, ktok + qtok, [pt], **kw)
                    for k, (tt, ap, toks) in enumerate(adds):
                        self.mm(ps[:, tt * 128:(tt + 1) * 128], ap, self.identb, False, k == len(adds) - 1,
                                toks + [T], [pt], skip_group_check=True)
                    info.append((j, r0, k_))
                a0 = info[0][1] * 128 if len(u) == 1 else 0
                w = 512 * len(u)
                if cbias is None:
                    self.act(P[:, a0:w], psp[:, a0:w], AF.Exp, list(ptoks), [Pt])
                else:
                    self.act(P[:, a0:w], psp[:, a0:w], AF.Exp, list(ptoks) + [T], [Pt], bias=cbias)
                pend[s_] = (P, Pt, info)
            sp = s_ - LA
            if sp >= 0:
                P, Pt, info = pend.pop(sp)
                for (j, r0, k_) in info:
                    vap, vtok = vfn(j)
                    for tt in range(r0, 4):
                        self.mm(O[:, tt * 65:(tt + 1) * 65], P[:, k_ * 512 + tt * 128:k_ * 512 + (tt + 1) * 128], vap,
                                False, j == 4 * I + tt, [Pt] + vtok, [Ot], skip_group_check=True)
        return O, Ot

    def run_chain(self, items, LA):
        prev = None
        for (pre, make, post, nunits) in items:
            if pre is not None:
                pre()
            g = make()
            res = None
            alive = True
            for _ in range(LA):
                try:
                    next(g)
                except StopIteration as e:
                    res = e.value
                    alive = False
                    break
            if prev is not None:
                self._finish(prev)
            for _ in range(nunits - LA):
                if not alive:
                    break
                try:
                    next(g)
                except StopIteration as e:
                    res = e.value
                    alive = False
            prev = (g, post, alive, res)
        if prev is not None:
            self._finish(prev)

    def flush_store(self):
        if getattr(self, "deferred", None) is not None:
            f = self.deferred
            self.deferred = None
            f()

    def _finish(self, p):
        g, post, alive, res = p
        while alive:
            try:
                next(g)
            except StopIteration as e:
                res = e.value
                alive = False
        post(*res)

    def load_sg(self, mi, I):
        self.dma("sp", self.sg, self.gate_d[mi, I * 512:(I + 1) * 512, :].rearrange("(t p) f -> p t f", p=128),
                 [self.Tgate[(mi, I * 4 + tt)] for tt in range(4)], [self.Tsg])

    def store_y(self, mi, I, r0=None, nfc=2):
        T = self.T_const
        if r0 is None:
            r0 = mi * 2
        for fc in range(nfc):
            ps, psb, pt = self.bank(self.trpool)
            for tt in range(4):
                self.tr(psb[:, tt * 128:(tt + 1) * 128], self.ytm[:, tt, fc * 128:(fc + 1) * 128], self.identb,
                        [self.Tytm, T], [pt])
            self.cp("act" if self.trpool == "as" else "dve", self.yTs[:, fc, :], psb[:, 0:512], [pt], [self.TyTs])
        self.dma("pool", self.yT_d[r0 * 128:(r0 + nfc) * 128, I * 512:(I + 1) * 512].rearrange("(c p) s -> p c s", p=128),
                 self.yTs[:, 0:nfc, :], [self.TyTs], [self.TyT[(r0 + i, I)] for i in range(nfc)])

    def proj_gate(self, l, mi, name):
        for hf in range(2):
            self.load_w(l, OFF[name] + hf * 128, 128, 2, hf * 128)
        gt = [self.bf("W", 15360, 256), self.bf("W", 15872, 256)]
        Tg = [Buf("gt0"), Buf("gt1")]
        for t in range(self.NT):
            ps, pt = self.proj_tm(self.wsl(2, 0, 256), 256, t, "B")
            b = t % 2
            self.act(gt[b], ps[:, 0:256], AF.Silu, [pt, self.Trstd], [Tg[b]], scale=self.rstd[:, t:t + 1])
            self.dma("pool", self.gate_d[mi, t * 128:(t + 1) * 128, :], gt[b], [Tg[b]], [self.Tgate[(mi, t)]])

    def proj_fm_to(self, l, name, ncols, dst3, Tdst, scale=1.0):
        for ch in range(ncols // 128):
            slot = ch % 2
            self.load_w(l, OFF[name] + ch * 128, 128, slot, 0, scale)
            for TB in range(self.NB):
                ps, pt = self.proj_fm(self.wsl(slot, 0, 128), 128, TB, "A")
                self.tt("dve", dst3[:, ch, TB * 512:(TB + 1) * 512], ps, self.rbc[:, TB * 512:(TB + 1) * 512],
                        ALU.mult, [pt, self.Trbc[TB]], [Tdst[TB]])

    def proj_v(self, l, name):
        for hf in range(2):
            self.load_w(l, OFF[name] + hf * 128, 128, 2, hf * 128)
        for t in range(self.NT):
            ps, pt = self.proj_tm(self.wsl(2, 0, 256), 256, t, "B")
            self.act(self.VT4[:, t, :, 0:64], ps[:, 0:256].rearrange("p (h d) -> p h d", h=4), AF.Copy,
                     [pt, self.Trstd], [self.TVT[t]], scale=self.rstd[:, t:t + 1])

    def mixer_b(self, l):
        S, NT, NB = self.S, self.NT, self.NB
        T = self.T_const
        lam_init = 0.8 - 0.6 * math.exp(-0.3 * l)
        self.proj_fm_to(l, "b_q", 256, self.QT3, self.TQT, scale=32 ** -0.5)
        self.proj_fm_to(l, "b_k", 256, self.KT3, self.TKT)
        self.proj_v(l, "b_v")
        self.proj_gate(l, 1, "b_gate")
        self.s.barrier()
        sm = self.f32("st", 1344, 32)
        Tsm = Buf("sm")
        lp = self.pkl(l, 17, 128)
        pr = self.f32("st", 1500, 64)
        self.tt("dve", pr[:, 0:32], lp[:, 0:32], lp[:, 32:64], ALU.mult, [T], [Tsm])
        self.tt("dve", pr[:, 32:64], lp[:, 64:96], lp[:, 96:128], ALU.mult, [T], [Tsm])
        self.s.add("dve", lambda e: e.tensor_reduce(sm[:, 0:2], pr.rearrange("p (a b) -> p a b", a=2), AX.X, ALU.add),
                   [Tsm], [Tsm])
        self.act(sm[:, 2:4], sm[:, 0:2], AF.Exp, [Tsm], [Tsm])
        self.tt("dve", sm[:, 4:5], sm[:, 2:3], sm[:, 3:4], ALU.subtract, [Tsm], [Tsm])
        self.ts("dve", sm[:, 5:6], sm[:, 4:5], -1.0, -lam_init, ALU.mult, ALU.add, [Tsm], [Tsm])
        neglam = sm[:, 5:6]
        gsub = self.pkl(l, 145, 64)
        rec = sm[:, 8:16]
        o_h = self.f32("W", 0, 256).rearrange("p (t d) -> p t d", t=4)
        sq = self.f32("W", 1024, 256).rearrange("p (t d) -> p t d", t=4)
        gsg = self.f32("W", 2048, 256).rearrange("p (t d) -> p t d", t=4)
        Toh, Tsq, Tgsg = Buf("oh"), Buf("sq"), Buf("gsg")
        cvar = 1.0 / (1.0 - lam_init) ** 2
        Qz = [self.bf("W", 3072, 512), self.bf("W", 4096, 512), self.bf("W", 5120, 512)]
        TQz = [Buf("qz0"), Buf("qz1"), Buf("qz2")]
        nqz = [0]
        for I in range(NB):
            self.load_sg(1, I)
            items = []
            Ores = {}
            for h in range(4):
                for m in range(2):
                    ch, base = h // 2, 0
                    mi = 2 * h + m
                    m4 = (h % 2) * 2 + m
                    qz, Tqz = Qz[nqz[0] % 3], TQz[nqz[0] % 3]
                    nqz[0] += 1

                    def pre(qz=qz, Tqz=Tqz, ch=ch, m4=m4, I=I):
                        self.ts("dve", qz, self.QT3[:, ch, I * 512:(I + 1) * 512], self.rmask[:, m4:m4 + 1], None,
                                ALU.mult, None, [self.TQT[I], T], [Tqz])
                    kfn = lambda j, ch=ch: (self.KT3[:, ch, j * 128:(j + 1) * 128], [self.TKT[j // 4]])
                    qfn = lambda c0, c1, qz=qz, Tqz=Tqz, I=I: (qz[:, c0 - I * 512:c1 - I * 512], [Tqz])
                    vfn = lambda j, h=h: (self.VT4[:, j, h, :], [self.TVT[j]])

                    def extra(j, tt, mi=mi, I=I):
                        i = 4 * I + tt
                        if j == i:
                            return [(self.bd[:, mi * 128:(mi + 1) * 128], [T])]
                        if j == i - 1:
                            return [(self.bs[:, mi * 128:(mi + 1) * 128], [T])]
                        return []

                    def make(kfn=kfn, qfn=qfn, vfn=vfn, extra=extra, base=base, I=I):
                        return self.attn_map_gen(I, [(kfn, qfn, base)], vfn, extra, None, pairs=True)

                    def post(O, Ot, h=h, m=m, I=I):
                        Ores[(h, m)] = (O, Ot)
                        if m == 0:
                            if h == 0:
                                self.flush_store()
                            return
                        (O0, T0), (O1, T1) = Ores[(h, 0)], Ores[(h, 1)]
                        O0v = O0[:, 0:260].rearrange("p (t d) -> p t d", t=4)
                        O1v = O1[:, 0:260].rearrange("p (t d) -> p t d", t=4)
                        self.s.add("dve", lambda e, O0v=O0v: e.reciprocal(rec[:, 0:4], O0v[:, :, 64]), [T0], [Tsm])
                        self.s.add("dve", lambda e, O1v=O1v: e.reciprocal(rec[:, 4:8], O1v[:, :, 64]), [T1], [Tsm])
                        self.ts("dve", rec[:, 4:8], rec[:, 4:8], neglam, None, ALU.mult, None, [Tsm], [Tsm])
                        for tt in range(4):
                            self.ts("dve", o_h[:, tt, :], O0v[:, tt, 0:64], rec[:, tt:tt + 1], None, ALU.mult, None,
                                    [T0, Tsm], [Toh])
                        for tt in range(4):
                            self.stt(o_h[:, tt, :], O1v[:, tt, 0:64], rec[:, 4 + tt:5 + tt], o_h[:, tt, :], ALU.mult, ALU.add,
                                     [T1, Tsm, Toh], [Toh])
                        self.tt("dve", sq, o_h, o_h, ALU.mult, [Toh], [Tsq])
                        self.s.add("dve", lambda e: e.tensor_reduce(sm[:, 16:20], sq, AX.X, ALU.add), [Tsq], [Tsm])
                        self.ts("dve", sm[:, 20:24], sm[:, 16:20], cvar / 64.0, EPS * cvar, ALU.mult, ALU.add, [Tsm], [Tsm])
                        self.tt("pool", sm[:, 24:28], sm[:, 20:24], self.mhalf[:, 0:4], ALU.pow, [Tsm, T], [Tsm])
                        for tt in range(4):
                            self.tt("dve", gsg[:, tt, :], self.sg[:, tt, h * 64:(h + 1) * 64], gsub, ALU.mult,
                                    [self.Tsg, T], [Tgsg])
                        for tt in range(4):
                            self.stt(self.ytm[:, tt, h * 64:(h + 1) * 64], o_h[:, tt, :], sm[:, 24 + tt:25 + tt], gsg[:, tt, :],
                                     ALU.mult, ALU.mult, [Toh, Tsm, Tgsg], [self.Tytm])
                    items.append((pre, make, post, 2 * I + 4))
            self.run_chain(items, 1)
            self.deferred = (lambda I=I: self.store_y(1, I))
        self.flush_store()

    def mixer_a(self, l):
        S, NT, NB = self.S, self.NT, self.NB
        T = self.T_const
        self.proj_gate(l, 0, "a_gate")
        for hf in range(2):
            self.load_w(l, OFF["a_cq"] + hf * 128, 128, 0, hf * 128)
        self.load_w(l, OFF["a_ckv"], 128, 1, 0)
        self.load_w(l, 320, 96, 1, 128)
        self.load_w(l, 320, 64, 2, 0)
        self.load_w(l, 400, 16, 2, 64)
        self.load_w(l, 384, 16, 2, 80)
        wq = self.bf("X1", 0, 768).rearrange("p (c n) -> p c n", c=2)
        wqs = self.bf("X1", 1536, 768).rearrange("p (c n) -> p c n", c=2)
        wkv = self.bf("X1", 3072, 512)
        Twq = Buf("wq")
        k = self.nw % 2
        self.nw += 1
        stg = self.wst[k][:, 0:768].rearrange("p (c n) -> p c n", c=2)
        self.dma("sp", stg, self.wuq_d[l].rearrange("(c p) n -> p c n", p=128), (), [self.Twst[k]])
        sc = 96 ** -0.5
        for c in range(2):
            g = self.pkl(l, 8 + c, 1)
            self.ts("pool", wq[:, c, :], stg[:, c, :], g, sc, ALU.mult, ALU.mult, [self.Twst[k], T], [Twq])
            s4 = stg[:, c, :].rearrange("p (h e) -> p h e", h=4)
            d4 = wqs[:, c, :].rearrange("p (h e) -> p h e", h=4)
            self.ts("pool", d4[:, :, 0:64], s4[:, :, 0:64], g, sc, ALU.mult, ALU.mult, [self.Twst[k], T], [Twq])
            self.ts("pool", d4[:, :, 64:80], s4[:, :, 80:96], g, sc, ALU.mult, ALU.mult, [self.Twst[k], T], [Twq])
            self.ts("pool", d4[:, :, 80:96], s4[:, :, 64:80], g, sc, ALU.mult, ALU.mult, [self.Twst[k], T], [Twq])
        k = self.nw % 2
        self.nw += 1
        stg2 = self.wst[k][:, 0:512]
        self.dma("sp", stg2, self.wukv_d[l], (), [self.Twst[k]])
        self.ts("pool", wkv, stg2, self.pkl(l, 10, 1), None, ALU.mult, None, [self.Twst[k], T], [Twq])
        wkv4 = wkv.rearrange("p (h e) -> p h e", h=4)
        sm = self.f32("st", 1344, 3 * NT + 8)
        Tsm = Buf("sma")
        junk = self.bf("W", 0, 256)
        Tj = Buf("junka")
        sq_q, sq_k = sm[:, 0:NT], sm[:, NT:2 * NT]
        for t in range(NT):
            ps, pt = self.proj_tm(self.wsl(0, 0, 256), 256, t, "B")
            self.act(junk, ps[:, 0:256], AF.Square, [pt, self.Trstd], [Tj, Tsm], scale=self.rstd[:, t:t + 1],
                     accum_out=sq_q[:, t:t + 1])
            ps, pt = self.proj_tm(self.wsl(1, 0, 128), 128, t, "B")
            self.act(junk[:, 0:128], ps[:, 0:128], AF.Square, [pt, self.Trstd], [Tj, Tsm], scale=self.rstd[:, t:t + 1],
                     accum_out=sq_k[:, t:t + 1])
        rqr = self.f32("st", 1760, 2 * NT)
        Trq = Buf("rqr")
        self.ts("dve", sq_q, sq_q, 1.0 / 256, EPS, ALU.mult, ALU.add, [Tsm], [Tsm])
        self.ts("dve", sq_k, sq_k, 1.0 / 128, EPS, ALU.mult, ALU.add, [Tsm], [Tsm])
        self.tt("pool", sm[:, 0:2 * NT], sm[:, 0:2 * NT], self.mhalfw, ALU.pow, [Tsm, T], [Tsm])
        self.tt("dve", rqr[:, 0:NT], sq_q, self.rstd, ALU.mult, [Tsm, self.Trstd], [Trq])
        self.tt("dve", rqr[:, NT:2 * NT], sq_k, self.rstd, ALU.mult, [Tsm, self.Trstd], [Trq])
        cqn = self.bf("W", 0, 1024).rearrange("p (c s) -> p c s", c=2)
        ckvn = self.bf("W", 2048, 512)
        ccb = self.f32("W", 3072, 512)
        ssb = self.f32("W", 5120, 512)
        t1 = self.f32("W", 7168, 512)
        t2 = self.f32("W", 9216, 512)
        rq_bc = self.f32("W", 11264, 512)
        rk_bc = self.f32("W", 13312, 512)
        Tcqn, Tckvn, Tcc, Tt1, Tt2, Trqb, Trkb = (Buf(n) for n in ("cqn", "ckvn", "cc", "t1a", "t2a", "rqb", "rkb"))
        diag = [self.f32("st", 2048 + i * 512, 128) for i in range(4)]
        Tdiag = [Buf(f"diaga{i}") for i in range(4)]
        Qh = [self.bf("QT", i * 2 * S, S) for i in range(2)]
        Kh = [self.bf("KT", i * 2 * S, S) for i in range(2)]
        rec = self.f32("st", 2016, 4)
        Trec = Buf("reca")
        for pair in range(2):
            self.s.barrier()
            for TB in range(NB):
                sl = slice(TB * 512, (TB + 1) * 512)
                Trb = self.Trbc[TB]
                self.dma("sp", ccb, self.cc_d[:, sl], (), [Tcc])
                self.dma("sp", ssb, self.ss_d[:, sl], (), [Tcc])
                self.bcast(rqr[:, 0:NT], TB, rq_bc, [Trq], [Trqb], diag, Tdiag)
                self.bcast(rqr[:, NT:2 * NT], TB, rk_bc, [Trq], [Trkb], diag, Tdiag)
                ps, pt = self.proj_fm(self.wsl(1, 0, 128), 128, TB, "A")
                self.tt("dve", ckvn, ps, rk_bc, ALU.mult, [pt, Trkb], [Tckvn])
                for hh in range(2):
                    h = pair * 2 + hh
                    ps, _, pt = self.bankA()
                    self.mm(ps[0:64, :], wkv4[:, h, 0:64], ckvn, True, True, [Twq, Tckvn], [pt])
                    self.cp("act", Kh[hh][0:64, sl], ps[0:64, :], [pt], [self.TKT[TB]])
                if pair == 0:
                    for tt_ in range(4):
                        t = TB * 4 + tt_
                        ps, _, pt = self.bankB()
                        self.mm(ps[:, 0:256].rearrange("p (h d) -> p h d", h=4), ckvn[:, tt_ * 128:(tt_ + 1) * 128],
                                wkv4[:, :, 64:128], True, True, [Twq, Tckvn], [pt])
                        self.cp("act", self.VT4[:, t, :, 0:64], ps[:, 0:256].rearrange("p (h d) -> p h d", h=4),
                                [pt], [self.TVT[t]])
                ps1, pt1 = self.proj_fm(self.wsl(1, 128, 96), 96, TB, "A")
                ps2, pt2 = self.proj_fm(self.wsl(2, 0, 96), 96, TB, "A")
                self.tt("dve", t1[64:96, :], ps1[64:96, :], ccb[64:96, :], ALU.mult, [pt1, Tcc], [Tt1])
                self.tt("dve", t2[64:96, :], ps2[64:96, :], ssb[64:96, :], ALU.mult, [pt2, Tcc], [Tt2])
                self.tt("dve", t1[64:96, :], t1[64:96, :], t2[64:96, :], ALU.add, [Tt1, Tt2], [Tt1])
                self.tt("dve", Kh[0][64:96, sl], t1[64:96, :], self.rbc[64:96, sl], ALU.mult, [Tt1, Trb], [self.TKT[TB]])
                self.tt("dve", Kh[1][64:96, sl], t1[64:96, :], self.rbc[64:96, sl], ALU.mult, [Tt1, Trb], [self.TKT[TB]])
                for c in range(2):
                    ps, pt = self.proj_fm(self.wsl(0, c * 128, 128), 128, TB, "A")
                    self.tt("dve", cqn[:, c, :], ps, rq_bc, ALU.mult, [pt, Trqb], [Tcqn])
                for hh in range(2):
                    h = pair * 2 + hh
                    psa, _, pta = self.bankA()
                    psb_, _, ptb = self.bankA()
                    for c in range(2):
                        self.mm(psa[0:96, :], wq[:, c, h * 96:(h + 1) * 96], cqn[:, c, :], c == 0, c == 1, [Twq, Tcqn], [pta])
                    for c in range(2):
                        self.mm(psb_[0:96, :], wqs[:, c, h * 96:(h + 1) * 96], cqn[:, c, :], c == 0, c == 1, [Twq, Tcqn], [ptb])
                    self.cp("act", Qh[hh][0:64, sl], psa[0:64, :], [pta], [self.TQT[TB]])
                    self.tt("dve", t1[64:96, :], psa[64:96, :], ccb[64:96, :], ALU.mult, [pta, Tcc], [Tt1])
                    self.tt("dve", t2[64:96, :], psb_[64:96, :], ssb[64:96, :], ALU.mult, [ptb, Tcc], [Tt2])
                    self.tt("dve", Qh[hh][64:96, sl], t1[64:96, :], t2[64:96, :], ALU.add, [Tt1, Tt2], [self.TQT[TB]])
            self.s.barrier()
            for I in range(NB):
                self.load_sg(0, I)
                items = []
                for hh in range(2):
                    h = pair * 2 + hh
                    kfn = lambda j, hh=hh: (Kh[hh][0:96, j * 128:(j + 1) * 128], [self.TKT[j // 4]])
                    qfn = lambda c0, c1, hh=hh, I=I: (Qh[hh][0:96, c0:c1], [self.TQT[I]])
                    vfn = lambda j, h=h: (self.VT4[:, j, h, :], [self.TVT[j]])

                    def extra(j, tt, I=I):
                        return [(self.cmb, [T])] if j == 4 * I + tt else []

                    def make(kfn=kfn, qfn=qfn, vfn=vfn, extra=extra, I=I):
                        return self.attn_map_gen(I, [(kfn, qfn, 0)], vfn, extra, None, pairs=True)

                    def post(O, Ot, hh=hh, h=h):
                        if hh == 0:
                            self.flush_store()
                        Ov = O[:, 0:260].rearrange("p (t d) -> p t d", t=4)
                        self.s.add("dve", lambda e, Ov=Ov: e.reciprocal(rec, Ov[:, :, 64]), [Ot], [Trec])
                        for tt in range(4):
                            self.stt(self.ytm[:, tt, hh * 64:(hh + 1) * 64], Ov[:, tt, 0:64], rec[:, tt:tt + 1],
                                     self.sg[:, tt, h * 64:(h + 1) * 64], ALU.mult, ALU.mult, [Ot, Trec, self.Tsg], [self.Tytm])
                    items.append((None, make, post, 2 * I + 4))
                self.run_chain(items, 1)
                self.deferred = (lambda I=I, pair=pair: self.store_y(0, I, r0=pair, nfc=1))
            self.flush_store()

    def mixer_c_proj(self, l):
        S, NT, NB = self.S, self.NT, self.NB
        T = self.T_const
        self.proj_gate(l, 2, "c_gate")
        self.proj_fm_to(l, "c_q", 256, self.QT3, self.TQT, scale=64 ** -0.5)
        self.proj_fm_to(l, "c_k", 256, self.KT3, self.TKT)
        self.proj_v(l, "c_v")
        self.QI3 = self.bf("QI", 0, 2 * S).rearrange("p (c s) -> p c s", c=2)
        self.TQI = [Buf(f"QI{i}") for i in range(NB)]
        self.proj_fm_to(l, "c_qidx", 256, self.QI3, self.TQI, scale=1.0 / 16.0)
        self.KX = self.bf("X1", 0, S)
        self.TKX = [Buf(f"KX{i}") for i in range(NB)]
        base = self.win_d[l, :, OFF["c_kidx"]:OFF["c_kidx"] + 32]
        k = self.nw % 2
        stg4 = self.wst[k].rearrange("p (c r n) -> p c r n", c=8, r=4)
        srcs = [(stg4[:, :, r, :], base.rearrange("(c p) n -> p c n", p=128)) for r in range(4)]
        self.load_w(l, 0, 128, 0, 0, src=srcs)
        for TB in range(NB):
            ps, pt = self.proj_fm(self.wsl(0, 0, 128), 128, TB, "A")
            self.tt("dve", self.KX[:, TB * 512:(TB + 1) * 512], ps, self.rbc[:, TB * 512:(TB + 1) * 512], ALU.mult,
                    [pt, self.Trbc[TB]], [self.TKX[TB]])
        self.load_w(l, OFF["c_widx"], 8, 1, 0)
        self.wabs = self.f32("st", 2048, NT * 8)
        self.wsgn = self.f32("st", 3072, NT * 8)
        self.Tww = Buf("ww")
        for t in range(NT):
            ps, pt = self.proj_tm(self.wsl(1, 0, 8), 8, t, "B")
            self.ts("dve", self.wabs[:, t * 8:(t + 1) * 8], ps[:, 0:8], self.rstd[:, t:t + 1], None, ALU.mult, None,
                    [pt, self.Trstd], [self.Tww])
        self.cp("dve", self.wsgn, self.wabs, [self.Tww], [self.Tww])

    def mixer_c_attn(self, l):
        S, NT, NB = self.S, self.NT, self.NB
        T = self.T_const
        nbis = self.nbis
        idx = self.f32("xT", 0, 4 * S).rearrange("p (g s) -> p g s", g=4)
        ob = self.o["rbc"]
        negms = [self.A8[:, ob + k * 4 * S: ob + (k + 1) * 4 * S].rearrange("p (g s) -> p g s", g=4) for k in range(2)]
        Tnegs = [[Buf(f"neg{k}_{g}") for g in range(4)] for k in range(2)]
        Tidx = [Buf(f"idx{g}") for g in range(4)]
        Rp = [self.bf("W", i * 2048, 1024).rearrange("p (b n) -> p b n", b=2) for i in range(4)]
        TRp = [Buf(f"Rp{i}") for i in range(4)]
        Dg = [self.bf("wbf", 8192 + i * 256, 128) for i in range(8)]
        self.yTs = self.bf("wbf", 10240, 1024).rearrange("p (c s) -> p c s", c=2)
        TDg = [Buf(f"Dg{i}") for i in range(8)]
        sm = self.f32("st", 1344, 64)
        Tsm = Buf("smc")
        rmax, rmin, lo, step, mid, cnt, tmp, thr = (sm[:, 4 * i:4 * i + 4] for i in range(8))
        rec = sm[:, 32:36]
        nmid = sm[:, 36:40]
        c256 = sm[:, 40:44]
        Tth = [Buf(f"th{g}") for g in range(4)]
        den = sm[:, 44:48]
        mone = sm[:, 48:52]
        self.memset("dve", mone, -1.0, [T])
        pw2 = self.f32("st", 1900, nbis)
        for it in range(nbis):
            self.memset("dve", pw2[:, it:it + 1], 2.0 ** -(it + 1), [T])
        Trec = Buf("recc")
        ncp = 0
        s2all = self.f32("st", 1600, 4 * nbis)
        Ts2 = Buf("s2all")

        def index_and_threshold(I, hb):
            negm = negms[I % 2]
            Tneg = Tnegs[I % 2]
            live = [None]
            est = 0

            def tick(k):
                if live[0] is None:
                    return
                for _ in range(k):
                    try:
                        next(live[0])
                    except StopIteration:
                        live[0] = None
                        return

            if True:
                tiles = [2 * hb, 2 * hb + 1]
                groups = []
                for tt in tiles:
                    i = 4 * I + tt
                    L = 128 * (i + 1)
                    nkb = (L + 511) // 512
                    for kb in range(nkb):
                        for half in range(2):
                            groups.append((tt, i, L, kb, half, kb == nkb - 1))
                state = {}

                def stageA(g, par):
                    tt, i, L, kb, half, lastkb = g
                    ncol = min(512, L - kb * 512)
                    for pr_ in range(2):
                        psp, ptoks = self.bankpair("ia")
                        rp = par * 2 + pr_
                        for b2 in range(2):
                            hq = pr_ * 2 + b2
                            b_ = 32 * hq
                            kw = {"tile_position": (96, 0)} if b_ == 96 else {}
                            self.mm(psp[:, b2 * 512:b2 * 512 + ncol], self.QI3[b_:b_ + 32, half, i * 128:(i + 1) * 128],
                                    self.KX[b_:b_ + 32, kb * 512:kb * 512 + ncol], True, True,
                                    [self.TQI[I]] + [self.TKX[kb]], [ptoks[b2]], **kw)
                        self.act(Rp[rp][:, :, 0:ncol], psp.rearrange("p (b n) -> p b n", b=2)[:, :, 0:ncol], AF.Relu,
                                 list(ptoks), [TRp[rp]])

                def stageB(g, par):
                    tt, i, L, kb, half, lastkb = g
                    ncol = min(512, L - kb * 512)
                    if kb == 0 and half == 0:
                        for hi in range(8):
                            self.ts("pool", Dg[hi], self.identb, self.wsgn[:, i * 8 + hi:i * 8 + hi + 1], 1.0, ALU.mult, ALU.mult,
                                    [T, self.Tww], [TDg[hi]])
                    if half == 0:
                        state["psB"] = self.bank("ib")
                    psB, _, ptB = state["psB"]
                    for hq in range(4):
                        hi = half * 4 + hq
                        rp = par * 2 + hq // 2
                        self.mm(psB[:, 0:ncol], Dg[hi], Rp[rp][:, hq % 2, 0:ncol], hi == 0, hi == 7, [TDg[hi], TRp[rp]], [ptB])
                    if half == 1:
                        self.cp("act", idx[:, tt, kb * 512:kb * 512 + ncol], psB[:, 0:ncol], [ptB], [Tidx[tt]])
                        if lastkb:
                            self.memset("pool", idx[0:64, tt, i * 128 + 64:i * 128 + 128], -1e30, [Tidx[tt]])

                kstep = (est + 2 * len(groups) - 1) // (2 * len(groups)) if est else 0
                for gi in range(len(groups) + 1):
                    if gi < len(groups):
                        stageA(groups[gi], gi % 2)
                    if gi >= 1:
                        stageB(groups[gi - 1], (gi - 1) % 2)
                    tick(kstep)
                tts = [tt for tt in tiles if 4 * I + tt >= 2]
                for tt in tiles:
                    if tt not in tts:
                        self.memset("dve", thr[:, tt:tt + 1], -1e29, [Tth[tt]])
                if tts:
                    Ls = {tt: 128 * (4 * I + tt + 1) for tt in tts}
                    for tt in tts:
                        L = Ls[tt]
                        self.s.add("dve", lambda e, tt=tt, L=L: e.tensor_reduce(rmax[:, tt:tt + 1], idx[:, tt, 0:L], AX.X, ALU.max),
                                   [Tidx[tt]], [Tth[tt]])
                    for tt in tts:
                        L = Ls[tt]
                        self.s.add("dve", lambda e, tt=tt, L=L: e.tensor_reduce(mid[:, tt:tt + 1], idx[:, tt, 0:320], AX.X, ALU.min),
                                   [Tidx[tt]], [Tth[tt]])
                    for tt in tts:
                        self.tt("dve", rmax[:, tt:tt + 1], rmax[:, tt:tt + 1], mid[:, tt:tt + 1], ALU.subtract, [Tth[tt]], [Tth[tt]])
                    for tt in tts:
                        self.ts("dve", s2all[:, tt * nbis:(tt + 1) * nbis], pw2, rmax[:, tt:tt + 1], None, ALU.mult, None,
                                [Tth[tt], T], [Ts2])
                    for tt in tts:
                        self.stt(mid[:, tt:tt + 1], rmax[:, tt:tt + 1], 0.5, mid[:, tt:tt + 1], ALU.mult, ALU.add,
                                 [Tth[tt]], [Tth[tt]])
                    for it in range(nbis):
                        for tt in tts:
                            L = Ls[tt]
                            self.ts("dve", negm[:, tt, 0:L], idx[:, tt, 0:L], mid[:, tt:tt + 1], None, ALU.is_ge, ALU.add,
                                    [Tidx[tt], Tth[tt]], [Tneg[tt], Tth[tt]], accum_out=cnt[:, tt:tt + 1], saturate=False)
                        for tt in tts:
                            self.ts("dve", tmp[:, tt:tt + 1], cnt[:, tt:tt + 1], 256.0, 0.5, ALU.is_ge, ALU.subtract,
                                    [Tth[tt]], [Tth[tt]])
                        for tt in tts:
                            self.stt(mid[:, tt:tt + 1], tmp[:, tt:tt + 1], s2all[:, tt * nbis + it:tt * nbis + it + 1],
                                     mid[:, tt:tt + 1], ALU.mult, ALU.add, [Tth[tt], Ts2], [Tth[tt]])
                    for tt in tts:
                        self.tt("dve", thr[:, tt:tt + 1], mid[:, tt:tt + 1], s2all[:, (tt + 1) * nbis - 1:(tt + 1) * nbis],
                                ALU.subtract, [Tth[tt], Ts2], [Tth[tt]])
                for tt in tiles:
                    L = 128 * (4 * I + tt + 1)
                    self.ts("dve", negm[:, tt, 0:L], idx[:, tt, 0:L], thr[:, tt:tt + 1], NEGM, ALU.is_lt, ALU.mult,
                            [Tidx[tt], Tth[tt]], [Tneg[tt]], saturate=False)

        def attention(I, hb):
            negm = negms[I % 2]
            Tneg = Tnegs[I % 2]
            if hb == 0:
                self.load_sg(2, I)
            items = []
            for h in (2 * hb, 2 * hb + 1):
                ch, base = h // 2, 0
                mi = 8 + h

                def pre(h=h):
                    self.ts("pool", qzc, self.QT3[:, h // 2, I * 512:(I + 1) * 512], self.rmask[:, 4 + h % 2:5 + h % 2], 1.0,
                            ALU.mult, ALU.mult, [self.TQT[I], T], [Tqzc])
                kfn = lambda j, ch=ch: (self.KT3[:, ch, j * 128:(j + 1) * 128], [self.TKT[j // 4]])
                qfn = lambda c0, c1, I=I: (qzc[:, c0 - I * 512:c1 - I * 512], [Tqzc])
                vfn = lambda j, h=h: (self.VT4[:, j, h, :], [self.TVT[j]])

                def extra(j, tt, mi=mi, I=I):
                    i = 4 * I + tt
                    r = [(negm[:, tt, j * 128:(j + 1) * 128], [Tneg[tt]])]
                    if j == i:
                        r.append((self.bd[:, mi * 128:(mi + 1) * 128], [T]))
                    if j == i - 1:
                        r.append((self.bs[:, mi * 128:(mi + 1) * 128], [T]))
                    return r

                def make(kfn=kfn, qfn=qfn, vfn=vfn, extra=extra, base=base):
                    return self.attn_map_gen(I, [(kfn, qfn, base)], vfn, extra, None, spool="as", opool="ao", pairs=True)

                def post(O, Ot, h=h):
                    if h == 0:
                        self.flush_store()
                    Ov = O[:, 0:260].rearrange("p (t d) -> p t d", t=4)
                    self.act(den, Ov[:, :, 64], AF.Copy, [Ot], [Trec])
                    self.tt("pool", rec, den, mone, ALU.pow, [Trec, T], [Trec])
                    for tt in range(4):
                        self.act(self.ytm[:, tt, h * 64:(h + 1) * 64], Ov[:, tt, 0:64], AF.Copy, [Ot, Trec], [self.Tytm],
                                 scale=rec[:, tt:tt + 1])
                    self.tt("pool", self.ytm[:, :, h * 64:(h + 1) * 64], self.ytm[:, :, h * 64:(h + 1) * 64],
                            self.sg[:, :, h * 64:(h + 1) * 64], ALU.mult, [self.Tsg, self.Tytm], [self.Tytm])
                items.append((pre, make, post, 2 * I + 4))
            self.run_chain(items, 1)
            if hb == 1:
                self.deferred = (lambda I=I: self.store_y(2, I))

        self.trpool = "as"
        qzc = self.bf("X2", 0, 512)
        Tqzc = Buf("qzc")
        for I in range(NB + 1):
            for hb in range(2):
                if I < NB:
                    index_and_threshold(I, hb)
                if I >= 1:
                    attention(I - 1, hb)
        self.flush_store()
        self.trpool = "A"

    def zero_y(self, rows):
        z = self.bf("W", 0, 512)
        Tz = Buf("z")
        self.memset("dve", z, 0.0, [Tz])
        for r in rows:
            for TB in range(self.NB):
                self.dma("pool", self.yT_d[r * 128:(r + 1) * 128, TB * 512:(TB + 1) * 512], z, [Tz],
                         [self.TyT[(r, TB)]])

    def record(self):
        S, NT, NB = self.S, self.NT, self.NB
        self.Tout = [Buf(f"out{t}") for t in range(NT)]
        Tx_in = [Buf(f"xin{t}") for t in range(NT)]
        Tx_res = [Buf(f"xres{t}") for t in range(NT)]
        self.setup()
        self.init_w()
        for l in range(self.depth):
            last = l == self.depth - 1
            xsrc = self.x_d if l == 0 else self.xres_d
            Tx = Tx_in if l == 0 else Tx_res
            self.TyT = {(r, TB): Buf(f"yT{r}_{TB}") for r in range(8) for TB in range(NB)}
            self.phase1(l, xsrc, Tx)
            self.s.barrier()
            for mi, m in enumerate("abcd"):
                if m not in self.mixers:
                    self.zero_y([2 * mi, 2 * mi + 1])
            self.init_attn()
            self.Tgate = {(mi, t): Buf(f"g{mi}_{t}") for mi in range(3) for t in range(NT)}
            if "d" in self.mixers:
                self.mixer_d(l)
                self.s.barrier()
            self.set_ones()
            if "b" in self.mixers:
                self.mixer_b(l)
                self.s.barrier()
            if "a" in self.mixers:
                self.mixer_a(l)
                self.s.barrier()
            if "c" in self.mixers:
                self.mixer_c_proj(l)
                self.s.barrier()
                self.mixer_c_attn(l)
            self.s.barrier()
            self.outproj(l, xsrc, Tx, self.Tout if last else Tx_res, last)
            self.s.barrier()


def build(S, depth=DEPTH, mixers="abcd", dbg=False, nbis=12):
    nc = bass.Bass("TRN2", target_bir_lowering=False)
    with ExitStack() as st:
        p = Prog(nc, st, S, depth, mixers, dbg, nbis)
        p.record()
        p.s.emit(st)
    return nc


def host_inputs(S, depth, norm_g, w_in, mla_qa_g, mla_w_uq, mla_kva_g, mla_w_ukv, diff_lambda,
                diff_subln_g, conv_w, w_out, rel_bias, final_g):
    f = np.float32
    c = _host_consts(S)
    pk = np.zeros((128, depth * PL + D), f)
    for l in range(depth):
        b = l * PL
        pk[:, b:b + 8] = np.asarray(norm_g[l], f).reshape(8, 128).T
        pk[:, b + 8:b + 10] = np.asarray(mla_qa_g[l], f).reshape(2, 128).T
        pk[:, b + 10] = np.asarray(mla_kva_g[l], f)
        pk[:, b + 11:b + 17] = np.asarray(conv_w[l], f).reshape(3, 2, 128).transpose(2, 1, 0).reshape(128, 6)
        pk[:, b + 17:b + 145] = np.asarray(diff_lambda[l], f).reshape(1, 128)
        pk[:, b + 145:b + 209] = np.asarray(diff_subln_g[l], f).reshape(1, 64)
    pk[:, depth * PL:] = np.asarray(final_g, f).reshape(1, D)
    cst = np.concatenate([c["ident_f"], c["ones_f"], c["cm"]], axis=1).astype(f)
    rmask = np.zeros((128, 8), f)
    for p_ in range(128):
        rmask[p_, p_ // 32] = 1.0
        rmask[p_, 4 + p_ // 64] = 1.0
    rb = np.asarray(rel_bias, f)
    bias = np.zeros((128, 2 * 12 * 128 + 12), f)
    for m in range(12):
        bias[:, m * 128:(m + 1) * 128] = rb[c["bk_diag"], m]
        bias[:, 1536 + m * 128:1536 + (m + 1) * 128] = rb[c["bk_sub"], m]
    bias[:, 3072:3084] = rb[15:16, :]
    shared = {
        "w_in": np.ascontiguousarray(np.asarray(w_in, f)[:depth]),
        "w_uq": np.ascontiguousarray(np.asarray(mla_w_uq, f)[:depth]),
        "w_ukv": np.ascontiguousarray(np.asarray(mla_w_ukv, f)[:depth]),
        "w_out": np.ascontiguousarray(np.asarray(w_out, f)[:depth]),
        "pk": pk, "cst": cst, "biasblk": bias, "cc4": c["cc4"], "ss4": c["ss4"], "rmask": rmask,
    }
    return shared


_NC_CACHE = {}


def kernel(x, norm_g, w_in, mla_qa_g, mla_w_uq, mla_kva_g, mla_w_ukv, diff_lambda,
           diff_subln_g, conv_w, w_out, rel_bias, final_g):
    x = np.asarray(x, np.float32)
    B, S, _ = x.shape
    shared = host_inputs(S, DEPTH, norm_g, w_in, mla_qa_g, mla_w_uq, mla_kva_g, mla_w_ukv, diff_lambda,
                         diff_subln_g, conv_w, w_out, rel_bias, final_g)
    key = (S, DEPTH)
    if key not in _NC_CACHE:
        _NC_CACHE[key] = build(S, DEPTH)
    nc = _NC_CACHE[key]
    in_maps = []
    for b in range(B):
        m = dict(shared)
        m["x"] = np.ascontiguousarray(x[b])
        in_maps.append(m)
    res = run_bass_kernel_spmd(nc, in_maps, core_ids=list(range(B)))
    return np.stack([np.asarray(r["out"], np.float32) for r in res.results], axis=0)
```

```python
import math
from contextlib import ExitStack

import numpy as np
import ml_dtypes

import concourse.bass as bass
import concourse.mybir as mybir
from concourse.bass_utils import run_bass_kernel_spmd

F32 = mybir.dt.float32
BF16 = mybir.dt.bfloat16
ALU = mybir.AluOpType
AF = mybir.ActivationFunctionType
AX = mybir.AxisListType

D = 1024
IN_COLS = 4040
DEPTH = 2
EPS = 1e-6
NEGM = -30000.0

OFF = {}
_o = 0
for _n, _w in (("a_cq", 256), ("a_ckv", 128), ("a_krope", 32), ("a_gate", 256),
               ("b_q", 256), ("b_k", 256), ("b_v", 256), ("b_gate", 256),
               ("c_q", 256), ("c_k", 256), ("c_v", 256),
               ("c_qidx", 256), ("c_kidx", 32), ("c_widx", 8), ("c_gate", 256),
               ("d_b", 256), ("d_c", 256), ("d_h", 256), ("d_gate", 256)):
    OFF[_n] = _o
    _o += _w
assert _o == IN_COLS


class Buf:
    __slots__ = ("name", "w", "r")

    def __init__(self, name=""):
        self.name = name
        self.w = None
        self.r = {}


class Op:
    __slots__ = ("eng", "fn", "deps", "inc", "pos", "val", "dma", "dsem", "dval", "dprev")


EPOCH = 30000
NDSEM = 8
ENGS = ("pe", "act", "dve", "pool", "sp")


class Sched:
    def __init__(self, nc):
        self.nc = nc
        self.ops = {e: [] for e in ENGS}

    def add(self, eng, fn, reads=(), writes=(), dma=False):
        op = Op()
        op.eng = eng
        op.fn = fn
        op.dma = dma
        op.inc = False
        op.val = 0
        lst = self.ops[eng]
        op.pos = len(lst)
        deps = {}
        for b in reads:
            if b.w is not None:
                deps[id(b.w)] = b.w
        for b in writes:
            if b.w is not None:
                deps[id(b.w)] = b.w
            for r in b.r.values():
                deps[id(r)] = r
        keep = []
        for d in deps.values():
            if not d.dma and d.eng == eng:
                if eng == "pe":
                    continue
                if op.pos - d.pos > 1:
                    continue
            keep.append(d)
            if not d.dma:
                d.inc = True
        op.deps = keep
        for b in reads:
            b.r[("d", id(op)) if dma else eng] = op
        for b in writes:
            b.w = op
            b.r = {}
        lst.append(op)
        return op

    def barrier(self):
        bar = {"pos": {e: len(self.ops[e]) for e in ENGS}, "snap": {}}
        for e in ENGS:
            for op in reversed(self.ops[e]):
                if not op.dma and op.fn is not None:
                    op.inc = True
                    break
        for e in ENGS:
            op = Op()
            op.eng = e
            op.fn = None
            op.dma = False
            op.inc = False
            op.val = 0
            op.pos = len(self.ops[e])
            op.deps = bar
            self.ops[e].append(op)

    def emit(self, stack):
        nc = self.nc
        engsems = {}
        for eng in ENGS:
            cnt = 0
            for op in self.ops[eng]:
                if op.dma:
                    continue
                if op.fn is None:
                    op.deps["snap"][eng] = cnt
                    continue
                if op.inc:
                    cnt += 1
                    op.val = cnt
            nep = (cnt + EPOCH - 1) // EPOCH + 1
            engsems[eng] = [stack.enter_context(nc.semaphore(f"s_{eng}_{i}")) for i in range(nep)]
        for eng in ENGS:
            dsems = None
            vals = [0] * NDSEM
            k = 0
            for op in self.ops[eng]:
                if op.fn is None:
                    op.deps["snap"]["d_" + eng] = (dsems, list(vals))
                    continue
                if not op.dma:
                    continue
                if dsems is None:
                    dsems = [stack.enter_context(nc.semaphore(f"d_{eng}_{i}")) for i in range(NDSEM)]
                s = k % NDSEM
                op.dsem = dsems[s]
                op.dprev = vals[s]
                vals[s] += 16
                op.dval = vals[s]
                k += 1

        def signal(d):
            if d.dma:
                return d.dsem, d.dval
            ep = (d.val - 1) // EPOCH
            return engsems[d.eng][ep], d.val - ep * EPOCH

        def run(e, eng):
            waited = {}
            for op in self.ops[eng]:
                if op.fn is None:
                    snap = op.deps["snap"]
                    for e2 in ENGS:
                        c = snap[e2]
                        if c > 0 and e2 != eng:
                            ep = (c - 1) // EPOCH
                            sem, val = engsems[e2][ep], c - ep * EPOCH
                            if waited.get(id(sem), 0) < val:
                                e.wait_ge(sem, val)
                                waited[id(sem)] = val
                        ds, vs = snap["d_" + e2]
                        if ds is not None:
                            for sem, val in zip(ds, vs):
                                if val > 0 and waited.get(id(sem), 0) < val:
                                    e.wait_ge(sem, val)
                                    waited[id(sem)] = val
                    continue
                waits = {}
                for d in op.deps:
                    sem, val = signal(d)
                    k = id(sem)
                    if k not in waits or waits[k][1] < val:
                        waits[k] = (sem, val)
                if op.dma and op.dprev > 0:
                    k = id(op.dsem)
                    if k not in waits or waits[k][1] < op.dprev:
                        waits[k] = (op.dsem, op.dprev)
                for k, (sem, val) in waits.items():
                    if waited.get(k, 0) < val:
                        e.wait_ge(sem, val)
                        waited[k] = val
                ins = op.fn(e)
                if op.dma:
                    ins.then_inc(op.dsem, 16)
                elif op.inc:
                    sem, _ = signal(op)
                    ins.then_inc(sem, 1)

        with nc.Block() as block:
            block.tensor(lambda e: run(e, "pe"))
            block.scalar(lambda e: run(e, "act"))
            block.vector(lambda e: run(e, "dve"))
            block.gpsimd(lambda e: run(e, "pool"))
            block.sync(lambda e: run(e, "sp"))


def _rel_bucket_np(rel):
    nb = 16
    max_exact = 8
    ret = np.where(rel > 0, nb, 0)
    n = np.abs(rel)
    nf = np.maximum(n, max_exact).astype(np.float32)
    large = max_exact + (np.log(nf / np.float32(max_exact)) / np.float32(math.log(128 / max_exact))
                         * np.float32(nb - max_exact)).astype(np.int32)
    large = np.minimum(large, nb - 1)
    return ret + np.where(n < max_exact, n, large)


def _host_consts(S):
    c = {}
    c["ident_f"] = np.eye(128, dtype=np.float32)
    c["ones_f"] = np.ones((128, 128), dtype=np.float32)
    q = np.arange(128)[:, None]
    k = np.arange(128)[None, :]
    cm = np.where((k // 64) <= (q // 64), 0.0, NEGM).astype(np.float32)
    c["cm"] = cm
    half = 16
    inv_freq = (np.float32(10000.0) ** (-np.arange(half, dtype=np.float32) / np.float32(half))).astype(np.float32)
    ang = np.arange(S, dtype=np.float32)[:, None] * inv_freq[None, :]
    cos = np.cos(ang).astype(np.float32).T
    sin = np.sin(ang).astype(np.float32).T
    cc = np.concatenate([cos, cos], 0)
    ss = np.concatenate([-sin, sin], 0)
    c["cc4"] = np.ascontiguousarray(np.tile(cc, (4, 1)))
    c["ss4"] = np.ascontiguousarray(np.tile(ss, (4, 1)))
    c["bk_diag"] = _rel_bucket_np(k - q)
    c["bk_sub"] = _rel_bucket_np(k - 128 - q)
    return c


PL = 8 + 2 + 1 + 6 + 128 + 64


class Prog:
    def __init__(self, nc, st, S, depth=DEPTH, mixers="abcd", dbg=False, nbis=12):
        self.nc = nc
        self.st = st
        self.S = S
        self.NT = S // 128
        self.NB = S // 512
        self.depth = depth
        self.mixers = mixers
        self.nbis = nbis
        self.s = Sched(nc)
        NT = self.NT
        dt = nc.dram_tensor
        self.x_d = dt("x", [S, D], F32, kind="ExternalInput").ap()
        self.win_d = dt("w_in", [depth, D, IN_COLS], F32, kind="ExternalInput").ap()
        self.wuq_d = dt("w_uq", [depth, 256, 384], F32, kind="ExternalInput").ap()
        self.wukv_d = dt("w_ukv", [depth, 128, 512], F32, kind="ExternalInput").ap()
        self.wout_d = dt("w_out", [depth, D, D], F32, kind="ExternalInput").ap()
        self.npk = depth * PL + D
        self.pk_d = dt("pk", [128, self.npk], F32, kind="ExternalInput").ap()
        self.cst_d = dt("cst", [128, 384], F32, kind="ExternalInput").ap()
        self.bias_d = dt("biasblk", [128, 2 * 12 * 128 + 12], F32, kind="ExternalInput").ap()
        self.rmask_d = dt("rmask", [128, 8], F32, kind="ExternalInput").ap()
        self.cc_d = dt("cc4", [128, S], F32, kind="ExternalInput").ap()
        self.ss_d = dt("ss4", [128, S], F32, kind="ExternalInput").ap()
        self.out_d = dt("out", [S, D], F32, kind="ExternalOutput").ap()
        self.xres_d = dt("xres", [S, D], F32).ap()
        self.yT_d = dt("yT", [D, S], BF16, kind="ExternalOutput" if dbg else "Internal").ap()
        self.gate_d = dt("gates", [3, S, 256], BF16).ap()
        o = {}
        off = 0

        def reg(name, nbytes):
            nonlocal off
            o[name] = off
            off += (nbytes + 31) // 32 * 32

        SR = max(S, 4096)
        NTR = SR // 128
        reg("P", 14336)
        reg("xT", 16 * SR)
        reg("rbc", 4 * SR)
        reg("wst", 2 * 4096)
        reg("wbf", 3 * 4096)
        reg("st", 4096)
        reg("QT", 4 * SR)
        reg("KT", 4 * SR)
        reg("VT", NTR * 4 * 65 * 2)
        reg("QI", 4 * SR + 32)
        reg("X1", 2 * SR)
        reg("W", 16384)
        reg("X2", 1024)
        self.o = o
        self.total = off
        self.A = st.enter_context(nc.sbuf_tensor("arena", [128, off // 4], F32))
        self.Ab = self.A.bitcast(BF16)
        self.A8 = self.A.bitcast(mybir.dt.float8e5)
        psh = [st.enter_context(nc.psum_tensor(f"ps{i}", [128, 1024], F32)) for i in range(4)]
        self.psp = [p[:, :] for p in psh]
        self.ps = [psh[i // 2][:, (i % 2) * 512:(i % 2 + 1) * 512] for i in range(8)]
        self.psb = [psh[i // 2].bitcast(BF16)[:, (i % 2) * 1024:(i % 2 + 1) * 1024] for i in range(8)]
        self.pstok = [Buf(f"ps{i}") for i in range(8)]
        self.rr = {}
        self.pools = {"A": [0, 1, 2, 3], "B": [4, 5, 6, 7], "as": [0, 1, 2, 3], "ao": [4, 5, 6, 7], "ia": [0, 1, 2, 3],
                      "ib": [4, 5, 6, 7]}
        self.trpool = "A"
        self.nPT = 4

    def f32(self, region, boff, n):
        b = (self.o[region] + boff) // 4
        return self.A[:, b:b + n]

    def bf(self, region, boff, n):
        b = (self.o[region] + boff) // 2
        return self.Ab[:, b:b + n]

    def bank(self, name):
        lst = self.pools[name]
        name = "A" if lst[0] == 0 else "B"
        k = self.rr.get(name, 0)
        self.rr[name] = k + 1
        i = lst[k % len(lst)]
        return self.ps[i], self.psb[i], self.pstok[i]

    def bankpair(self, name):
        lst = self.pools[name]
        name = "A" if lst[0] == 0 else "B"
        k = self.rr.get(name, 0)
        if k % 2:
            k += 1
        self.rr[name] = k + 2
        i = lst[k % len(lst)]
        return self.psp[i // 2], [self.pstok[i], self.pstok[i + 1]]

    def bankA(self):
        return self.bank("A")

    def bankB(self):
        return self.bank("B")

    def mm(self, out, lhsT, rhs, start, stop, R, Wt, **kw):
        self.s.add("pe", lambda e: e.matmul(out, lhsT, rhs, start=start, stop=stop, **kw), R, Wt)

    def tr(self, out, in_, ident, R, Wt):
        self.s.add("pe", lambda e: e.transpose(out, in_, ident), R, Wt)

    def act(self, out, in_, func, R, Wt, **kw):
        self.s.add("act", lambda e: e.activation(out, in_, func, **kw), R, Wt)

    def ts(self, eng, out, in0, s1, s2, op0, op1, R, Wt, **kw):
        if op1 is None:
            self.s.add(eng, lambda e: e.tensor_scalar(out, in0, s1, None, op0, **kw), R, Wt)
        else:
            self.s.add(eng, lambda e: e.tensor_scalar(out, in0, s1, s2, op0, op1, **kw), R, Wt)

    def tt(self, eng, out, in0, in1, op, R, Wt):
        self.s.add(eng, lambda e: e.tensor_tensor(out, in0, in1, op), R, Wt)

    def stt(self, out, in0, scalar, in1, op0, op1, R, Wt):
        self.s.add("dve", lambda e: e.scalar_tensor_tensor(out, in0, scalar, in1, op0, op1), R, Wt)

    def cp(self, eng, out, in_, R, Wt):
        if eng == "act":
            self.s.add("act", lambda e: e.activation(out, in_, AF.Copy), R, Wt)
        else:
            self.s.add(eng, lambda e: e.tensor_copy(out, in_), R, Wt)

    def memset(self, eng, ap, val, Wt):
        self.s.add(eng, lambda e: e.memset(ap, val), (), Wt)

    def dma(self, q, out, in_, R, Wt):
        self.s.add(q, lambda e: e.dma_start(out, in_), R, Wt, dma=True)


    def setup(self):
        S, NT = self.S, self.NT
        self.cst = self.f32("P", 0, 384)
        self.ident_f = self.cst[:, 0:128]
        self.ones_f = self.cst[:, 128:256]
        self.cm_f = self.cst[:, 256:384]
        self.identb = self.bf("P", 1536, 128)
        self.cmb = self.bf("P", 1792, 128)
        self.bd = self.bf("P", 2048, 12 * 128)
        self.bs = self.bf("P", 5120, 12 * 128)
        self.pk = self.f32("P", 8192, self.npk)
        self.cfar = self.f32("P", 8192 + 4 * self.npk, 12)
        self.T_const = Buf("const")
        T = self.T_const
        stg = self.f32("xT", 0, 3084)
        Tstg = Buf("stg")
        self.dma("sp", self.cst, self.cst_d, (), [T])
        self.dma("sp", self.pk, self.pk_d, (), [T])
        self.rmask = self.f32("P", 14016, 8)
        self.dma("sp", self.rmask, self.rmask_d, (), [T])
        self.dma("sp", stg, self.bias_d, (), [Tstg])
        self.cp("dve", self.identb, self.ident_f, [T], [T])
        self.cp("dve", self.cmb, self.cm_f, [T], [T])
        self.cp("dve", self.cfar, stg[:, 3072:3084], [Tstg], [T])
        for m in range(12):
            self.stt(self.bd[:, m * 128:(m + 1) * 128], stg[:, m * 128:(m + 1) * 128], self.cfar[:, m:m + 1],
                     self.cm_f, ALU.subtract, ALU.add, [Tstg, T], [T])
            self.ts("dve", self.bs[:, m * 128:(m + 1) * 128], stg[:, 1536 + m * 128:1536 + (m + 1) * 128],
                    self.cfar[:, m:m + 1], None, ALU.subtract, None, [Tstg, T], [T])
        self.ssq = self.f32("st", 0, NT)
        self.var = self.f32("st", 128, NT)
        self.rstd = self.f32("st", 256, NT)
        self.mhalf = self.f32("st", 384, NT)
        self.zeros_b = self.bf("st", 512, 260)
        self.mhalfw = self.f32("st", 1080, 2 * NT)
        self.memset("dve", self.mhalf, -0.5, [T])
        self.memset("dve", self.mhalfw, -0.5, [T])
        self.memset("dve", self.zeros_b, 0.0, [T])
        self.s.barrier()

    def pkl(self, l, off, n):
        return self.pk[:, l * PL + off: l * PL + off + n]

    def phase1(self, l, xsrc, Tx):
        S, NT, NB = self.S, self.NT, self.NB
        T = self.T_const
        xt = [self.f32("W", 0, 1024), self.f32("W", 4096, 1024)]
        Txt = [Buf("xt0"), Buf("xt1")]
        junk = self.bf("W", 8192, 1024)
        Tjunk = Buf("junk")
        diag = [self.f32("W", 10240 + i * 512, 128) for i in range(4)]
        Tdiag = [Buf(f"diag{i}") for i in range(4)]
        xTb = self.bf("xT", 0, 8 * S)
        self.xT = xTb
        self.TxT = [Buf(f"xT{t}") for t in range(NT)]
        Tssq = Buf("ssq")
        for t in range(NT):
            b = t % 2
            self.dma("sp", xt[b], xsrc[t * 128:(t + 1) * 128, :], [Tx[t]], [Txt[b]])
            self.act(junk, xt[b], AF.Square, [Txt[b]], [Tjunk, Tssq], accum_out=self.ssq[:, t:t + 1])
            for hb in range(2):
                ps, _, pt = self.bankA()
                for c4 in range(4):
                    c = hb * 4 + c4
                    self.tr(ps[:, c4 * 128:(c4 + 1) * 128], xt[b][:, c * 128:(c + 1) * 128], self.ident_f,
                            [Txt[b], T], [pt])
                dst = xTb.rearrange("p (c s) -> p c s", c=8)[:, hb * 4:(hb + 1) * 4, t * 128:(t + 1) * 128]
                self.cp("dve" if hb == 0 else "act", dst, ps.rearrange("p (c s) -> p c s", c=4), [pt], [self.TxT[t]])
        Trs = Buf("rstd")
        self.ts("dve", self.var, self.ssq, 1.0 / D, EPS, ALU.mult, ALU.add, [Tssq], [Trs])
        self.tt("pool", self.rstd, self.var, self.mhalf, ALU.pow, [Trs, T], [Trs])
        self.Trstd = Trs
        self.rbc = self.f32("rbc", 0, S)
        self.Trbc = [Buf(f"rbc{i}") for i in range(NB)]
        for TB in range(NB):
            self.bcast(self.rstd, TB, self.rbc[:, TB * 512:(TB + 1) * 512], [Trs], [self.Trbc[TB]], diag, Tdiag)

    def bcast(self, vec, TB, dst, R, Wt, diag, Tdiag):
        T = self.T_const
        ps, _, pt = self.bankA()
        for tt in range(4):
            t = TB * 4 + tt
            self.ts("dve", diag[tt], self.ident_f, vec[:, t:t + 1], None, ALU.mult, None, R + [T], [Tdiag[tt]])
            self.mm(ps[:, tt * 128:(tt + 1) * 128], self.ones_f, diag[tt], True, True, [Tdiag[tt], T], [pt])
        self.cp("act", dst, ps, [pt], Wt)

    def init_w(self):
        self.wst = [self.f32("wst", 0, 1024), self.f32("wst", 4096, 1024)]
        self.Twst = [Buf("wst0"), Buf("wst1")]
        self.wbf = [self.bf("wbf", i * 4096, 2048).rearrange("p (c n) -> p c n", c=8) for i in range(3)]
        self.Twbf = [Buf(f"wbf{i}") for i in range(3)]
        self.nw = 0
        self.wcache = set()
        self.prefetching = False

    def load_w(self, l, c0, n, slot, col, scale=1.0, src=None, gain=None, stg_view=None):
        key = (l, c0, n, slot, col, float(scale))
        if src is None and gain is None:
            if key in self.wcache:
                self.wcache.discard(key)
                return
            if self.prefetching:
                self.wcache.add(key)
        k = self.nw % 2
        self.nw += 1
        T = self.T_const
        stg = self.wst[k].rearrange("p (c n) -> p c n", c=8)
        if src is None:
            src = self.win_d[l, :, c0:c0 + n].rearrange("(c p) n -> p c n", p=128)
        if isinstance(src, list):
            for dv, sv in src:
                self.dma("sp", dv, sv, (), [self.Twst[k]])
        else:
            self.dma("sp", stg[:, :, 0:n], src, (), [self.Twst[k]])
        for c in range(8):
            g = self.pkl(l, c, 1) if gain is None else gain(c)
            self.ts("pool", self.wbf[slot][:, c, col:col + n], stg[:, c, 0:n], g, float(scale),
                    ALU.mult, ALU.mult, [self.Twst[k], T], [self.Twbf[slot]])

    def prefetch_w(self, l, specs):
        self.prefetching = True
        for (c0, n, slot, col, scale) in specs:
            self.load_w(l, c0, n, slot, col, scale)
        self.prefetching = False

    def proj_fm(self, lhs, M, TB, pool="A", ncol=512, c0=0):
        ps, psb, pt = self.bankA() if pool == "A" else self.bankB()
        xT3 = self.xT.rearrange("p (c s) -> p c s", c=8)
        R = [self.TxT[TB * 4 + i] for i in range(4)]
        for c in range(8):
            ap, toks = lhs(c)
            self.mm(ps[0:M, 0:ncol], ap, xT3[:, c, TB * 512 + c0:TB * 512 + c0 + ncol], c == 0, c == 7,
                    R + toks, [pt])
        return ps, pt

    def proj_tm(self, rhs, N, t, pool="A"):
        ps, psb, pt = self.bankA() if pool == "A" else self.bankB()
        xT3 = self.xT.rearrange("p (c s) -> p c s", c=8)
        for c in range(8):
            ap, toks = rhs(c)
            self.mm(ps[:, 0:N], xT3[:, c, t * 128:(t + 1) * 128], ap, c == 0, c == 7,
                    [self.TxT[t]] + toks, [pt])
        return ps, pt

    def wsl(self, slot, col, n):
        return lambda c: (self.wbf[slot][:, c, col:col + n], [self.Twbf[slot]])

    def mixer_d(self, l):
        S, NB = self.S, self.NB
        T = self.T_const
        u = self.f32("QI", 0, S + 2)
        Tu = Buf("u")
        self.memset("dve", u[:, 0:2], 0.0, [Tu])
        t1 = self.f32("W", 0, 512)
        t2 = self.f32("W", 2048, 512)
        acc = self.f32("W", 4096, 512)
        yo = [self.bf("W", 6144, 512), self.bf("W", 7168, 512)]
        Tt1, Tt2, Tacc = Buf("t1"), Buf("t2"), Buf("acc")
        Tyo = [Buf("yo0"), Buf("yo1")]
        k = 0
        for fc in range(2):
            for gi, name in enumerate(("d_b", "d_c", "d_h", "d_gate")):
                self.load_w(l, OFF[name] + fc * 128, 128, gi // 2, (gi % 2) * 128)
            cw = self.pkl(l, 11 + fc * 3, 3)
            for TB in range(NB):
                pool = "A" if TB % 2 == 0 else "B"
                psb_, ptb = self.proj_fm(self.wsl(0, 0, 128), 128, TB, pool)
                psc, ptc = self.proj_fm(self.wsl(0, 128, 128), 128, TB, pool)
                psh, pth = self.proj_fm(self.wsl(1, 0, 128), 128, TB, pool)
                psg, ptg = self.proj_fm(self.wsl(1, 128, 128), 128, TB, pool)
                rb = self.rbc[:, TB * 512:(TB + 1) * 512]
                Trb = [self.Trbc[TB]]
                ub = u[:, 2 + TB * 512: 2 + (TB + 1) * 512]
                self.tt("dve", t1, psc, rb, ALU.mult, [ptc] + Trb, [Tt1])
                self.tt("dve", t2, psh, rb, ALU.mult, [pth] + Trb, [Tt2])
                self.tt("dve", ub, t1, t2, ALU.mult, [Tt1, Tt2], [Tu])
                self.ts("dve", acc, u[:, TB * 512: TB * 512 + 512], cw[:, 0:1], None, ALU.mult, None,
                        [Tu, T], [Tacc])
                self.stt(acc, u[:, TB * 512 + 1: TB * 512 + 513], cw[:, 1:2], acc, ALU.mult, ALU.add,
                         [Tu, T, Tacc], [Tacc])
                self.stt(acc, ub, cw[:, 2:3], acc, ALU.mult, ALU.add, [Tu, T, Tacc], [Tacc])
                self.tt("dve", t1, psb_, rb, ALU.mult, [ptb] + Trb, [Tt1])
                self.tt("dve", acc, acc, t1, ALU.mult, [Tacc, Tt1], [Tacc])
                self.tt("dve", t2, psg, rb, ALU.mult, [ptg] + Trb, [Tt2])
                self.act(t1, t2, AF.Silu, [Tt2], [Tt1])
                y = yo[k % 2]
                Ty = Tyo[k % 2]
                k += 1
                self.tt("dve", y, acc, t1, ALU.mult, [Tacc, Tt1], [Ty])
                r0 = 768 + fc * 128
                self.dma("pool", self.yT_d[r0:r0 + 128, TB * 512:(TB + 1) * 512], y, [Ty],
                         [self.TyT[(r0 // 128, TB)]])

    def outproj(self, l, xsrc, Tx, Tdst, last):
        S, NT, NB = self.S, self.NT, self.NB
        T = self.T_const
        wo = self.bf("QT", 0, 8 * 1024).rearrange("p (c n) -> p c n", c=8)
        Two = Buf("wo")
        for j in range(8):
            k = self.nw % 2
            self.nw += 1
            stg = self.wst[k].rearrange("p (c n) -> p c n", c=8)
            self.dma("sp", stg, self.wout_d[l, :, j * 128:(j + 1) * 128].rearrange("(c p) n -> p c n", p=128),
                     (), [self.Twst[k]])
            self.cp("pool" if j % 2 else "dve", wo[:, :, j * 128:(j + 1) * 128], stg, [self.Twst[k]], [Two])
        base = 16384
        yTb = [self.bf("QT", base + i * 8192, 4096).rearrange("p (c s) -> p c s", c=8) for i in range(2)]
        TyTb = [Buf("yTb0"), Buf("yTb1")]
        xt = [self.f32("QT", base + 16384 + i * 4096, 1024) for i in range(2)]
        Txt = [Buf("oxt0"), Buf("oxt1")]
        xn = [self.f32("QT", base + 24576 + i * 4096, 1024) for i in range(2)]
        Txn = [Buf("xn0"), Buf("xn1")]
        junk = self.bf("W", 8192, 1024)
        Tjunk = Buf("junk2")
        st4 = self.f32("st", 1040, 8)
        Tst4 = Buf("st4")
        fg = self.pk[:, self.depth * PL: self.depth * PL + D]
        dst = self.xres_d if not last else self.out_d
        for TB in range(NB):
            yb = yTb[TB % 2]
            Ty = TyTb[TB % 2]
            self.dma("sp", yb, self.yT_d[:, TB * 512:(TB + 1) * 512].rearrange("(c p) s -> p c s", p=128),
                     [self.TyT[(r, TB)] for r in range(8)], [Ty])
            for tt_ in range(4):
                t = TB * 4 + tt_
                b = t % 2
                self.dma("sp", xt[b], xsrc[t * 128:(t + 1) * 128, :], [Tx[t]], [Txt[b]])
                for half in range(2):
                    ps, _, pt = self.bankA()
                    for c in range(8):
                        self.mm(ps, yb[:, c, tt_ * 128:(tt_ + 1) * 128], wo[:, c, half * 512:(half + 1) * 512],
                                c == 0, c == 7, [Ty, Two], [pt])
                    self.tt("dve", xn[b][:, half * 512:(half + 1) * 512], ps, xt[b][:, half * 512:(half + 1) * 512],
                            ALU.add, [pt, Txt[b]], [Txn[b]])
                if not last:
                    self.dma("pool", dst[t * 128:(t + 1) * 128, :], xn[b], [Txn[b]], [Tdst[t]])
                else:
                    c0 = (t % 2) * 4
                    self.act(junk, xn[b], AF.Square, [Txn[b]], [Tjunk, Tst4], accum_out=st4[:, c0:c0 + 1])
                    self.ts("dve", st4[:, c0 + 1:c0 + 2], st4[:, c0:c0 + 1], 1.0 / D, EPS, ALU.mult, ALU.add,
                            [Tst4], [Tst4])
                    self.tt("pool", st4[:, c0 + 2:c0 + 3], st4[:, c0 + 1:c0 + 2], self.mhalf[:, 0:1], ALU.pow,
                            [Tst4, T], [Tst4])
                    self.stt(xn[b], xn[b], st4[:, c0 + 2:c0 + 3], fg, ALU.mult, ALU.mult, [Txn[b], Tst4, T], [Txn[b]])
                    self.dma("pool", dst[t * 128:(t + 1) * 128, :], xn[b], [Txn[b]], [Tdst[t]])

    def init_attn(self):
        S, NT = self.S, self.NT
        self.QT3 = self.bf("QT", 0, 2 * S).rearrange("p (c s) -> p c s", c=2)
        self.KT3 = self.bf("KT", 0, 2 * S).rearrange("p (c s) -> p c s", c=2)
        self.VT4 = self.bf("VT", 0, NT * 260).rearrange("p (t h d) -> p t h d", t=NT, h=4)
        self.TQT = [Buf(f"QT{i}") for i in range(self.NB)]
        self.TKT = [Buf(f"KT{i}") for i in range(self.NB)]
        self.TVT = [Buf(f"VT{t}") for t in range(NT)]
        self.PT = [self.bf("W", 8192 + i * 1024, 512) for i in range(4)]
        self.TPT = [Buf(f"PT{i}") for i in range(4)]
        self.npt = 0
        self.nPT = 4
        self.ytm = self.bf("W", 12288, 1024).rearrange("p (t f) -> p t f", t=4)
        self.Tytm = Buf("ytm")
        self.sg = self.bf("W", 14336, 1024).rearrange("p (t f) -> p t f", t=4)
        self.Tsg = Buf("sg")
        self.yTs = self.bf("W", 6144, 1024).rearrange("p (c s) -> p c s", c=2)
        self.TyTs = Buf("yTs")

    def set_ones(self):
        for t in range(self.NT):
            self.memset("pool", self.VT4[:, t, :, 64:65], 1.0, [self.TVT[t]])

    def attn_map(self, I, qk, vfn, extra, cbias, pairs=True):
        g = self.attn_map_gen(I, qk, vfn, extra, cbias, pairs=pairs)
        while True:
            try:
                next(g)
            except StopIteration as e:
                return e.value

    def attn_map_gen(self, I, qk, vfn, extra, cbias, spool="A", opool="B", LA=2, pairs=False):
        T = self.T_const
        O, _, Ot = self.bank(opool)
        self.mm(O[:, 0:260], self.identb, self.zeros_b, True, True, [T], [Ot])
        if pairs:
            units = [(2 * p, 2 * p + 1) for p in range(2 * I)] + [(j,) for j in range(4 * I, 4 * I + 4)]
            LA = 1
        else:
            units = [(j,) for j in range(4 * I + 4)]
        pend = {}
        for s_ in range(len(units) + LA):
            if s_ > 0:
                yield
            if s_ < len(units):
                u = units[s_]
                if pairs:
                    psp, ptoks = self.bankpair(spool)
                    pi = self.npt % 2
                    P = self.bf("W", 8192 + pi * 2048, 1024)
                    Pt = self.TPT[pi]
                else:
                    ps1, _, pt1 = self.bank(spool)
                    psp, ptoks = ps1, [pt1]
                    pi = self.npt % self.nPT
                    P, Pt = self.PT[pi], self.TPT[pi]
                self.npt += 1
                info = []
                for k_, j in enumerate(u):
                    r0 = max(0, j - 4 * I)
                    a = r0 * 128
                    ps = psp[:, k_ * 512:(k_ + 1) * 512]
                    pt = ptoks[k_]
                    adds = []
                    for tt in range(r0, 4):
                        for (ap, toks) in extra(j, tt):
                            adds.append((tt, ap, toks))
                    n = len(qk)
                    for k, (kfn, qfn, base) in enumerate(qk):
                        kap, ktok = kfn(j)
                        qap, qtok = qfn(I * 512 + a, (I + 1) * 512)
                        kw = {"tile_position": (96, 0)} if base == 96 else {}
                        self.mm(ps[:, a:512], kap, qap, k == 0, (k == n - 1) and not adds, ktok + qtok, [pt], **kw)
                    for k, (tt, ap, toks) in enumerate(adds):
                        self.mm(ps[:, tt * 128:(tt + 1) * 128], ap, self.identb, False, k == len(adds) - 1,
                                toks + [T], [pt], skip_group_check=True)
                    info.append((j, r0, k_))
                a0 = info[0][1] * 128 if len(u) == 1 else 0
                w = 512 * len(u)
                if cbias is None:
                    self.act(P[:, a0:w], psp[:, a0:w], AF.Exp, list(ptoks), [Pt])
                else:
                    self.act(P[:, a0:w], psp[:, a0:w], AF.Exp, list(ptoks) + [T], [Pt], bias=cbias)
                pend[s_] = (P, Pt, info)
            sp = s_ - LA
            if sp >= 0:
                P, Pt, info = pend.pop(sp)
                for (j, r0, k_) in info:
                    vap, vtok = vfn(j)
                    for tt in range(r0, 4):
                        self.mm(O[:, tt * 65:(tt + 1) * 65], P[:, k_ * 512 + tt * 128:k_ * 512 + (tt + 1) * 128], vap,
                                False, j == 4 * I + tt, [Pt] + vtok, [Ot], skip_group_check=True)
        return O, Ot

    def run_chain(self, items, LA):
        prev = None
        for (pre, make, post, nunits) in items:
            if pre is not None:
                pre()
            g = make()
            res = None
            alive = True
            for _ in range(LA):
                try:
                    next(g)
                except StopIteration as e:
                    res = e.value
                    alive = False
                    break
            if prev is not None:
                self._finish(prev)
            for _ in range(nunits - LA):
                if not alive:
                    break
                try:
                    next(g)
                except StopIteration as e:
                    res = e.value
                    alive = False
            prev = (g, post, alive, res)
        if prev is not None:
            self._finish(prev)

    def flush_store(self):
        if getattr(self, "deferred", None) is not None:
            f = self.deferred
            self.deferred = None
            f()

    def _finish(self, p):
        g, post, alive, res = p
        while alive:
            try:
                next(g)
            except StopIteration as e:
                res = e.value
                alive = False
        post(*res)

    def load_sg(self, mi, I):
        self.dma("sp", self.sg, self.gate_d[mi, I * 512:(I + 1) * 512, :].rearrange("(t p) f -> p t f", p=128),
                 [self.Tgate[(mi, I * 4 + tt)] for tt in range(4)], [self.Tsg])

    def store_y(self, mi, I, r0=None, nfc=2):
        T = self.T_const
        if r0 is None:
            r0 = mi * 2
        for fc in range(nfc):
            ps, psb, pt = self.bank(self.trpool)
            for tt in range(4):
                self.tr(psb[:, tt * 128:(tt + 1) * 128], self.ytm[:, tt, fc * 128:(fc + 1) * 128], self.identb,
                        [self.Tytm, T], [pt])
            self.cp("act" if self.trpool == "as" else "dve", self.yTs[:, fc, :], psb[:, 0:512], [pt], [self.TyTs])
        self.dma("pool", self.yT_d[r0 * 128:(r0 + nfc) * 128, I * 512:(I + 1) * 512].rearrange("(c p) s -> p c s", p=128),
                 self.yTs[:, 0:nfc, :], [self.TyTs], [self.TyT[(r0 + i, I)] for i in range(nfc)])

    def proj_gate(self, l, mi, name):
        for hf in range(2):
            self.load_w(l, OFF[name] + hf * 128, 128, 2, hf * 128)
        gt = [self.bf("W", 15360, 256), self.bf("W", 15872, 256)]
        Tg = [Buf("gt0"), Buf("gt1")]
        for t in range(self.NT):
            ps, pt = self.proj_tm(self.wsl(2, 0, 256), 256, t, "B")
            b = t % 2
            self.act(gt[b], ps[:, 0:256], AF.Silu, [pt, self.Trstd], [Tg[b]], scale=self.rstd[:, t:t + 1])
            self.dma("pool", self.gate_d[mi, t * 128:(t + 1) * 128, :], gt[b], [Tg[b]], [self.Tgate[(mi, t)]])

    def proj_fm_to(self, l, name, ncols, dst3, Tdst, scale=1.0):
        for ch in range(ncols // 128):
            slot = ch % 2
            self.load_w(l, OFF[name] + ch * 128, 128, slot, 0, scale)
            for TB in range(self.NB):
                ps, pt = self.proj_fm(self.wsl(slot, 0, 128), 128, TB, "A")
                self.tt("dve", dst3[:, ch, TB * 512:(TB + 1) * 512], ps, self.rbc[:, TB * 512:(TB + 1) * 512],
                        ALU.mult, [pt, self.Trbc[TB]], [Tdst[TB]])

    def proj_v(self, l, name):
        for hf in range(2):
            self.load_w(l, OFF[name] + hf * 128, 128, 2, hf * 128)
        for t in range(self.NT):
            ps, pt = self.proj_tm(self.wsl(2, 0, 256), 256, t, "B")
            self.act(self.VT4[:, t, :, 0:64], ps[:, 0:256].rearrange("p (h d) -> p h d", h=4), AF.Copy,
                     [pt, self.Trstd], [self.TVT[t]], scale=self.rstd[:, t:t + 1])

    def mixer_b(self, l):
        S, NT, NB = self.S, self.NT, self.NB
        T = self.T_const
        lam_init = 0.8 - 0.6 * math.exp(-0.3 * l)
        self.proj_fm_to(l, "b_q", 256, self.QT3, self.TQT, scale=32 ** -0.5)
        self.proj_fm_to(l, "b_k", 256, self.KT3, self.TKT)
        self.proj_v(l, "b_v")
        self.proj_gate(l, 1, "b_gate")
        self.s.barrier()
        if "a" in self.mixers:
            self.prefetch_w(l, [(OFF["a_gate"], 128, 2, 0, 1.0), (OFF["a_gate"] + 128, 128, 2, 128, 1.0),
                                (OFF["a_cq"], 128, 0, 0, 1.0), (OFF["a_cq"] + 128, 128, 0, 128, 1.0),
                                (OFF["a_ckv"], 128, 1, 0, 1.0), (320, 96, 1, 128, 1.0)])
        sm = self.f32("st", 1344, 32)
        Tsm = Buf("sm")
        lp = self.pkl(l, 17, 128)
        pr = self.f32("st", 1500, 64)
        self.tt("dve", pr[:, 0:32], lp[:, 0:32], lp[:, 32:64], ALU.mult, [T], [Tsm])
        self.tt("dve", pr[:, 32:64], lp[:, 64:96], lp[:, 96:128], ALU.mult, [T], [Tsm])
        self.s.add("dve", lambda e: e.tensor_reduce(sm[:, 0:2], pr.rearrange("p (a b) -> p a b", a=2), AX.X, ALU.add),
                   [Tsm], [Tsm])
        self.act(sm[:, 2:4], sm[:, 0:2], AF.Exp, [Tsm], [Tsm])
        self.tt("dve", sm[:, 4:5], sm[:, 2:3], sm[:, 3:4], ALU.subtract, [Tsm], [Tsm])
        self.ts("dve", sm[:, 5:6], sm[:, 4:5], -1.0, -lam_init, ALU.mult, ALU.add, [Tsm], [Tsm])
        neglam = sm[:, 5:6]
        gsub = self.pkl(l, 145, 64)
        rec = sm[:, 8:16]
        o_h = self.f32("W", 0, 256).rearrange("p (t d) -> p t d", t=4)
        sq = self.f32("W", 1024, 256).rearrange("p (t d) -> p t d", t=4)
        gsg = self.f32("W", 2048, 256).rearrange("p (t d) -> p t d", t=4)
        Toh, Tsq, Tgsg = Buf("oh"), Buf("sq"), Buf("gsg")
        cvar = 1.0 / (1.0 - lam_init) ** 2
        Qz = [self.bf("W", 3072, 512), self.bf("W", 4096, 512), self.bf("W", 5120, 512)]
        TQz = [Buf("qz0"), Buf("qz1"), Buf("qz2")]
        nqz = [0]
        for I in range(NB):
            self.load_sg(1, I)
            items = []
            Ores = {}
            for h in range(4):
                for m in range(2):
                    ch, base = h // 2, 0
                    mi = 2 * h + m
                    m4 = (h % 2) * 2 + m
                    qz, Tqz = Qz[nqz[0] % 3], TQz[nqz[0] % 3]
                    nqz[0] += 1

                    def pre(qz=qz, Tqz=Tqz, ch=ch, m4=m4, I=I):
                        self.ts("dve", qz, self.QT3[:, ch, I * 512:(I + 1) * 512], self.rmask[:, m4:m4 + 1], None,
                                ALU.mult, None, [self.TQT[I], T], [Tqz])
                    kfn = lambda j, ch=ch: (self.KT3[:, ch, j * 128:(j + 1) * 128], [self.TKT[j // 4]])
                    qfn = lambda c0, c1, qz=qz, Tqz=Tqz, I=I: (qz[:, c0 - I * 512:c1 - I * 512], [Tqz])
                    vfn = lambda j, h=h: (self.VT4[:, j, h, :], [self.TVT[j]])

                    def extra(j, tt, mi=mi, I=I):
                        i = 4 * I + tt
                        if j == i:
                            return [(self.bd[:, mi * 128:(mi + 1) * 128], [T])]
                        if j == i - 1:
                            return [(self.bs[:, mi * 128:(mi + 1) * 128], [T])]
                        return []

                    def make(kfn=kfn, qfn=qfn, vfn=vfn, extra=extra, base=base, I=I):
                        return self.attn_map_gen(I, [(kfn, qfn, base)], vfn, extra, None, pairs=True)

                    def post(O, Ot, h=h, m=m, I=I):
                        Ores[(h, m)] = (O, Ot)
                        if m == 0:
                            if h == 0:
                                self.flush_store()
                            return
                        (O0, T0), (O1, T1) = Ores[(h, 0)], Ores[(h, 1)]
                        O0v = O0[:, 0:260].rearrange("p (t d) -> p t d", t=4)
                        O1v = O1[:, 0:260].rearrange("p (t d) -> p t d", t=4)
                        self.s.add("dve", lambda e, O0v=O0v: e.reciprocal(rec[:, 0:4], O0v[:, :, 64]), [T0], [Tsm])
                        self.s.add("dve", lambda e, O1v=O1v: e.reciprocal(rec[:, 4:8], O1v[:, :, 64]), [T1], [Tsm])
                        self.ts("dve", rec[:, 4:8], rec[:, 4:8], neglam, None, ALU.mult, None, [Tsm], [Tsm])
                        for tt in range(4):
                            self.ts("dve", o_h[:, tt, :], O0v[:, tt, 0:64], rec[:, tt:tt + 1], None, ALU.mult, None,
                                    [T0, Tsm], [Toh])
                        for tt in range(4):
                            self.stt(o_h[:, tt, :], O1v[:, tt, 0:64], rec[:, 4 + tt:5 + tt], o_h[:, tt, :], ALU.mult, ALU.add,
                                     [T1, Tsm, Toh], [Toh])
                        self.tt("dve", sq, o_h, o_h, ALU.mult, [Toh], [Tsq])
                        self.s.add("dve", lambda e: e.tensor_reduce(sm[:, 16:20], sq, AX.X, ALU.add), [Tsq], [Tsm])
                        self.ts("dve", sm[:, 20:24], sm[:, 16:20], cvar / 64.0, EPS * cvar, ALU.mult, ALU.add, [Tsm], [Tsm])
                        self.tt("pool", sm[:, 24:28], sm[:, 20:24], self.mhalf[:, 0:4], ALU.pow, [Tsm, T], [Tsm])
                        for tt in range(4):
                            self.tt("dve", gsg[:, tt, :], self.sg[:, tt, h * 64:(h + 1) * 64], gsub, ALU.mult,
                                    [self.Tsg, T], [Tgsg])
                        for tt in range(4):
                            self.stt(self.ytm[:, tt, h * 64:(h + 1) * 64], o_h[:, tt, :], sm[:, 24 + tt:25 + tt], gsg[:, tt, :],
                                     ALU.mult, ALU.mult, [Toh, Tsm, Tgsg], [self.Tytm])
                    items.append((pre, make, post, 2 * I + 4))
            self.run_chain(items, 1)
            self.deferred = (lambda I=I: self.store_y(1, I))
        self.flush_store()

    def mixer_a(self, l):
        S, NT, NB = self.S, self.NT, self.NB
        T = self.T_const
        self.proj_gate(l, 0, "a_gate")
        for hf in range(2):
            self.load_w(l, OFF["a_cq"] + hf * 128, 128, 0, hf * 128)
        self.load_w(l, OFF["a_ckv"], 128, 1, 0)
        self.load_w(l, 320, 96, 1, 128)
        self.load_w(l, 320, 64, 2, 0)
        self.load_w(l, 400, 16, 2, 64)
        self.load_w(l, 384, 16, 2, 80)
        wq = self.bf("X1", 0, 768).rearrange("p (c n) -> p c n", c=2)
        wqs = self.bf("X1", 1536, 768).rearrange("p (c n) -> p c n", c=2)
        wkv = self.bf("X1", 3072, 512)
        Twq = Buf("wq")
        k = self.nw % 2
        self.nw += 1
        stg = self.wst[k][:, 0:768].rearrange("p (c n) -> p c n", c=2)
        self.dma("sp", stg, self.wuq_d[l].rearrange("(c p) n -> p c n", p=128), (), [self.Twst[k]])
        sc = 96 ** -0.5
        for c in range(2):
            g = self.pkl(l, 8 + c, 1)
            self.ts("pool", wq[:, c, :], stg[:, c, :], g, sc, ALU.mult, ALU.mult, [self.Twst[k], T], [Twq])
            s4 = stg[:, c, :].rearrange("p (h e) -> p h e", h=4)
            d4 = wqs[:, c, :].rearrange("p (h e) -> p h e", h=4)
            self.ts("pool", d4[:, :, 0:64], s4[:, :, 0:64], g, sc, ALU.mult, ALU.mult, [self.Twst[k], T], [Twq])
            self.ts("pool", d4[:, :, 64:80], s4[:, :, 80:96], g, sc, ALU.mult, ALU.mult, [self.Twst[k], T], [Twq])
            self.ts("pool", d4[:, :, 80:96], s4[:, :, 64:80], g, sc, ALU.mult, ALU.mult, [self.Twst[k], T], [Twq])
        k = self.nw % 2
        self.nw += 1
        stg2 = self.wst[k][:, 0:512]
        self.dma("sp", stg2, self.wukv_d[l], (), [self.Twst[k]])
        self.ts("pool", wkv, stg2, self.pkl(l, 10, 1), None, ALU.mult, None, [self.Twst[k], T], [Twq])
        wkv4 = wkv.rearrange("p (h e) -> p h e", h=4)
        sm = self.f32("st", 1344, 3 * NT + 8)
        Tsm = Buf("sma")
        junk = self.bf("W", 0, 256)
        Tj = Buf("junka")
        sq_q, sq_k = sm[:, 0:NT], sm[:, NT:2 * NT]
        for t in range(NT):
            ps, pt = self.proj_tm(self.wsl(0, 0, 256), 256, t, "B")
            self.act(junk, ps[:, 0:256], AF.Square, [pt, self.Trstd], [Tj, Tsm], scale=self.rstd[:, t:t + 1],
                     accum_out=sq_q[:, t:t + 1])
            ps, pt = self.proj_tm(self.wsl(1, 0, 128), 128, t, "B")
            self.act(junk[:, 0:128], ps[:, 0:128], AF.Square, [pt, self.Trstd], [Tj, Tsm], scale=self.rstd[:, t:t + 1],
                     accum_out=sq_k[:, t:t + 1])
        rqr = self.f32("st", 1760, 2 * NT)
        Trq = Buf("rqr")
        self.ts("dve", sq_q, sq_q, 1.0 / 256, EPS, ALU.mult, ALU.add, [Tsm], [Tsm])
        self.ts("dve", sq_k, sq_k, 1.0 / 128, EPS, ALU.mult, ALU.add, [Tsm], [Tsm])
        self.tt("pool", sm[:, 0:2 * NT], sm[:, 0:2 * NT], self.mhalfw, ALU.pow, [Tsm, T], [Tsm])
        self.tt("dve", rqr[:, 0:NT], sq_q, self.rstd, ALU.mult, [Tsm, self.Trstd], [Trq])
        self.tt("dve", rqr[:, NT:2 * NT], sq_k, self.rstd, ALU.mult, [Tsm, self.Trstd], [Trq])
        cqn = self.bf("W", 0, 1024).rearrange("p (c s) -> p c s", c=2)
        ckvn = self.bf("W", 2048, 512)
        ccb = self.f32("W", 3072, 512)
        ssb = self.f32("W", 5120, 512)
        t1 = self.f32("W", 7168, 512)
        t2 = self.f32("W", 9216, 512)
        rq_bc = self.f32("W", 11264, 512)
        rk_bc = self.f32("W", 13312, 512)
        Tcqn, Tckvn, Tcc, Tt1, Tt2, Trqb, Trkb = (Buf(n) for n in ("cqn", "ckvn", "cc", "t1a", "t2a", "rqb", "rkb"))
        diag = [self.f32("st", 2048 + i * 512, 128) for i in range(4)]
        Tdiag = [Buf(f"diaga{i}") for i in range(4)]
        Qh = [self.bf("QT", i * 2 * S, S) for i in range(2)]
        Kh = [self.bf("KT", i * 2 * S, S) for i in range(2)]
        rec = self.f32("st", 2016, 4)
        Trec = Buf("reca")
        for pair in range(2):
            self.s.barrier()
            for TB in range(NB):
                sl = slice(TB * 512, (TB + 1) * 512)
                Trb = self.Trbc[TB]
                self.dma("sp", ccb, self.cc_d[:, sl], (), [Tcc])
                self.dma("sp", ssb, self.ss_d[:, sl], (), [Tcc])
                self.bcast(rqr[:, 0:NT], TB, rq_bc, [Trq], [Trqb], diag, Tdiag)
                self.bcast(rqr[:, NT:2 * NT], TB, rk_bc, [Trq], [Trkb], diag, Tdiag)
                ps, pt = self.proj_fm(self.wsl(1, 0, 128), 128, TB, "A")
                self.tt("dve", ckvn, ps, rk_bc, ALU.mult, [pt, Trkb], [Tckvn])
                for hh in range(2):
                    h = pair * 2 + hh
                    ps, _, pt = self.bankA()
                    self.mm(ps[0:64, :], wkv4[:, h, 0:64], ckvn, True, True, [Twq, Tckvn], [pt])
                    self.cp("act", Kh[hh][0:64, sl], ps[0:64, :], [pt], [self.TKT[TB]])
                if pair == 0:
                    for tt_ in range(4):
                        t = TB * 4 + tt_
                        ps, _, pt = self.bankB()
                        self.mm(ps[:, 0:256].rearrange("p (h d) -> p h d", h=4), ckvn[:, tt_ * 128:(tt_ + 1) * 128],
                                wkv4[:, :, 64:128], True, True, [Twq, Tckvn], [pt])
                        self.cp("act", self.VT4[:, t, :, 0:64], ps[:, 0:256].rearrange("p (h d) -> p h d", h=4),
                                [pt], [self.TVT[t]])
                ps1, pt1 = self.proj_fm(self.wsl(1, 128, 96), 96, TB, "A")
                ps2, pt2 = self.proj_fm(self.wsl(2, 0, 96), 96, TB, "A")
                self.tt("dve", t1[64:96, :], ps1[64:96, :], ccb[64:96, :], ALU.mult, [pt1, Tcc], [Tt1])
                self.tt("dve", t2[64:96, :], ps2[64:96, :], ssb[64:96, :], ALU.mult, [pt2, Tcc], [Tt2])
                self.tt("dve", t1[64:96, :], t1[64:96, :], t2[64:96, :], ALU.add, [Tt1, Tt2], [Tt1])
                self.tt("dve", Kh[0][64:96, sl], t1[64:96, :], self.rbc[64:96, sl], ALU.mult, [Tt1, Trb], [self.TKT[TB]])
                self.tt("dve", Kh[1][64:96, sl], t1[64:96, :], self.rbc[64:96, sl], ALU.mult, [Tt1, Trb], [self.TKT[TB]])
                for c in range(2):
                    ps, pt = self.proj_fm(self.wsl(0, c * 128, 128), 128, TB, "A")
                    self.tt("dve", cqn[:, c, :], ps, rq_bc, ALU.mult, [pt, Trqb], [Tcqn])
                for hh in range(2):
                    h = pair * 2 + hh
                    psa, _, pta = self.bankA()
                    psb_, _, ptb = self.bankA()
                    for c in range(2):
                        self.mm(psa[0:96, :], wq[:, c, h * 96:(h + 1) * 96], cqn[:, c, :], c == 0, c == 1, [Twq, Tcqn], [pta])
                    for c in range(2):
                        self.mm(psb_[0:96, :], wqs[:, c, h * 96:(h + 1) * 96], cqn[:, c, :], c == 0, c == 1, [Twq, Tcqn], [ptb])
                    self.cp("act", Qh[hh][0:64, sl], psa[0:64, :], [pta], [self.TQT[TB]])
                    self.tt("dve", t1[64:96, :], psa[64:96, :], ccb[64:96, :], ALU.mult, [pta, Tcc], [Tt1])
                    self.tt("dve", t2[64:96, :], psb_[64:96, :], ssb[64:96, :], ALU.mult, [ptb, Tcc], [Tt2])
                    self.tt("dve", Qh[hh][64:96, sl], t1[64:96, :], t2[64:96, :], ALU.add, [Tt1, Tt2], [self.TQT[TB]])
            self.s.barrier()
            for I in range(NB):
                self.load_sg(0, I)
                items = []
                for hh in range(2):
                    h = pair * 2 + hh
                    kfn = lambda j, hh=hh: (Kh[hh][0:96, j * 128:(j + 1) * 128], [self.TKT[j // 4]])
                    qfn = lambda c0, c1, hh=hh, I=I: (Qh[hh][0:96, c0:c1], [self.TQT[I]])
                    vfn = lambda j, h=h: (self.VT4[:, j, h, :], [self.TVT[j]])

                    def extra(j, tt, I=I):
                        return [(self.cmb, [T])] if j == 4 * I + tt else []

                    def make(kfn=kfn, qfn=qfn, vfn=vfn, extra=extra, I=I):
                        return self.attn_map_gen(I, [(kfn, qfn, 0)], vfn, extra, None, pairs=True)

                    def post(O, Ot, hh=hh, h=h):
                        if hh == 0:
                            self.flush_store()
                        Ov = O[:, 0:260].rearrange("p (t d) -> p t d", t=4)
                        self.s.add("dve", lambda e, Ov=Ov: e.reciprocal(rec, Ov[:, :, 64]), [Ot], [Trec])
                        for tt in range(4):
                            self.stt(self.ytm[:, tt, hh * 64:(hh + 1) * 64], Ov[:, tt, 0:64], rec[:, tt:tt + 1],
                                     self.sg[:, tt, h * 64:(h + 1) * 64], ALU.mult, ALU.mult, [Ot, Trec, self.Tsg], [self.Tytm])
                    items.append((None, make, post, 2 * I + 4))
                self.run_chain(items, 1)
                self.deferred = (lambda I=I, pair=pair: self.store_y(0, I, r0=pair, nfc=1))
            self.flush_store()

    def mixer_c_proj(self, l):
        S, NT, NB = self.S, self.NT, self.NB
        T = self.T_const
        self.proj_gate(l, 2, "c_gate")
        self.proj_fm_to(l, "c_q", 256, self.QT3, self.TQT, scale=64 ** -0.5)
        self.proj_fm_to(l, "c_k", 256, self.KT3, self.TKT)
        self.proj_v(l, "c_v")
        self.QI3 = self.bf("QI", 0, 2 * S).rearrange("p (c s) -> p c s", c=2)
        self.TQI = [Buf(f"QI{i}") for i in range(NB)]
        self.proj_fm_to(l, "c_qidx", 256, self.QI3, self.TQI, scale=1.0 / 16.0)
        self.KX = self.bf("X1", 0, S)
        self.TKX = [Buf(f"KX{i}") for i in range(NB)]
        base = self.win_d[l, :, OFF["c_kidx"]:OFF["c_kidx"] + 32]
        k = self.nw % 2
        stg4 = self.wst[k].rearrange("p (c r n) -> p c r n", c=8, r=4)
        srcs = [(stg4[:, :, r, :], base.rearrange("(c p) n -> p c n", p=128)) for r in range(4)]
        self.load_w(l, 0, 128, 0, 0, src=srcs)
        for TB in range(NB):
            ps, pt = self.proj_fm(self.wsl(0, 0, 128), 128, TB, "A")
            self.tt("dve", self.KX[:, TB * 512:(TB + 1) * 512], ps, self.rbc[:, TB * 512:(TB + 1) * 512], ALU.mult,
                    [pt, self.Trbc[TB]], [self.TKX[TB]])
        self.load_w(l, OFF["c_widx"], 8, 1, 0)
        self.wabs = self.f32("st", 2048, NT * 8)
        self.wsgn = self.f32("st", 3072, NT * 8)
        self.Tww = Buf("ww")
        for t in range(NT):
            ps, pt = self.proj_tm(self.wsl(1, 0, 8), 8, t, "B")
            self.ts("dve", self.wabs[:, t * 8:(t + 1) * 8], ps[:, 0:8], self.rstd[:, t:t + 1], None, ALU.mult, None,
                    [pt, self.Trstd], [self.Tww])
        self.cp("dve", self.wsgn, self.wabs, [self.Tww], [self.Tww])

    def mixer_c_attn(self, l):
        S, NT, NB = self.S, self.NT, self.NB
        T = self.T_const
        nbis = self.nbis
        idx = self.f32("xT", 0, 4 * S).rearrange("p (g s) -> p g s", g=4)
        ob = self.o["rbc"]
        negms = [self.A8[:, ob + k * 4 * S: ob + (k + 1) * 4 * S].rearrange("p (g s) -> p g s", g=4) for k in range(2)]
        Tnegs = [[Buf(f"neg{k}_{g}") for g in range(4)] for k in range(2)]
        Tidx = [Buf(f"idx{g}") for g in range(4)]
        Rp = [self.bf("W", i * 2048, 1024).rearrange("p (b n) -> p b n", b=2) for i in range(4)]
        TRp = [Buf(f"Rp{i}") for i in range(4)]
        Dg = [self.bf("wbf", 8192 + i * 256, 128) for i in range(8)]
        self.yTs = self.bf("wbf", 10240, 1024).rearrange("p (c s) -> p c s", c=2)
        TDg = [Buf(f"Dg{i}") for i in range(8)]
        sm = self.f32("st", 1344, 64)
        Tsm = Buf("smc")
        rmax, rmin, lo, step, mid, cnt, tmp, thr = (sm[:, 4 * i:4 * i + 4] for i in range(8))
        rec = sm[:, 32:36]
        nmid = sm[:, 36:40]
        c256 = sm[:, 40:44]
        Tth = [Buf(f"th{g}") for g in range(4)]
        den = sm[:, 44:48]
        mone = sm[:, 48:52]
        self.memset("dve", mone, -1.0, [T])
        pw2 = self.f32("st", 1900, nbis)
        for it in range(nbis):
            self.memset("dve", pw2[:, it:it + 1], 2.0 ** -(it + 1), [T])
        Trec = Buf("recc")
        ncp = 0
        s2all = self.f32("st", 1600, 4 * nbis)
        Ts2 = Buf("s2all")

        def index_and_threshold(I, hb):
            negm = negms[I % 2]
            Tneg = Tnegs[I % 2]
            live = [None]
            est = 0

            def tick(k):
                if live[0] is None:
                    return
                for _ in range(k):
                    try:
                        next(live[0])
                    except StopIteration:
                        live[0] = None
                        return

            if True:
                tiles = [2 * hb, 2 * hb + 1]
                groups = []
                for tt in tiles:
                    i = 4 * I + tt
                    L = 128 * (i + 1)
                    nkb = (L + 511) // 512
                    for kb in range(nkb):
                        for half in range(2):
                            groups.append((tt, i, L, kb, half, kb == nkb - 1))
                state = {}

                def stageA(g, par):
                    tt, i, L, kb, half, lastkb = g
                    ncol = min(512, L - kb * 512)
                    for pr_ in range(2):
                        psp, ptoks = self.bankpair("ia")
                        rp = par * 2 + pr_
                        for b2 in range(2):
                            hq = pr_ * 2 + b2
                            b_ = 32 * hq
                            kw = {"tile_position": (96, 0)} if b_ == 96 else {}
                            self.mm(psp[:, b2 * 512:b2 * 512 + ncol], self.QI3[b_:b_ + 32, half, i * 128:(i + 1) * 128],
                                    self.KX[b_:b_ + 32, kb * 512:kb * 512 + ncol], True, True,
                                    [self.TQI[I]] + [self.TKX[kb]], [ptoks[b2]], **kw)
                        self.act(Rp[rp][:, :, 0:ncol], psp.rearrange("p (b n) -> p b n", b=2)[:, :, 0:ncol], AF.Relu,
                                 list(ptoks), [TRp[rp]])

                def stageB(g, par):
                    tt, i, L, kb, half, lastkb = g
                    ncol = min(512, L - kb * 512)
                    if kb == 0 and half == 0:
                        for hi in range(8):
                            self.ts("pool", Dg[hi], self.identb, self.wsgn[:, i * 8 + hi:i * 8 + hi + 1], 1.0, ALU.mult, ALU.mult,
                                    [T, self.Tww], [TDg[hi]])
                    if half == 0:
                        state["psB"] = self.bank("ib")
                    psB, _, ptB = state["psB"]
                    for hq in range(4):
                        hi = half * 4 + hq
                        rp = par * 2 + hq // 2
                        self.mm(psB[:, 0:ncol], Dg[hi], Rp[rp][:, hq % 2, 0:ncol], hi == 0, hi == 7, [TDg[hi], TRp[rp]], [ptB])
                    if half == 1:
                        self.cp("act", idx[:, tt, kb * 512:kb * 512 + ncol], psB[:, 0:ncol], [ptB], [Tidx[tt]])
                        if lastkb:
                            self.memset("pool", idx[0:64, tt, i * 128 + 64:i * 128 + 128], -1e30, [Tidx[tt]])

                kstep = (est + 2 * len(groups) - 1) // (2 * len(groups)) if est else 0
                for gi in range(len(groups) + 1):
                    if gi < len(groups):
                        stageA(groups[gi], gi % 2)
                    if gi >= 1:
                        stageB(groups[gi - 1], (gi - 1) % 2)
                    tick(kstep)
                tts = [tt for tt in tiles if 4 * I + tt >= 2]
                for tt in tiles:
                    if tt not in tts:
                        self.memset("dve", thr[:, tt:tt + 1], -1e29, [Tth[tt]])
                if tts:
                    Ls = {tt: 128 * (4 * I + tt + 1) for tt in tts}
                    for tt in tts:
                        L = Ls[tt]
                        self.s.add("dve", lambda e, tt=tt, L=L: e.tensor_reduce(rmax[:, tt:tt + 1], idx[:, tt, 0:L], AX.X, ALU.max),
                                   [Tidx[tt]], [Tth[tt]])
                    for tt in tts:
                        L = Ls[tt]
                        self.s.add("dve", lambda e, tt=tt, L=L: e.tensor_reduce(mid[:, tt:tt + 1], idx[:, tt, 0:320], AX.X, ALU.min),
                                   [Tidx[tt]], [Tth[tt]])
                    for tt in tts:
                        self.tt("dve", rmax[:, tt:tt + 1], rmax[:, tt:tt + 1], mid[:, tt:tt + 1], ALU.subtract, [Tth[tt]], [Tth[tt]])
                    for tt in tts:
                        self.ts("dve", s2all[:, tt * nbis:(tt + 1) * nbis], pw2, rmax[:, tt:tt + 1], None, ALU.mult, None,
                                [Tth[tt], T], [Ts2])
                    for tt in tts:
                        self.stt(mid[:, tt:tt + 1], rmax[:, tt:tt + 1], 0.5, mid[:, tt:tt + 1], ALU.mult, ALU.add,
                                 [Tth[tt]], [Tth[tt]])
                    for it in range(nbis):
                        for tt in tts:
                            L = Ls[tt]
                            self.ts("dve", negm[:, tt, 0:L], idx[:, tt, 0:L], mid[:, tt:tt + 1], None, ALU.is_ge, ALU.add,
                                    [Tidx[tt], Tth[tt]], [Tneg[tt], Tth[tt]], accum_out=cnt[:, tt:tt + 1], saturate=False)
                        for tt in tts:
                            self.ts("dve", tmp[:, tt:tt + 1], cnt[:, tt:tt + 1], 256.0, 0.5, ALU.is_ge, ALU.subtract,
                                    [Tth[tt]], [Tth[tt]])
                        for tt in tts:
                            self.stt(mid[:, tt:tt + 1], tmp[:, tt:tt + 1], s2all[:, tt * nbis + it:tt * nbis + it + 1],
                                     mid[:, tt:tt + 1], ALU.mult, ALU.add, [Tth[tt], Ts2], [Tth[tt]])
                    for tt in tts:
                        self.tt("dve", thr[:, tt:tt + 1], mid[:, tt:tt + 1], s2all[:, (tt + 1) * nbis - 1:(tt + 1) * nbis],
                                ALU.subtract, [Tth[tt], Ts2], [Tth[tt]])
                for tt in tiles:
                    L = 128 * (4 * I + tt + 1)
                    self.ts("dve", negm[:, tt, 0:L], idx[:, tt, 0:L], thr[:, tt:tt + 1], NEGM, ALU.is_lt, ALU.mult,
                            [Tidx[tt], Tth[tt]], [Tneg[tt]], saturate=False)

        def attention(I, hb):
            negm = negms[I % 2]
            Tneg = Tnegs[I % 2]
            if hb == 0:
                self.load_sg(2, I)
            items = []
            for h in (2 * hb, 2 * hb + 1):
                ch, base = h // 2, 0
                mi = 8 + h

                def pre(h=h):
                    self.ts("pool", qzc, self.QT3[:, h // 2, I * 512:(I + 1) * 512], self.rmask[:, 4 + h % 2:5 + h % 2], 1.0,
                            ALU.mult, ALU.mult, [self.TQT[I], T], [Tqzc])
                kfn = lambda j, ch=ch: (self.KT3[:, ch, j * 128:(j + 1) * 128], [self.TKT[j // 4]])
                qfn = lambda c0, c1, I=I: (qzc[:, c0 - I * 512:c1 - I * 512], [Tqzc])
                vfn = lambda j, h=h: (self.VT4[:, j, h, :], [self.TVT[j]])

                def extra(j, tt, mi=mi, I=I):
                    i = 4 * I + tt
                    r = [(negm[:, tt, j * 128:(j + 1) * 128], [Tneg[tt]])]
                    if j == i:
                        r.append((self.bd[:, mi * 128:(mi + 1) * 128], [T]))
                    if j == i - 1:
                        r.append((self.bs[:, mi * 128:(mi + 1) * 128], [T]))
                    return r

                def make(kfn=kfn, qfn=qfn, vfn=vfn, extra=extra, base=base):
                    return self.attn_map_gen(I, [(kfn, qfn, base)], vfn, extra, None, spool="as", opool="ao", pairs=True)

                def post(O, Ot, h=h):
                    if h == 0:
                        self.flush_store()
                    Ov = O[:, 0:260].rearrange("p (t d) -> p t d", t=4)
                    self.act(den, Ov[:, :, 64], AF.Copy, [Ot], [Trec])
                    self.tt("pool", rec, den, mone, ALU.pow, [Trec, T], [Trec])
                    for tt in range(4):
                        self.act(self.ytm[:, tt, h * 64:(h + 1) * 64], Ov[:, tt, 0:64], AF.Copy, [Ot, Trec], [self.Tytm],
                                 scale=rec[:, tt:tt + 1])
                    self.tt("pool", self.ytm[:, :, h * 64:(h + 1) * 64], self.ytm[:, :, h * 64:(h + 1) * 64],
                            self.sg[:, :, h * 64:(h + 1) * 64], ALU.mult, [self.Tsg, self.Tytm], [self.Tytm])
                items.append((pre, make, post, 2 * I + 4))
            self.run_chain(items, 1)
            if hb == 1:
                self.deferred = (lambda I=I: self.store_y(2, I))

        self.trpool = "as"
        qzc = self.bf("X2", 0, 512)
        Tqzc = Buf("qzc")
        for I in range(NB + 1):
            for hb in range(2):
                if I < NB:
                    index_and_threshold(I, hb)
                if I >= 1:
                    attention(I - 1, hb)
        self.flush_store()
        self.trpool = "A"

    def zero_y(self, rows):
        z = self.bf("W", 0, 512)
        Tz = Buf("z")
        self.memset("dve", z, 0.0, [Tz])
        for r in rows:
            for TB in range(self.NB):
                self.dma("pool", self.yT_d[r * 128:(r + 1) * 128, TB * 512:(TB + 1) * 512], z, [Tz],
                         [self.TyT[(r, TB)]])

    def record(self):
        S, NT, NB = self.S, self.NT, self.NB
        self.Tout = [Buf(f"out{t}") for t in range(NT)]
        Tx_in = [Buf(f"xin{t}") for t in range(NT)]
        Tx_res = [Buf(f"xres{t}") for t in range(NT)]
        self.setup()
        self.init_w()
        for l in range(self.depth):
            last = l == self.depth - 1
            xsrc = self.x_d if l == 0 else self.xres_d
            Tx = Tx_in if l == 0 else Tx_res
            self.TyT = {(r, TB): Buf(f"yT{r}_{TB}") for r in range(8) for TB in range(NB)}
            self.phase1(l, xsrc, Tx)
            if "d" in self.mixers:
                self.prefetch_w(l, [(OFF[nm], 128, gi // 2, (gi % 2) * 128, 1.0)
                                    for gi, nm in enumerate(("d_b", "d_c", "d_h", "d_gate"))])
            self.s.barrier()
            for mi, m in enumerate("abcd"):
                if m not in self.mixers:
                    self.zero_y([2 * mi, 2 * mi + 1])
            self.init_attn()
            self.Tgate = {(mi, t): Buf(f"g{mi}_{t}") for mi in range(3) for t in range(NT)}
            if "d" in self.mixers:
                self.mixer_d(l)
            self.set_ones()
            if "b" in self.mixers:
                self.mixer_b(l)
                self.s.barrier()
            if "a" in self.mixers:
                self.mixer_a(l)
                self.s.barrier()
            if "c" in self.mixers:
                self.mixer_c_proj(l)
                self.s.barrier()
                self.mixer_c_attn(l)
            self.s.barrier()
            self.outproj(l, xsrc, Tx, self.Tout if last else Tx_res, last)
            self.s.barrier()


def build(S, depth=DEPTH, mixers="abcd", dbg=False, nbis=12):
    nc = bass.Bass("TRN2", target_bir_lowering=False)
    with ExitStack() as st:
        p = Prog(nc, st, S, depth, mixers, dbg, nbis)
        p.record()
        p.s.emit(st)
    return nc


def host_inputs(S, depth, norm_g, w_in, mla_qa_g, mla_w_uq, mla_kva_g, mla_w_ukv, diff_lambda,
                diff_subln_g, conv_w, w_out, rel_bias, final_g):
    f = np.float32
    c = _host_consts(S)
    pk = np.zeros((128, depth * PL + D), f)
    for l in range(depth):
        b = l * PL
        pk[:, b:b + 8] = np.asarray(norm_g[l], f).reshape(8, 128).T
        pk[:, b + 8:b + 10] = np.asarray(mla_qa_g[l], f).reshape(2, 128).T
        pk[:, b + 10] = np.asarray(mla_kva_g[l], f)
        pk[:, b + 11:b + 17] = np.asarray(conv_w[l], f).reshape(3, 2, 128).transpose(2, 1, 0).reshape(128, 6)
        pk[:, b + 17:b + 145] = np.asarray(diff_lambda[l], f).reshape(1, 128)
        pk[:, b + 145:b + 209] = np.asarray(diff_subln_g[l], f).reshape(1, 64)
    pk[:, depth * PL:] = np.asarray(final_g, f).reshape(1, D)
    cst = np.concatenate([c["ident_f"], c["ones_f"], c["cm"]], axis=1).astype(f)
    rmask = np.zeros((128, 8), f)
    for p_ in range(128):
        rmask[p_, p_ // 32] = 1.0
        rmask[p_, 4 + p_ // 64] = 1.0
    rb = np.asarray(rel_bias, f)
    bias = np.zeros((128, 2 * 12 * 128 + 12), f)
    for m in range(12):
        bias[:, m * 128:(m + 1) * 128] = rb[c["bk_diag"], m]
        bias[:, 1536 + m * 128:1536 + (m + 1) * 128] = rb[c["bk_sub"], m]
    bias[:, 3072:3084] = rb[15:16, :]
    shared = {
        "w_in": np.ascontiguousarray(np.asarray(w_in, f)[:depth]),
        "w_uq": np.ascontiguousarray(np.asarray(mla_w_uq, f)[:depth]),
        "w_ukv": np.ascontiguousarray(np.asarray(mla_w_ukv, f)[:depth]),
        "w_out": np.ascontiguousarray(np.asarray(w_out, f)[:depth]),
        "pk": pk, "cst": cst, "biasblk": bias, "cc4": c["cc4"], "ss4": c["ss4"], "rmask": rmask,
    }
    return shared


_NC_CACHE = {}


def kernel(x, norm_g, w_in, mla_qa_g, mla_w_uq, mla_kva_g, mla_w_ukv, diff_lambda,
           diff_subln_g, conv_w, w_out, rel_bias, final_g):
    x = np.asarray(x, np.float32)
    B, S, _ = x.shape
    shared = host_inputs(S, DEPTH, norm_g, w_in, mla_qa_g, mla_w_uq, mla_kva_g, mla_w_ukv, diff_lambda,
                         diff_subln_g, conv_w, w_out, rel_bias, final_g)
    key = (S, DEPTH)
    if key not in _NC_CACHE:
        _NC_CACHE[key] = build(S, DEPTH)
    nc = _NC_CACHE[key]
    in_maps = []
    for b in range(B):
        m = dict(shared)
        m["x"] = np.ascontiguousarray(x[b])
        in_maps.append(m)
    res = run_bass_kernel_spmd(nc, in_maps, core_ids=list(range(B)))
    return np.stack([np.asarray(r["out"], np.float32) for r in res.results], axis=0)
```
